# Optimizing a Trainium2 kernel written in Bass

```python
import jax, jax.numpy as jnp
from jax import lax
import numpy as np

D_MODEL = 1024
BATCH = 4
SEQ = 8192
DEPTH = 2

HEAD_DIM = 64
N_HEADS = D_MODEL // HEAD_DIM
D_MIX = N_HEADS * HEAD_DIM
H_A = N_HEADS // 2
H_B = N_HEADS - H_A
H_C = N_HEADS // 2
H_D = N_HEADS - H_C
D_FF = 256 * ((8 * D_MODEL + 3 * 256 - 1) // (3 * 256))
FFN_RES_WEIGHT = 0.5
Q_BLOCK = 128
MOBA_BLOCK = 256
MOBA_TOPK = 3
MOBA_Q_BLOCK = 64
DILATED_BRANCHES = ((128, 1), (512, 4), (2048, 16))
ROPE_THETA = 10000.0
RMS_EPS = 1e-6
N_EVEN = (DEPTH + 1) // 2
N_ODD = DEPTH // 2

kernel_name = 'hybrid_sb_moba_fox_dilated_macaron'


def rms_norm(x, g):
    xf = x.astype(jnp.float32)
    y = xf * lax.rsqrt(jnp.mean(xf * xf, axis=-1, keepdims=True) + RMS_EPS)
    return (y * g.astype(jnp.float32)).astype(x.dtype)


def swiglu(x, w_gate, w_up, w_down):
    return (jax.nn.silu(x @ w_gate) * (x @ w_up)) @ w_down


def rope_tables(seq):
    inv = 1.0 / (ROPE_THETA ** (jnp.arange(0, HEAD_DIM, 2, dtype=jnp.float32) / HEAD_DIM))
    ang = jnp.arange(seq, dtype=jnp.float32)[:, None] * inv[None, :]
    return jnp.cos(ang), jnp.sin(ang)


def apply_rope(x, cos, sin):
    x1, x2 = jnp.split(x, 2, axis=-1)
    c = cos.astype(x.dtype)
    s = sin.astype(x.dtype)
    return jnp.concatenate([x1 * c - x2 * s, x2 * c + x1 * s], axis=-1)


def split_heads(t, n_heads):
    b, s, _ = t.shape
    return t.reshape(b, s, n_heads, HEAD_DIM).transpose(0, 2, 1, 3)


def merge_heads(t):
    b, h, s, d = t.shape
    return t.transpose(0, 2, 1, 3).reshape(b, s, h * d)


def to_query_blocks(t, blk):
    b, h, s = t.shape[:3]
    t = t.reshape(b, h, s // blk, blk, *t.shape[3:])
    return jnp.moveaxis(t, 2, 0)


def from_query_blocks(t):
    n, b, h, blk, d = t.shape
    return jnp.moveaxis(t, 0, 2).reshape(b, h, n * blk, d)


def stick_breaking_attention(q, k, v):
    b, h, s, dh = q.shape
    scale = dh ** -0.5
    kpos = jnp.arange(s)

    def one_block(args):
        q_blk, i = args
        qpos = i * Q_BLOCK + jnp.arange(Q_BLOCK)
        strict = kpos[None, :] < qpos[:, None]
        z = jnp.einsum('bhqd,bhkd->bhqk', q_blk, k).astype(jnp.float32) * scale
        log_keep = jnp.where(strict, jax.nn.log_sigmoid(-z), 0.0)
        suffix = lax.cumsum(log_keep, axis=3, reverse=True)
        between = jnp.concatenate([suffix[..., 1:], jnp.zeros_like(suffix[..., :1])], axis=-1)
        weight = jnp.where(strict, jnp.exp(jax.nn.log_sigmoid(z) + between), 0.0)
        return jnp.einsum('bhqk,bhkd->bhqd', weight.astype(v.dtype), v)

    out = lax.map(one_block, (to_query_blocks(q, Q_BLOCK), jnp.arange(s // Q_BLOCK)))
    return from_query_blocks(out)


def moba_attention(q, k, v):
    b, h, s, dh = q.shape
    scale = dh ** -0.5
    n_kb = -(-s // MOBA_BLOCK)
    pad = n_kb * MOBA_BLOCK - s
    k_pad = jnp.pad(k, ((0, 0), (0, 0), (0, pad), (0, 0)))
    v_pad = jnp.pad(v, ((0, 0), (0, 0), (0, pad), (0, 0)))
    k_blocks = k_pad.reshape(b, h, n_kb, MOBA_BLOCK, dh)
    v_blocks = v_pad.reshape(b, h, n_kb, MOBA_BLOCK, dh)
    k_mean = jnp.mean(k_blocks.astype(jnp.float32), axis=3)
    top_k = min(MOBA_TOPK, n_kb)
    bi = jnp.arange(b)[:, None, None, None]
    hi = jnp.arange(h)[None, :, None, None]
    blk_ids = jnp.arange(n_kb)
    local = jnp.arange(MOBA_BLOCK)
    n_sel = top_k * MOBA_BLOCK

    def one_block(args):
        q_blk, i = args
        q0 = i * MOBA_Q_BLOCK
        qpos = q0 + jnp.arange(MOBA_Q_BLOCK)
        own = q0 // MOBA_BLOCK
        gate = jnp.einsum('bhqd,bhnd->bhqn', q_blk.astype(jnp.float32), k_mean)
        gate = jnp.where(blk_ids < own, gate, -jnp.inf)
        _, sel = lax.top_k(gate, top_k)
        sel_ok = jnp.arange(top_k) < own
        k_sel = k_blocks[bi, hi, sel]
        v_sel = v_blocks[bi, hi, sel]
        s_sel = jnp.einsum('bhqd,bhqnkd->bhqnk', q_blk, k_sel).astype(jnp.float32) * scale
        s_sel = jnp.where(sel_ok[:, None], s_sel, -jnp.inf).reshape(b, h, MOBA_Q_BLOCK, n_sel)
        k_own = lax.dynamic_slice_in_dim(k_pad, own * MOBA_BLOCK, MOBA_BLOCK, axis=2)
        v_own = lax.dynamic_slice_in_dim(v_pad, own * MOBA_BLOCK, MOBA_BLOCK, axis=2)
        s_own = jnp.einsum('bhqd,bhkd->bhqk', q_blk, k_own).astype(jnp.float32) * scale
        s_own = jnp.where(own * MOBA_BLOCK + local[None, :] <= qpos[:, None], s_own, -jnp.inf)
        p = jax.nn.softmax(jnp.concatenate([s_sel, s_own], axis=-1), axis=-1).astype(v.dtype)
        p_sel = p[..., :n_sel].reshape(b, h, MOBA_Q_BLOCK, top_k, MOBA_BLOCK)
        p_own = p[..., n_sel:]
        return (jnp.einsum('bhqnk,bhqnkd->bhqd', p_sel, v_sel)
                + jnp.einsum('bhqk,bhkd->bhqd', p_own, v_own))

    out = lax.map(one_block, (to_query_blocks(q, MOBA_Q_BLOCK), jnp.arange(s // MOBA_Q_BLOCK)))
    return from_query_blocks(out)


def forgetting_attention(q, k, v, log_f):
    b, h, s, dh = q.shape
    scale = dh ** -0.5
    c = jnp.cumsum(log_f, axis=-1)
    kpos = jnp.arange(s)

    def one_block(args):
        q_blk, c_blk, i = args
        qpos = i * Q_BLOCK + jnp.arange(Q_BLOCK)
        logits = (jnp.einsum('bhqd,bhkd->bhqk', q_blk, k).astype(jnp.float32) * scale
                  + c_blk[..., :, None] - c[..., None, :])
        logits = jnp.where(kpos[None, :] <= qpos[:, None], logits, -jnp.inf)
        p = jax.nn.softmax(logits, axis=-1).astype(v.dtype)
        return jnp.einsum('bhqk,bhkd->bhqd', p, v)

    out = lax.map(one_block, (to_query_blocks(q, Q_BLOCK), to_query_blocks(c, Q_BLOCK),
                              jnp.arange(s // Q_BLOCK)))
    return from_query_blocks(out)


def dilated_window_attention(q, k, v):
    b, h, s, dh = q.shape
    scale = dh ** -0.5

    def one_block(args):
        q_blk, i = args
        qpos = i * Q_BLOCK + jnp.arange(Q_BLOCK)
        maxes, denoms, outs = [], [], []
        for window, dil in DILATED_BRANCHES:
            steps = jnp.arange(window // dil + 1)
            kpos = qpos[:, None] - dil * steps[None, :]
            idx = jnp.maximum(kpos, 0)
            k_g = jnp.take(k, idx, axis=2)
            v_g = jnp.take(v, idx, axis=2)
            logits = jnp.einsum('bhqd,bhqnd->bhqn', q_blk, k_g).astype(jnp.float32) * scale
            logits = jnp.where(kpos >= 0, logits, -jnp.inf)
            m = jnp.max(logits, axis=-1, keepdims=True)
            e = jnp.exp(logits - m)
            l = jnp.sum(e, axis=-1, keepdims=True)
            outs.append(jnp.einsum('bhqn,bhqnd->bhqd', e, v_g.astype(jnp.float32)) / l)
            maxes.append(m)
            denoms.append(l)
        m_all = jnp.stack(maxes)
        s_all = jnp.stack(denoms) * jnp.exp(m_all - jnp.max(m_all, axis=0, keepdims=True))
        alpha = s_all / jnp.sum(s_all, axis=0, keepdims=True)
        return jnp.sum(alpha * jnp.stack(outs), axis=0).astype(v.dtype)

    out = lax.map(one_block, (to_query_blocks(q, Q_BLOCK), jnp.arange(s // Q_BLOCK)))
    return from_query_blocks(out)


def mix_stick_moba(xn, w_in, g_q_b, g_k_b, cos, sin):
    q, k, v = jnp.split(xn @ w_in, 3, axis=-1)
    q, k, v = split_heads(q, N_HEADS), split_heads(k, N_HEADS), split_heads(v, N_HEADS)
    out_a = stick_breaking_attention(q[:, :H_A], k[:, :H_A], v[:, :H_A])
    q_b = apply_rope(rms_norm(q[:, H_A:], g_q_b), cos, sin)
    k_b = apply_rope(rms_norm(k[:, H_A:], g_k_b), cos, sin)
    out_b = moba_attention(q_b, k_b, v[:, H_A:])
    return merge_heads(jnp.concatenate([out_a.astype(xn.dtype), out_b.astype(xn.dtype)], axis=1))


def mix_forget_dilated(xn, w_in, b_f, g_q_c, g_k_c, g_q_d, g_k_d, cos, sin):
    proj = xn @ w_in
    q, k, v = jnp.split(proj[..., :3 * D_MIX], 3, axis=-1)
    q, k, v = split_heads(q, N_HEADS), split_heads(k, N_HEADS), split_heads(v, N_HEADS)
    log_f = jax.nn.log_sigmoid((proj[..., 3 * D_MIX:] + b_f).astype(jnp.float32)).transpose(0, 2, 1)
    q_c = rms_norm(q[:, :H_C], g_q_c)
    k_c = rms_norm(k[:, :H_C], g_k_c)
    out_c = forgetting_attention(q_c, k_c, v[:, :H_C], log_f)
    q_d = apply_rope(rms_norm(q[:, H_C:], g_q_d), cos, sin)
    k_d = apply_rope(rms_norm(k[:, H_C:], g_k_d), cos, sin)
    out_d = dilated_window_attention(q_d, k_d, v[:, H_C:])
    return merge_heads(jnp.concatenate([out_c.astype(xn.dtype), out_d.astype(xn.dtype)], axis=1))


def setup_inputs(seed: int = 0) -> dict:
    key = jax.random.key(seed)
    ks = jax.random.split(key, 20)

    def normal(k, shape, scale):
        return jax.random.normal(k, shape, jnp.float32) * scale

    def gain(k, shape):
        return 1.0 + 0.02 * jax.random.normal(k, shape, jnp.float32)

    return {
        'x': normal(ks[0], (BATCH, SEQ, D_MODEL), 1.0),
        'norm_ffn1': gain(ks[1], (DEPTH, D_MODEL)),
        'ffn1_w_gate': normal(ks[2], (DEPTH, D_MODEL, D_FF), D_MODEL ** -0.5),
        'ffn1_w_up': normal(ks[3], (DEPTH, D_MODEL, D_FF), D_MODEL ** -0.5),
        'ffn1_w_down': normal(ks[4], (DEPTH, D_FF, D_MODEL), D_FF ** -0.5),
        'norm_mix': gain(ks[5], (DEPTH, D_MODEL)),
        'w_in_ab': normal(ks[6], (N_EVEN, D_MODEL, 3 * D_MIX), D_MODEL ** -0.5),
        'g_q_b': gain(ks[7], (N_EVEN, HEAD_DIM)),
        'g_k_b': gain(ks[8], (N_EVEN, HEAD_DIM)),
        'w_in_cd': normal(ks[9], (N_ODD, D_MODEL, 3 * D_MIX + H_C), D_MODEL ** -0.5),
        'b_f': normal(ks[10], (N_ODD, H_C), 0.1),
        'g_q_c': gain(ks[11], (N_ODD, HEAD_DIM)),
        'g_k_c': gain(ks[12], (N_ODD, HEAD_DIM)),
        'g_q_d': gain(ks[13], (N_ODD, HEAD_DIM)),
        'g_k_d': gain(ks[14], (N_ODD, HEAD_DIM)),
        'w_out': normal(ks[15], (DEPTH, D_MIX, D_MODEL), D_MIX ** -0.5),
        'norm_ffn2': gain(ks[16], (DEPTH, D_MODEL)),
        'ffn2_w_gate': normal(ks[17], (DEPTH, D_MODEL, D_FF), D_MODEL ** -0.5),
        'ffn2_w_up': normal(ks[18], (DEPTH, D_MODEL, D_FF), D_MODEL ** -0.5),
        'ffn2_w_down': normal(ks[19], (DEPTH, D_FF, D_MODEL), D_FF ** -0.5),
    }


def reference(x, norm_ffn1, ffn1_w_gate, ffn1_w_up, ffn1_w_down, norm_mix, w_in_ab, g_q_b, g_k_b,
              w_in_cd, b_f, g_q_c, g_k_c, g_q_d, g_k_d, w_out, norm_ffn2, ffn2_w_gate, ffn2_w_up,
              ffn2_w_down):
    cos, sin = rope_tables(x.shape[1])
    for layer in range(DEPTH):
        j = layer // 2
        x = x + FFN_RES_WEIGHT * swiglu(rms_norm(x, norm_ffn1[layer]), ffn1_w_gate[layer],
                                        ffn1_w_up[layer], ffn1_w_down[layer])
        xn = rms_norm(x, norm_mix[layer])
        if layer % 2 == 0:
            mixed = mix_stick_moba(xn, w_in_ab[j], g_q_b[j], g_k_b[j], cos, sin)
        else:
            mixed = mix_forget_dilated(xn, w_in_cd[j], b_f[j], g_q_c[j], g_k_c[j],
                                       g_q_d[j], g_k_d[j], cos, sin)
        x = x + mixed @ w_out[layer]
        x = x + FFN_RES_WEIGHT * swiglu(rms_norm(x, norm_ffn2[layer]), ffn2_w_gate[layer],
                                        ffn2_w_up[layer], ffn2_w_down[layer])
    return x
```

```python
import numpy as np
import ml_dtypes
from contextlib import ExitStack
import concourse.bass as bass
import concourse.mybir as mybir
from concourse.bass_utils import run_bass_kernel_spmd

F32 = mybir.dt.float32
BF16 = mybir.dt.bfloat16
AF = mybir.ActivationFunctionType
ALU = mybir.AluOpType
AX = mybir.AxisListType
NPBF = ml_dtypes.bfloat16

D = 1024
DFF = 2816
NH = 16
DH = 64
S_FULL = 8192
B_FULL = 4
EPS = 1e-6
NEG = -30000.0
STW = 1540


class _State:
    __slots__ = ("last_w", "readers", "sem", "cnt")

    def __init__(self):
        self.last_w = None
        self.readers = []
        self.sem = None
        self.cnt = 0


class Buf:
    __slots__ = ("t", "st", "name")

    def __init__(self, t, name, st=None):
        self.t = t
        self.name = name
        self.st = st if st is not None else _State()

    def __getitem__(self, k):
        return self.t[k]

    last_w = property(lambda s: s.st.last_w, lambda s, v: setattr(s.st, "last_w", v))
    readers = property(lambda s: s.st.readers, lambda s, v: setattr(s.st, "readers", v))
    sem = property(lambda s: s.st.sem, lambda s, v: setattr(s.st, "sem", v))
    cnt = property(lambda s: s.st.cnt, lambda s, v: setattr(s.st, "cnt", v))


class Ins:
    __slots__ = ("eng", "fn", "deps", "needs_inc", "count", "epoch", "dma_tok", "idx", "inc")

    def __init__(self, eng, fn):
        self.eng = eng
        self.fn = fn
        self.deps = []
        self.needs_inc = False
        self.count = 0
        self.epoch = 0
        self.dma_tok = None
        self.idx = 0
        self.inc = 16


ENGS = ("pe", "act", "dve", "pool", "sp")
EPOCH = 20000


class Prog:
    ARENA_BYTES = 212000

    def __init__(self, nc, stack):
        self.nc = nc
        self.stack = stack
        self.ins = {e: [] for e in ENGS}
        self.nsb = 0
        self.arena = stack.enter_context(nc.sbuf_tensor("arena", [128, self.ARENA_BYTES // 2], BF16))
        self.off = 0
        self.live = []
        self.sem_pool = []
        self.nsem = 0
        self.cc_sem = None
        self.cc_cnt = 0
        self.banks = [stack.enter_context(nc.psum_tensor(f"bank{i}", [128, 512], F32)) for i in range(8)]
        self.bank_state = [_State() for _ in range(8)]

    def sb(self, shape, dt, name=None):
        self.nsb += 1
        name = name or f"sb{self.nsb}"
        esz = 4 if dt == F32 else 2
        n = 1
        for d in shape[1:]:
            n *= d
        nbytes = (n * esz + 63) // 64 * 64
        assert self.off + nbytes <= self.ARENA_BYTES, (name, self.off, nbytes)
        v = self.arena[0:shape[0], self.off // 2:(self.off + n * esz) // 2]
        if dt == F32:
            v = v.bitcast(F32)
        if len(shape) == 3:
            v = v.rearrange("p (a b) -> p a b", a=shape[1])
        self.off += nbytes
        b = Buf(v, name)
        self.live.append((self.off - nbytes, b))
        return b

    def mark(self):
        return self.off

    def release(self, m):
        keep = []
        for off, b in self.live:
            if off >= m:
                if b.sem is not None:
                    self.sem_pool.append((b.sem, b.cnt))
                    b.st.sem = None
            else:
                keep.append((off, b))
        self.live = keep
        self.off = m

    def ps(self, bank, dt=F32, name=None):
        v = self.banks[bank][:, :]
        if dt == BF16:
            v = v.bitcast(BF16)
        return Buf(v, name or f"bank{bank}", self.bank_state[bank])

    def _add(self, eng, fn, reads, writes, dma=False):
        i = Ins(eng, fn)
        i.idx = len(self.ins[eng])
        deps = []
        for b in reads:
            if b.last_w is not None:
                deps.append(b.last_w)
        for b in writes:
            if b.last_w is not None:
                deps.append(b.last_w)
            for r in b.readers:
                deps.append(r)
        seen = set()
        for d in deps:
            if id(d) in seen or d is i:
                continue
            seen.add(id(d))
            if d.dma_tok is None and d.eng == eng:
                if eng == "pe":
                    continue
                if not any((b.last_w is d) for b in list(reads) + list(writes)):
                    continue
            d.needs_inc = True
            i.deps.append(d)
        if dma:
            key = None
            for b in list(writes) + list(reads):
                key = b
                break
            if key.sem is None:
                if self.sem_pool:
                    key.sem, key.cnt = self.sem_pool.pop()
                else:
                    self.nsem += 1
                    key.sem = self.stack.enter_context(self.nc.semaphore(f"d{self.nsem}_{key.name}"))
                    key.cnt = 0
            key.cnt += 16
            i.dma_tok = (key.sem, key.cnt)
        for b in reads:
            b.readers.append(i)
        for b in writes:
            b.last_w = i
            b.readers = []
        self.ins[eng].append(i)
        return i

    def pe(self, fn, reads, writes):
        return self._add("pe", fn, reads, writes)

    def act(self, fn, reads, writes):
        return self._add("act", fn, reads, writes)

    def dve(self, fn, reads, writes):
        return self._add("dve", fn, reads, writes)

    def pool(self, fn, reads, writes):
        return self._add("pool", fn, reads, writes)

    def dma(self, out_ap, in_ap, reads, writes, eng=None):
        if eng is None:
            eng = "sp"
        return self._add(eng, lambda e: e.dma_start(out=out_ap, in_=in_ap), reads, writes, dma=True)

    def collective(self, in_ap, out_ap, groups):
        if self.cc_sem is None:
            self.cc_sem = self.stack.enter_context(self.nc.semaphore("cc_sem"))
        i = Ins("pool", lambda e: e.collective_compute("AllReduce", ALU.add, replica_groups=groups,
                                                       ins=[in_ap], outs=[out_ap]))
        self.cc_cnt += 1
        i.dma_tok = (self.cc_sem, self.cc_cnt)
        i.inc = 1
        self.ins["pool"].append(i)
        return i

    def barrier(self):
        lasts = []
        for e in ENGS:
            if e == "sp":
                continue
            for i in reversed(self.ins[e]):
                if i.fn is not None and i.dma_tok is None:
                    lasts.append(i)
                    break
        dmas = {}
        for e in ENGS:
            for i in self.ins[e]:
                if i.dma_tok is not None:
                    dmas[id(i.dma_tok[0])] = i
        for e in ENGS:
            i = Ins(e, None)
            for d in lasts:
                if d.eng != e:
                    d.needs_inc = True
                    i.deps.append(d)
            for d in dmas.values():
                i.deps.append(d)
            self.ins[e].append(i)

    def emit(self):
        nc = self.nc
        sems = {}
        for e in ENGS:
            c = 0
            ep = 0
            for i in self.ins[e]:
                if i.dma_tok is None and i.needs_inc:
                    if c >= EPOCH:
                        ep += 1
                        c = 0
                    c += 1
                    i.count = c
                    i.epoch = ep
                    if (e, ep) not in sems:
                        sems[(e, ep)] = self.stack.enter_context(nc.semaphore(f"s_{e}{ep}"))
        block = self.stack.enter_context(nc.Block())
        engobj = {"pe": block.tensor, "act": block.scalar, "dve": block.vector,
                  "pool": block.gpsimd, "sp": block.sync}

        def make(e):
            def body(eng):
                waited = {}
                for i in self.ins[e]:
                    need = {}
                    for d in i.deps:
                        if d.dma_tok is not None:
                            sem, val = d.dma_tok
                        else:
                            sem, val = sems[(d.eng, d.epoch)], d.count
                        k = id(sem)
                        if waited.get(k, 0) >= val:
                            continue
                        if k not in need or need[k][1] < val:
                            need[k] = (sem, val)
                    for k, (sem, val) in need.items():
                        eng.wait_ge(sem, val)
                        waited[k] = val
                    if i.fn is None:
                        continue
                    r = i.fn(eng)
                    if i.dma_tok is not None:
                        r.then_inc(i.dma_tok[0], i.inc)
                    elif i.needs_inc:
                        r.then_inc(sems[(e, i.epoch)], 1)
            return body

        for e in ENGS:
            if self.ins[e]:
                engobj[e](make(e))


def new_prog():
    nc = bass.Bass("TRN2", target_bir_lowering=False)
    stack = ExitStack()
    return nc, stack, Prog(nc, stack)


class Ctx:
    pass


def load_const(p, ap, shape, dt, name):
    b = p.sb(shape, dt, name)
    p.dma(b[:], ap, [], [b])
    return b


def load_weight_bf16(p, w_ap, K, N, name, stage, eng_rr):
    kc = K // 128
    wv = w_ap.rearrange("(c p) n -> p c n", p=128)
    npiece = (N + STW - 1) // STW
    pw = (N + npiece - 1) // npiece
    chunks = []
    for c in range(kc):
        wb = p.sb([128, N], BF16, f"{name}{c}")
        chunks.append(wb)
        for n0 in range(0, N, pw):
            n1 = min(N, n0 + pw)
            st = stage[eng_rr[0] % len(stage)]
            eng_rr[0] += 1
            p.dma(st[:, 0:n1 - n0], wv[:, c, n0:n1], [], [st])
            src = st[:, 0:n1 - n0]
            dst = wb[:, n0:n1]
            if eng_rr[0] % 2 == 0:
                p.dve(lambda e, d=dst, s=src: e.tensor_copy(out=d, in_=s), [st], [wb])
            else:
                p.pool(lambda e, d=dst, s=src: e.tensor_copy(out=d, in_=s), [st], [wb])
    return chunks


def rms_to_T(p, cx, x_sb, g_bc, xnT, col0, nsub):
    for i in range(nsub):
        xs = x_sb[i]
        ss = cx.small[cx.rr % len(cx.small)]
        cx.rr += 1
        junk = cx.junk
        p.act(lambda e, o=junk[:], a=xs[:], s=ss[:, 0:1]: e.activation(out=o, in_=a, func=AF.Square, accum_out=s),
              [xs], [junk, ss])
        p.act(lambda e, s=ss: e.activation(out=s[:, 1:2], in_=s[:, 0:1], func=AF.Sqrt, scale=1.0 / D, bias=cx.eps[:, 0:1]),
              [ss, cx.eps], [ss])
        p.dve(lambda e, s=ss: e.reciprocal(out=s[:, 2:3], in_=s[:, 1:2]), [ss], [ss])
        xn = cx.xn[cx.rr % len(cx.xn)]
        p.dve(lambda e, o=xn[:], a=xs[:], s=ss[:, 2:3], g=g_bc[:]: e.scalar_tensor_tensor(
            out=o, in0=a, scalar=s, in1=g, op0=ALU.mult, op1=ALU.mult), [xs, ss, g_bc], [xn])
        transpose_to(p, cx, xn, xnT, 0, col0 + 128 * i, D // 128)


def transpose_to(p, cx, src, dstT, c0, col, nchunk, flat=None):
    tp = cx.tps[cx.rrt % len(cx.tps)]
    cx.rrt += 1
    sv = flat if flat is not None else src[:]
    for c in range(nchunk):
        p.pe(lambda e, o=tp[:, c * 128:(c + 1) * 128], a=sv[:, c * 128:(c + 1) * 128], idn=cx.ident[:]:
             e.transpose(out=o, in_=a, identity=idn), [src, cx.ident], [tp])
    o = dstT[:, c0:c0 + nchunk, col:col + 128]
    i_ = tp[:, 0:nchunk * 128].rearrange("p (c t) -> p c t", c=nchunk)
    if cx.rrt % 2 == 0:
        p.act(lambda e, o=o, i_=i_: e.copy(out=o, in_=i_), [tp], [dstT])
    else:
        p.dve(lambda e, o=o, i_=i_: e.tensor_copy(out=o, in_=i_), [tp], [dstT])


def emit_out(p, cx, src_buf, src_ap, dsts, tmps, npart=128):
    if len(dsts) == 1:
        p.dma(dsts[0], src_ap, [src_buf], [])
        return
    for s_ in range(2):
        t = tmps[s_]
        p.pool(lambda e, o=t[:], a=src_ap, m=cx.rk[0:npart, s_:s_ + 1]: e.tensor_scalar(
            out=o, in0=a, scalar1=m, scalar2=None, op0=ALU.mult), [src_buf, cx.rk], [t])
        p.dma(dsts[s_], t[:], [t], [])


def load_sel(p, cx, dst_buf, dst_ap, srcs, tmp, npart=128):
    if len(srcs) == 1:
        p.dma(dst_ap, srcs[0], [], [dst_buf])
        return
    tb, ta = tmp
    p.dma(ta, srcs[0], [], [tb])
    p.pool(lambda e, o=dst_ap, a=ta, m=cx.rk[0:npart, 0:1]: e.tensor_scalar(
        out=o, in0=a, scalar1=m, scalar2=None, op0=ALU.mult), [tb, cx.rk], [dst_buf])
    p.dma(ta, srcs[1], [], [tb])
    p.dve(lambda e, o=dst_ap, b=ta, m=cx.rk[0:npart, 1:2]: e.scalar_tensor_tensor(
        out=o, in0=b, scalar=m, in1=o, op0=ALU.mult, op1=ALU.add), [tb, cx.rk, dst_buf], [dst_buf])


def make_ctx(p, ident_ap, rank_ap=None):
    cx = Ctx()
    cx.rr = 0
    cx.rrt = 0
    cx.ident = load_const(p, ident_ap, [128, 128], BF16, "ident")
    cx.eps = p.sb([128, 1], F32, "eps")
    p.dve(lambda e: e.memset(cx.eps[:], EPS), [], [cx.eps])
    cx.small = [p.sb([128, 4], F32, f"small{i}") for i in range(4)]
    cx.small8 = [p.sb([128, 24], F32, f"small8_{i}") for i in range(4)]
    cx.junk = p.sb([128, D], BF16, "junk")
    cx.xn = [p.sb([128, D], BF16, f"xn{i}") for i in range(1)]
    cx.tps = [p.ps(6 + i, BF16, f"tps{i}") for i in range(2)]
    cx.mm = [p.ps(i, F32, f"mm{i}") for i in range(6)]
    cx.stage = [p.sb([128, STW], F32, f"stage{i}") for i in range(2)]
    cx.rk = None
    if rank_ap is not None:
        cx.rk = load_const(p, rank_ap, [128, 2], F32, "rank")
    return cx


TG = 512
NSUB = TG // 128
KC = D // 128


def stage_ffn(p, cx, T, x_in, x_out, gain, wg, wu, wd):
    m = p.mark()
    rr = [0]
    mm_ps = cx.mm
    g_bc = p.sb([128, D], F32, "ffn_g")
    p.dma(g_bc[:], gain.partition_broadcast(128), [], [g_bc])
    Wg = load_weight_bf16(p, wg, D, DFF, "Wg", cx.stage, rr)
    Wu = load_weight_bf16(p, wu, D, DFF, "Wu", cx.stage, rr)
    Wd = load_weight_bf16(p, wd, DFF, D, "Wd", cx.stage, rr)
    nf = DFF // 128
    xpool = [p.sb([128, D], F32, f"ffn_x{i}") for i in range(5)]
    xnT = p.sb([128, KC, TG], BF16, "ffn_xnT")
    hT = p.sb([128, nf, TG], BF16, "ffn_hT")
    sg = [p.sb([128, TG], F32, f"ffn_sg{i}") for i in range(1)]
    xv = x_in.rearrange("(n p) d -> n p d", p=128)
    ov = x_out.rearrange("(n p) d -> n p d", p=128)
    for g in range(T // TG):
        xs = [xpool[(g * NSUB + i) % len(xpool)] for i in range(NSUB)]
        for i in range(NSUB):
            p.dma(xs[i][:], xv[g * NSUB + i], [], [xs[i]])
        rms_to_T(p, cx, xs, g_bc, xnT, 0, NSUB)
        for f in range(nf):
            pg = mm_ps[(2 * f) % len(mm_ps)]
            pu = mm_ps[(2 * f + 1) % len(mm_ps)]
            for k in range(KC):
                p.pe(lambda e, o=pg[:], w=Wg[k][:, f * 128:(f + 1) * 128], a=xnT[:, k, :], k=k:
                     e.matmul(o, lhsT=w, rhs=a, start=(k == 0), stop=(k == KC - 1)), [Wg[k], xnT], [pg])
            for k in range(KC):
                p.pe(lambda e, o=pu[:], w=Wu[k][:, f * 128:(f + 1) * 128], a=xnT[:, k, :], k=k:
                     e.matmul(o, lhsT=w, rhs=a, start=(k == 0), stop=(k == KC - 1)), [Wu[k], xnT], [pu])
            s = sg[f % len(sg)]
            p.act(lambda e, o=s[:], a=pg[:]: e.activation(out=o, in_=a, func=AF.Silu), [pg], [s])
            p.dve(lambda e, o=hT[:, f, :], a=s[:], b=pu[:]: e.tensor_tensor(out=o, in0=a, in1=b, op=ALU.mult),
                  [s, pu], [hT])
        for i in range(NSUB):
            for half in range(2):
                py = mm_ps[(2 * i + half) % len(mm_ps)]
                for f in range(nf):
                    p.pe(lambda e, o=py[:], a=hT[:, f, i * 128:(i + 1) * 128], w=Wd[f][:, half * 512:(half + 1) * 512], f=f:
                         e.matmul(o, lhsT=a, rhs=w, start=(f == 0), stop=(f == nf - 1)), [hT, Wd[f]], [py])
                p.dve(lambda e, o=xs[i][:, half * 512:(half + 1) * 512], a=py[:]: e.scalar_tensor_tensor(
                    out=o, in0=a, scalar=0.5, in1=o, op0=ALU.mult, op1=ALU.add), [py, xs[i]], [xs[i]])
            p.dma(ov[g * NSUB + i], xs[i][:], [xs[i]], [])
    p.barrier()
    p.release(m)


def stage_outproj(p, cx, T, x_in, attn_in, x_out, wo):
    attn_list = list(attn_in) if isinstance(attn_in, (list, tuple)) else [attn_in]
    m = p.mark()
    rr = [0]
    mm_ps = cx.mm
    Wo = load_weight_bf16(p, wo, D, D, "Wo", cx.stage, rr)
    xpool = [p.sb([128, D], F32, f"op_x{i}") for i in range(6)]
    apool = [p.sb([128, D], BF16, f"op_a{i}") for i in range(6)]
    aT = p.sb([128, KC, TG], BF16, "op_aT")
    xv = x_in.rearrange("(n p) d -> n p d", p=128)
    avs = [a.rearrange("(n p) d -> n p d", p=128) for a in attn_list]
    acand = [p.sb([128, D], BF16, f"op_c{i}") for i in range(2)] if len(avs) == 2 else None
    ov = x_out.rearrange("(n p) d -> n p d", p=128)
    for g in range(T // TG):
        xs = [xpool[(g * NSUB + i) % len(xpool)] for i in range(NSUB)]
        as_ = [apool[(g * NSUB + i) % len(apool)] for i in range(NSUB)]
        for i in range(NSUB):
            p.dma(xs[i][:], xv[g * NSUB + i], [], [xs[i]])
            load_sel(p, cx, as_[i], as_[i][:], [a[g * NSUB + i] for a in avs],
                     (acand[i % 2], acand[i % 2][:]) if acand else None)
        for i in range(NSUB):
            transpose_to(p, cx, as_[i], aT, 0, 128 * i, KC)
        for i in range(NSUB):
            for half in range(2):
                py = mm_ps[(2 * i + half) % len(mm_ps)]
                for k in range(KC):
                    p.pe(lambda e, o=py[:], a=aT[:, k, i * 128:(i + 1) * 128], w=Wo[k][:, half * 512:(half + 1) * 512], k=k:
                         e.matmul(o, lhsT=a, rhs=w, start=(k == 0), stop=(k == KC - 1)), [aT, Wo[k]], [py])
                p.dve(lambda e, o=xs[i][:, half * 512:(half + 1) * 512], a=py[:]: e.tensor_tensor(
                    out=o, in0=a, in1=o, op=ALU.add), [py, xs[i]], [xs[i]])
            p.dma(ov[g * NSUB + i], xs[i][:], [xs[i]], [])
    p.barrier()
    p.release(m)


def stage_qkv(p, cx, T, layer, x_in, gain, w_in, gq, gk, cos, sin, qT_out, kT_out, v_out, bf=None, logf_out=None,
              ktok_out=None, pm=False):
    m = p.mark()
    rr = [0]
    mm_ps = cx.mm
    NW = 3 * D + (8 if layer == 1 else 0)
    g_bc = p.sb([128, D], F32, "qkv_g")
    p.dma(g_bc[:], gain.partition_broadcast(128), [], [g_bc])
    Win = load_weight_bf16(p, w_in, D, NW, "Win", cx.stage, rr)
    gains = {}
    for nm, ap, sc in [("q", gq, 0.125), ("k", gk, 1.0)]:
        for j, a in enumerate(ap):
            t = p.sb([128, DH], F32, f"gain_{nm}{j}")
            p.dma(t[:], a.partition_broadcast(128), [], [t])
            if sc != 1.0:
                p.dve(lambda e, t=t, sc=sc: e.tensor_scalar(out=t[:], in0=t[:], scalar1=sc, scalar2=None, op0=ALU.mult),
                      [t], [t])
            gains[(nm, j)] = t
    if layer == 1:
        bf_bc = p.sb([128, 8], F32, "bf_bc")
        p.dma(bf_bc[:], bf.partition_broadcast(128), [], [bf_bc])
        lf = [p.sb([128, 32], F32, f"lf{i}") for i in range(2)]
    cs_pool = [(p.sb([128, 32], F32, f"cos{i}"), p.sb([128, 32], F32, f"sin{i}")) for i in range(2)]
    xpool = [p.sb([128, D], F32, f"qkv_x{i}") for i in range(6)]
    xnT = p.sb([128, KC, TG], BF16, "qkv_xnT")
    qT = p.sb([128, 8, TG], BF16, "qkv_qT")
    kT = p.sb([128, 8, TG], BF16, "qkv_kT")
    vb = [p.sb([128, D], BF16, f"qkv_v{i}") for i in range(2)]
    sq_l = [p.sb([128, 8, DH], F32, f"qkv_sq{i}") for i in range(2)]
    qn_l = [p.sb([128, 8, DH], F32, f"qkv_qn{i}") for i in range(2)]
    tmp_l = [[p.sb([128, 8, 32], F32, f"qkv_t{j}_{i}") for i in range(4)] for j in range(2)]
    qb = [p.sb([128, 8, DH], BF16, f"qkv_qb{i}") for i in range(4)]
    L = lambda a: list(a) if isinstance(a, (list, tuple)) else [a]
    xv = x_in.rearrange("(n p) d -> n p d", p=128)
    if pm:
        vvs = [[a[:, :, n_, :].rearrange("n p d -> p n d") for n_ in range(T // 128)] for a in L(v_out)]
    else:
        vvs = [a.rearrange("(n p) d -> n p d", p=128) for a in L(v_out)]
    cv = cos.rearrange("(n p) d -> n p d", p=128)
    sv = sin.rearrange("(n p) d -> n p d", p=128)
    qTvs = [a.rearrange("(hp two) d t -> (two d) hp t", two=2) for a in L(qT_out)]
    kTvs = [a.rearrange("(hp two) d t -> (two d) hp t", two=2) for a in L(kT_out)]
    two = len(qTvs) == 2
    tq = [p.sb([128, 8, TG], BF16, f"qkv_tq{i}") for i in range(2)] if two else None
    tv = [p.sb([128, NH, DH], BF16, f"qkv_tv{i}") for i in range(2)] if two else None
    tk = [p.sb([128, 8, DH], BF16, f"qkv_tk{i}") for i in range(2)] if two else None
    tl = [p.sb([128, 8], F32, f"qkv_tl{i}") for i in range(2)] if two else None
    if ktok_out is None:
        ktvs = None
    elif pm:
        ktvs = [[a[:, :, n_, :].rearrange("n p d -> p n d") for n_ in range(T // 128)] for a in L(ktok_out)]
    else:
        ktvs = [a.rearrange("(n p) d -> n p d", p=128) for a in L(ktok_out)]
    if layer == 1:
        lvs = L(logf_out)
        lfT = [p.sb([8, TG], F32, f"qkv_lfT{i}") for i in range(2)]
        tlT = [p.sb([8, TG], F32, f"qkv_tlT{i}") for i in range(2)] if two else None
        ident32 = p.sb([128, 128], F32, "ident32")
        p.dve(lambda e: e.tensor_copy(out=ident32[:], in_=cx.ident[:]), [cx.ident], [ident32])
    nb = 0
    for g in range(T // TG):
        xs = [xpool[(g * NSUB + i) % len(xpool)] for i in range(NSUB)]
        for i in range(NSUB):
            p.dma(xs[i][:], xv[g * NSUB + i], [], [xs[i]])
        rms_to_T(p, cx, xs, g_bc, xnT, 0, NSUB)
        for i in range(NSUB):
            n = g * NSUB + i
            cosb, sinb = cs_pool[n % 2]
            p.dma(cosb[:], cv[n], [], [cosb])
            p.dma(sinb[:], sv[n], [], [sinb])
            vbuf = vb[n % 2]
            for cg in range(6):
                ps = mm_ps[(nb) % len(mm_ps)]
                nb += 1
                for k in range(KC):
                    p.pe(lambda e, o=ps[:], a=xnT[:, k, i * 128:(i + 1) * 128], w=Win[k][:, cg * 512:(cg + 1) * 512], k=k:
                         e.matmul(o, lhsT=a, rhs=w, start=(k == 0), stop=(k == KC - 1)), [xnT, Win[k]], [ps])
                if cg >= 4:
                    p.act(lambda e, o=vbuf[:, (cg - 4) * 512:(cg - 3) * 512], a=ps[:]: e.copy(out=o, in_=a), [ps], [vbuf])
                    continue
                isq = cg < 2
                hg = cg % 2
                normed = (layer == 1) or (hg == 1)
                roped = (hg == 1)
                qbuf = qb[nb % 4]
                sq, qn, tmp = sq_l[nb % 2], qn_l[nb % 2], tmp_l[nb % 2]
                ps3 = ps[:].rearrange("p (h d) -> p h d", h=8)
                if not normed:
                    p.act(lambda e, o=qbuf[:], a=ps3, sc=(0.125 if isq else 1.0): e.mul(out=o, in_=a, mul=sc), [ps], [qbuf])
                else:
                    gj = 0 if layer == 0 else hg
                    gt = gains[("q" if isq else "k", gj)]
                    ss = cx.small8[cx.rr % len(cx.small8)]
                    cx.rr += 1
                    p.act(lambda e, o=sq[:], a=ps3: e.activation(out=o, in_=a, func=AF.Square), [ps], [sq])
                    p.dve(lambda e, o=ss[:, 0:8], a=sq[:]: e.tensor_reduce(out=o, in_=a, axis=AX.X, op=ALU.add), [sq], [ss])
                    p.act(lambda e, s=ss: e.activation(out=s[:, 8:16], in_=s[:, 0:8], func=AF.Sqrt, scale=1.0 / DH,
                                                       bias=cx.eps[:, 0:1]), [ss, cx.eps], [ss])
                    p.dve(lambda e, s=ss: e.reciprocal(out=s[:, 16:24], in_=s[:, 8:16]), [ss], [ss])
                    p.dve(lambda e, o=qn[:], a=ps3, s=ss: e.tensor_tensor(
                        out=o, in0=a, in1=s[:, 16:24].unsqueeze(2).to_broadcast([128, 8, DH]), op=ALU.mult), [ps, ss], [qn])
                    gbc = gt[:].unsqueeze(1).to_broadcast([128, 8, DH])
                    if not roped:
                        p.dve(lambda e, o=qbuf[:], a=qn[:], g_=gbc: e.tensor_tensor(out=o, in0=a, in1=g_, op=ALU.mult),
                              [qn, gt], [qbuf])
                    else:
                        p.dve(lambda e, o=qn[:], a=qn[:], g_=gbc: e.tensor_tensor(out=o, in0=a, in1=g_, op=ALU.mult),
                              [qn, gt], [qn])
                        cb = cosb[:].unsqueeze(1).to_broadcast([128, 8, 32])
                        sb_ = sinb[:].unsqueeze(1).to_broadcast([128, 8, 32])
                        x1 = qn[:, :, 0:32]
                        x2 = qn[:, :, 32:64]
                        t0, t1, t2, t3 = tmp
                        p.pool(lambda e, o=t0[:], a=x1, b=cb: e.tensor_tensor(out=o, in0=a, in1=b, op=ALU.mult), [qn, cosb], [t0])
                        p.pool(lambda e, o=t1[:], a=x2, b=sb_: e.tensor_tensor(out=o, in0=a, in1=b, op=ALU.mult), [qn, sinb], [t1])
                        p.dve(lambda e, o=t2[:], a=x2, b=cb: e.tensor_tensor(out=o, in0=a, in1=b, op=ALU.mult), [qn, cosb], [t2])
                        p.dve(lambda e, o=t3[:], a=x1, b=sb_: e.tensor_tensor(out=o, in0=a, in1=b, op=ALU.mult), [qn, sinb], [t3])
                        p.dve(lambda e, o=qbuf[:, :, 0:32], a=t0[:], b=t1[:]: e.tensor_tensor(out=o, in0=a, in1=b, op=ALU.subtract),
                              [t0, t1], [qbuf])
                        p.dve(lambda e, o=qbuf[:, :, 32:64], a=t2[:], b=t3[:]: e.tensor_tensor(out=o, in0=a, in1=b, op=ALU.add),
                              [t2, t3], [qbuf])
                transpose_to(p, cx, qbuf, qT if isq else kT, hg * 4, 128 * i, 4,
                             flat=qbuf.t.rearrange("p h d -> p (h d)"))
                if ktvs is not None and (not isq) and hg == 0:
                    emit_out(p, cx, qbuf, qbuf[:] if pm else qbuf.t.rearrange("p h d -> p (h d)"), [a[n] for a in ktvs], tk)
            if layer == 1:
                ps = mm_ps[(nb) % len(mm_ps)]
                nb += 1
                for k in range(KC):
                    p.pe(lambda e, o=ps[:, 0:8], a=xnT[:, k, i * 128:(i + 1) * 128], w=Win[k][:, 3 * D:3 * D + 8], k=k:
                         e.matmul(o, lhsT=a, rhs=w, start=(k == 0), stop=(k == KC - 1)), [xnT, Win[k]], [ps])
                l = lf[n % 2]
                p.dve(lambda e, o=l[:, 0:8], a=ps[:, 0:8], b=bf_bc[:]: e.tensor_tensor(out=o, in0=a, in1=b, op=ALU.add),
                      [ps, bf_bc], [l])
                p.act(lambda e, l=l: e.activation(out=l[:, 8:16], in_=l[:, 0:8], func=AF.Exp, scale=-1.0), [l], [l])
                p.act(lambda e, l=l: e.activation(out=l[:, 16:24], in_=l[:, 8:16], func=AF.Ln, bias=1.0), [l], [l])
                p.dve(lambda e, l=l: e.tensor_scalar(out=l[:, 24:32], in0=l[:, 16:24], scalar1=-1.0, scalar2=None, op0=ALU.mult),
                      [l], [l])
                pst = mm_ps[(nb) % len(mm_ps)]
                nb += 1
                p.pe(lambda e, o=pst[0:8, 0:128], a=l[:, 24:32]: e.transpose(out=o, in_=a, identity=ident32[:]),
                     [l, ident32], [pst])
                lt = lfT[g % 2]
                p.act(lambda e, o=lt[:, i * 128:(i + 1) * 128], a=pst[0:8, 0:128]: e.copy(out=o, in_=a), [pst], [lt])
                if i == NSUB - 1:
                    emit_out(p, cx, lt, lt[:], [a[:, g * TG:(g + 1) * TG] for a in lvs], tlT, npart=8)
            emit_out(p, cx, vbuf, vbuf[:].rearrange("p (h d) -> p h d", h=NH) if pm else vbuf[:], [a[n] for a in vvs], tv)
        emit_out(p, cx, qT, qT[:], [a[:, :, g * TG:(g + 1) * TG] for a in qTvs], tq)
        emit_out(p, cx, kT, kT[:], [a[:, :, g * TG:(g + 1) * TG] for a in kTvs], tq)
    p.barrier()
    p.release(m)


def build_test_tok(T, layer, which):
    nc, stack, p = new_prog()
    with stack:
        dt = lambda n, sh, d=F32, k="ExternalInput": nc.dram_tensor(n, sh, d, kind=k).ap()
        ident = dt("ident", [128, 128], BF16)
        cx = make_ctx(p, ident)
        x = dt("x", [T, D])
        if which == "ffn":
            y = dt("y", [T, D], F32, "ExternalOutput")
            stage_ffn(p, cx, T, x, y, dt("gain", [D]), dt("wg", [D, DFF]), dt("wu", [D, DFF]), dt("wd", [DFF, D]))
        elif which == "outproj":
            y = dt("y", [T, D], F32, "ExternalOutput")
            stage_outproj(p, cx, T, x, dt("attn", [T, D], BF16), y, dt("wo", [D, D]))
        elif which == "qkv":
            NW = 3 * D + (8 if layer == 1 else 0)
            ng = 1 if layer == 0 else 2
            gq = [dt(f"gq{j}", [DH]) for j in range(ng)]
            gk = [dt(f"gk{j}", [DH]) for j in range(ng)]
            qT = dt("qT", [NH, DH, T], BF16, "ExternalOutput")
            kT = dt("kT", [NH, DH, T], BF16, "ExternalOutput")
            v = dt("v", [T, D], BF16, "ExternalOutput")
            kw = {}
            if layer == 1:
                kw = dict(bf=dt("bf", [8]), logf_out=dt("logf", [8, T], F32, "ExternalOutput"))
            stage_qkv(p, cx, T, layer, x, dt("gain", [D]), dt("w_in", [D, NW]), gq, gk,
                      dt("cos", [T, 32]), dt("sin", [T, 32]), qT, kT, v, **kw)
        p.emit()
    return nc


class AttnWork:
    pass


def attn_setup(p, cx, S, layer, consts, fused=False):
    W = AttnWork()
    W.S = S
    W.fused = fused
    if fused:
        W.cq = p.sb([64, S], BF16, "a_cq")
        W.cv = p.sb([128, S // 128, DH], BF16, "a_cv")
        W.otmp = [p.sb([128, 4, DH], BF16, f"a_otmp{i}") for i in range(2)]
        if layer == 0:
            W.J = load_const(p, consts["antiid"], [128, 128], BF16, "antiid")
            W.ktok = p.sb([128, S // 128, DH], BF16, "a_ktok")
            W.vnat = p.sb([128, S // 128, DH], BF16, "a_vnat")
    W.qT = p.sb([64, S], BF16, "a_qT")
    W.kT = p.sb([64, S], BF16, "a_kT")
    W.v = p.sb([128, S // 128, 65], BF16, "a_v")
    p.dve(lambda e: e.memset(W.v[:, :, 64:65], 1.0), [], [W.v])
    W.tri = load_const(p, consts["tri"], [128, 128], BF16, "tri")
    W.osb = [p.sb([128, 4, DH], BF16, f"a_o{i}") for i in range(2)]
    W.rinv = [p.sb([128, 1], F32, f"a_rinv{i}") for i in range(4)]
    W.pt = [p.sb([128, 512], BF16, f"a_pt{i}") for i in range(4)]
    if layer == 0:
        W.sbmask = load_const(p, consts["sbmask"], [128, 128], BF16, "sbmask")
        W.onehot = load_const(p, consts["onehot"], [32, S], BF16, "onehot")
        W.QA = p.sb([32, S], BF16, "a_QA")
        W.ones = p.sb([128, 512], F32, "a_ones")
        p.dve(lambda e: e.memset(W.ones[:], 1.0), [], [W.ones])
        W.e = [p.sb([128, 512], F32, f"a_e{i}") for i in range(4)]
        W.sp = [p.sb([128, 512], F32, f"a_sp{i}") for i in range(4)]
        W.cs = [p.sb([128, 512], F32, f"a_cs{i}") for i in range(4)]
        W.ecs = [p.sb([128, 512], F32, f"a_ecs{i}") for i in range(4)]
        W.a = [p.sb([128, 512], BF16, f"a_a{i}") for i in range(4)]
        W.at = [p.sb([128, 512], BF16, f"a_at{i}") for i in range(2)]
        W.km = p.sb([64, 32], F32, "a_km")
        W.kmh = p.sb([64, 32], BF16, "a_kmh")
        W.kml = p.sb([64, 32], BF16, "a_kml")
        W.gm = p.sb([128, 32], F32, "a_gm")
        W.top8 = [p.sb([128, 8], F32, f"a_top{i}") for i in range(2)]
        W.sel = [p.sb([128, 32], F32, f"a_sel{i}") for i in range(2)]
        W.bsel = [p.sb([128, 32], BF16, f"a_bsel{i}") for i in range(2)]
    else:
        W.dilm = load_const(p, consts["dilm"].rearrange("k p t -> p k t"), [128, 20, 512], BF16, "dilm")
        W.QA = p.sb([6, S], BF16, "a_QA")
        W.KA = p.sb([6, S], BF16, "a_KA")
    return W


def _L(a):
    return list(a) if isinstance(a, (list, tuple)) else [a]


def load_job(p, cx, W, q_ap, k_ap, v_ap):
    tq = (W.cq, W.cq[:]) if W.fused else None
    tv = (W.cv, W.cv[:]) if W.fused else None
    load_sel(p, cx, W.qT, W.qT[:], _L(q_ap), tq, npart=64)
    if k_ap is not None:
        load_sel(p, cx, W.kT, W.kT[:], _L(k_ap), tq, npart=64)
    if v_ap is not None:
        _load_tok(p, cx, W, W.v, W.v[:, :, 0:DH], v_ap)


def _load_tok(p, cx, W, dst_buf, dst_ap, src):
    srcs = _L(src)
    if len(srcs[0].shape) == 2:
        load_sel(p, cx, dst_buf, dst_ap, [a.rearrange("(k p) d -> p k d", p=128) for a in srcs],
                 (W.cv, W.cv[:]) if W.fused else None)
    else:
        r = lambda ap: ap.rearrange("p (h k) d -> p h k d", h=2)
        load_sel(p, cx, dst_buf, r(dst_ap), srcs, (W.cv, r(W.cv[:])))


def store_out(p, cx, W, osb, dsts):
    emit_out(p, cx, osb, osb[:], dsts, W.otmp if W.fused else None)


def sb_reverse(p, cx, W, ktok_ap, v_ap):
    S = W.S
    nb = S // 128
    _load_tok(p, cx, W, W.ktok, W.ktok[:], ktok_ap)
    _load_tok(p, cx, W, W.vnat, W.vnat[:], v_ap)
    for r0 in range(0, nb, 4):
        ps = p.ps(r0 // 4 % 2, F32)
        for rb in range(r0, r0 + 4):
            k = nb - 1 - rb
            p.pe(lambda e, o=ps[0:DH, (rb - r0) * 128:(rb - r0 + 1) * 128], a=W.ktok[:, k, :]:
                 e.matmul(o, lhsT=a, rhs=W.J[:], start=True, stop=True), [W.ktok, W.J], [ps])
        p.act(lambda e, o=W.kT[:, r0 * 128:(r0 + 4) * 128], a=ps[0:DH, :]: e.copy(out=o, in_=a), [ps], [W.kT])
    for r0 in range(0, nb, 8):
        ps = p.ps(2 + r0 // 8 % 2, F32)
        for rb in range(r0, r0 + 8):
            k = nb - 1 - rb
            p.pe(lambda e, o=ps[:, (rb - r0) * DH:(rb - r0 + 1) * DH], a=W.vnat[:, k, :]:
                 e.matmul(o, lhsT=W.J[:], rhs=a, start=True, stop=True), [W.vnat, W.J], [ps])
        p.dve(lambda e, o=W.v[:, r0:r0 + 8, 0:DH], a=ps[:].rearrange("p (k d) -> p k d", d=DH): e.tensor_copy(out=o, in_=a),
              [ps], [W.v])


def sb_job(p, cx, W, q_ap, k_ap, v_ap, o_ap, ktok_ap=None):
    S = W.S
    if ktok_ap is None:
        load_job(p, cx, W, q_ap, k_ap, v_ap)
    else:
        load_job(p, cx, W, q_ap, None, None)
        sb_reverse(p, cx, W, ktok_ap, v_ap)
    nb = S // 128
    ovs = [a.rearrange("(g q p) d -> g p q d", p=128, q=4) for a in _L(o_ap)]
    NS = len(W.e)
    LAG = NS - 1
    units = []
    for i in range(nb):
        rb0 = nb - 1 - i
        nun = (128 * (i + 1) + 511) // 512
        for u in range(nun):
            c0 = rb0 * 128 + 512 * u
            units.append(dict(i=i, u=u, nun=nun, c0=c0, N=min(512, S - c0), idx=len(units)))

    def stage_a(t):
        i, u, c0, N, k = t["i"], t["u"], t["c0"], t["N"], t["idx"]
        z = p.ps(k % 2, F32)
        p.pe(lambda e, o=z[:, 0:N], a=W.qT[:, i * 128:(i + 1) * 128], b=W.kT[:, c0:c0 + N], u=u:
             e.matmul(o, lhsT=a, rhs=b, start=True, stop=(u != 0)), [W.qT, W.kT], [z])
        if u == 0:
            p.pe(lambda e, o=z[:, 0:128]: e.matmul(o, lhsT=cx.ident[:], rhs=W.sbmask[:], start=False, stop=True),
                 [cx.ident, W.sbmask], [z])
        e_sb, sp, cs, ecs, a = (W.e[k % NS], W.sp[k % NS], W.cs[k % NS], W.ecs[k % NS], W.a[k % NS])
        p.act(lambda e, o=e_sb[:, 0:N], i_=z[:, 0:N]: e.activation(out=o, in_=i_, func=AF.Exp), [z], [e_sb])
        p.act(lambda e, o=sp[:, 0:N], i_=e_sb[:, 0:N]: e.activation(out=o, in_=i_, func=AF.Ln, bias=1.0), [e_sb], [sp])
        if u == 0:
            init, rd = 0.0, [W.ones, sp]
        else:
            cprev = W.cs[(k - 1) % NS]
            pN = units[k - 1]["N"]
            init, rd = cprev[:, pN - 1:pN], [W.ones, sp, cprev]
        p.dve(lambda e, o=cs[:, 0:N], d0=W.ones[:, 0:N], d1=sp[:, 0:N], init=init: e.tensor_tensor_scan(
            out=o, data0=d0, data1=d1, initial=init, op0=ALU.mult, op1=ALU.add), rd, [cs])

    def stage_a2(t):
        N, k = t["N"], t["idx"]
        e_sb, cs, ecs, a = (W.e[k % NS], W.cs[k % NS], W.ecs[k % NS], W.a[k % NS])
        p.act(lambda e, o=ecs[:, 0:N], i_=cs[:, 0:N]: e.activation(out=o, in_=i_, func=AF.Exp, scale=-1.0), [cs], [ecs])
        p.pool(lambda e, o=a[:, 0:N], x=e_sb[:, 0:N], y=ecs[:, 0:N]: e.tensor_tensor(out=o, in0=x, in1=y, op=ALU.mult),
               [e_sb, ecs], [a])

    def stage_b(t):
        i, u, nun, c0, N, k = t["i"], t["u"], t["nun"], t["c0"], t["N"], t["idx"]
        nk = N // 128
        a = W.a[k % NS]
        at = W.at[k % 2]
        o_ps = p.ps(4 + i % 2, F32)
        atp = p.ps(2 + k % 2, BF16)
        for kb in range(nk):
            p.pe(lambda e, o=atp[:, kb * 128:(kb + 1) * 128], x=a[:, kb * 128:(kb + 1) * 128]:
                 e.transpose(out=o, in_=x, identity=cx.ident[:]), [a, cx.ident], [atp])
        p.act(lambda e, o=at[:, 0:N], x=atp[:, 0:N]: e.copy(out=o, in_=x), [atp], [at])
        for kb in range(nk):
            p.pe(lambda e, o=o_ps[:, 0:DH], x=at[:, kb * 128:(kb + 1) * 128], vv=W.v[:, c0 // 128 + kb, 0:DH],
                 st=(u == 0 and kb == 0), sp_=(u == nun - 1 and kb == nk - 1):
                 e.matmul(o, lhsT=x, rhs=vv, start=st, stop=sp_), [at, W.v], [o_ps])
        if u == nun - 1:
            osb = W.osb[(i // 4) % 2]
            p.act(lambda e, o=osb[:, i % 4, :], x=o_ps[:, 0:DH]: e.copy(out=o, in_=x), [o_ps], [osb])
            if i % 4 == 3:
                store_out(p, cx, W, osb, [a_[i // 4] for a_ in ovs])

    nu = len(units)
    for k in range(nu + LAG):
        if k < nu:
            stage_a(units[k])
        if 0 <= k - 1 < nu:
            stage_a2(units[k - 1])
        if 0 <= k - LAG < nu:
            stage_b(units[k - LAG])


def moba_pre(p, cx, W):
    S = W.S
    nkb = S // 256
    p.dve(lambda e: e.tensor_reduce(out=W.km[:, 0:nkb], in_=W.kT[:].rearrange("p (n s) -> p n s", s=256),
                                    axis=AX.X, op=ALU.add), [W.kT], [W.km])
    p.dve(lambda e: e.tensor_scalar(out=W.km[:, 0:nkb], in0=W.km[:, 0:nkb], scalar1=1.0 / 256, scalar2=None, op0=ALU.mult),
          [W.km], [W.km])
    p.dve(lambda e: e.tensor_copy(out=W.kmh[:, 0:nkb], in_=W.km[:, 0:nkb]), [W.km], [W.kmh])
    p.dve(lambda e: e.tensor_tensor(out=W.kml[:, 0:nkb], in0=W.km[:, 0:nkb], in1=W.kmh[:, 0:nkb], op=ALU.subtract),
          [W.km, W.kmh], [W.kml])
    p.dve(lambda e: e.memset(W.gm[:], -1e30), [], [W.gm])
    for i in range(S // 128):
        own = i // 2
        bsel = W.bsel[i % 2]
        if own == 0:
            p.dve(lambda e, b=bsel: e.memset(b[:], 0.0), [], [bsel])
        else:
            gp = p.ps(4 + i % 2, F32)
            p.pe(lambda e, o=gp[:, 0:nkb], a=W.qT[:, i * 128:(i + 1) * 128]: e.matmul(o, lhsT=a, rhs=W.kmh[:, 0:nkb], start=True, stop=False),
                 [W.qT, W.kmh], [gp])
            p.pe(lambda e, o=gp[:, 0:nkb], a=W.qT[:, i * 128:(i + 1) * 128]: e.matmul(o, lhsT=a, rhs=W.kml[:, 0:nkb], start=False, stop=True),
                 [W.qT, W.kml], [gp])
            p.dve(lambda e, o=W.gm[:, 0:own], a=gp[:, 0:own]: e.tensor_copy(out=o, in_=a), [gp], [W.gm])
            top = W.top8[i % 2]
            sel = W.sel[i % 2]
            p.dve(lambda e, t=top: e.max(out=t[:], in_=W.gm[:]), [W.gm], [top])
            p.dve(lambda e, t=top: e.tensor_scalar(out=t[:, 3:4], in0=t[:, 2:3], scalar1=-1e29, scalar2=None, op0=ALU.max),
                  [top], [top])
            p.dve(lambda e, s=sel, t=top: e.tensor_scalar(out=s[:], in0=W.gm[:], scalar1=t[:, 3:4], scalar2=None, op0=ALU.is_ge),
                  [W.gm, top], [sel])
            p.dve(lambda e, s=sel, b=bsel: e.tensor_scalar(out=b[:], in0=s[:], scalar1=-NEG, scalar2=NEG, op0=ALU.mult, op1=ALU.add),
                  [sel], [bsel])
            p.dve(lambda e, b=bsel, own=own: e.memset(b[:, own:own + 1], 0.0), [], [bsel])
        tp = cx.tps[i % 2]
        p.pe(lambda e, o=tp[0:32, 0:128], b=bsel: e.transpose(out=o, in_=b[:], identity=cx.ident[:]), [bsel, cx.ident], [tp])
        p.act(lambda e, o=W.QA[:, i * 128:(i + 1) * 128], x=tp[0:32, 0:128]: e.copy(out=o, in_=x), [tp], [W.QA])


def softmax_job(p, cx, W, kind, q_ap, k_ap, v_ap, o_ap, fox_rows=None):
    S = W.S
    load_job(p, cx, W, q_ap, k_ap, v_ap)
    QA = KA = None
    if kind == "moba":
        moba_pre(p, cx, W)
        QA, KA = W.QA, W.onehot
    elif kind == "fox":
        p.dve(lambda e: e.memset(W.QA[:], 1.0), [], [W.QA])
        p.dve(lambda e: e.memset(W.KA[:], 1.0), [], [W.KA])
        p.dma(W.QA[0:3, :], fox_rows[0:3, :], [], [W.QA])
        p.dma(W.KA[3:6, :], fox_rows[3:6, :], [], [W.KA])
        QA, KA = W.QA, W.KA
    ovs = [a.rearrange("(g q p) d -> g p q d", p=128, q=4) for a in _L(o_ap)]
    units = []
    for g in range(S // 512):
        jlo = max(0, 4 * g - 16) if kind == "dil" else 0
        for j in range(jlo, 4 * g + 4):
            units.append(dict(g=g, j=j, jlo=jlo, idx=len(units)))
    NP = len(W.pt)

    def stage_a(t):
        g, j, k = t["g"], t["j"], t["idx"]
        r = max(0, j - 4 * g)
        c0 = 128 * r
        N = 512 - c0
        sps = p.ps(k % 2, F32)
        q0 = g * 512 + c0
        diag = (kind != "dil") and j >= 4 * g
        p.pe(lambda e, o=sps[:, 0:N], a=W.kT[:, j * 128:(j + 1) * 128], b=W.qT[:, q0:q0 + N], sp_=(QA is None and not diag):
             e.matmul(o, lhsT=a, rhs=b, start=True, stop=sp_), [W.kT, W.qT], [sps])
        if QA is not None:
            p.pe(lambda e, o=sps[:, 0:N], a=KA[:, j * 128:(j + 1) * 128], b=QA[:, q0:q0 + N], sp_=(not diag):
                 e.matmul(o, lhsT=a, rhs=b, start=False, stop=sp_), [KA, QA], [sps])
        if diag:
            p.pe(lambda e, o=sps[:, 0:128]: e.matmul(o, lhsT=cx.ident[:], rhs=W.tri[:], start=False, stop=True),
                 [cx.ident, W.tri], [sps])
        pt = W.pt[k % NP]
        p.act(lambda e, o=pt[:, 0:N], x=sps[:, 0:N]: e.activation(out=o, in_=x, func=AF.Exp), [sps], [pt])
        if kind == "dil":
            dm = W.dilm[:, 4 * g - j + 3, c0:512]
            p.dve(lambda e, o=pt[:, 0:N], m_=dm: e.tensor_tensor(out=o, in0=o, in1=m_, op=ALU.mult), [pt, W.dilm], [pt])

    def stage_b(t):
        g, j, jlo, k = t["g"], t["j"], t["jlo"], t["idx"]
        r = max(0, j - 4 * g)
        pt = W.pt[k % NP]
        accs = [p.ps(2 + qb, F32) for qb in range(4)]
        osb = W.osb[g % 2]
        for qb in range(r, 4):
            lc = (qb - r) * 128
            last = (j == 4 * g + qb)
            p.pe(lambda e, o=accs[qb][:, 0:65], x=pt[:, lc:lc + 128], vv=W.v[:, j, :], st=(j == jlo), sp_=last:
                 e.matmul(o, lhsT=x, rhs=vv, start=st, stop=sp_), [pt, W.v], [accs[qb]])
            if last:
                rv = W.rinv[qb]
                p.dve(lambda e, o=rv[:], x=accs[qb][:, 64:65]: e.reciprocal(out=o, in_=x), [accs[qb]], [rv])
                p.dve(lambda e, o=osb[:, qb, :], x=accs[qb][:, 0:DH], s_=rv[:, 0:1]: e.tensor_scalar(
                    out=o, in0=x, scalar1=s_, scalar2=None, op0=ALU.mult), [accs[qb], rv], [osb])
        if j == 4 * g + 3:
            store_out(p, cx, W, osb, [a_[g] for a_ in ovs])

    LAG = 2
    for k, t in enumerate(units):
        stage_a(t)
        if k >= LAG:
            stage_b(units[k - LAG])
    for t in units[max(0, len(units) - LAG):]:
        stage_b(t)


def fox_pre(p, cx, S, logf4, caug):
    m = p.mark()
    PW = min(2048, S)
    lf = [p.sb([4, PW], F32, f"fx_lf{i}") for i in range(2)]
    ones = p.sb([4, PW], F32, "fx_ones")
    c = [p.sb([4, PW], F32, f"fx_c{i}") for i in range(2)]
    r1 = p.sb([4, PW], F32, "fx_r1")
    r2 = p.sb([4, PW], F32, "fx_r2")
    rows = [[p.sb([4, PW], BF16, f"fx_row{k}_{i}") for k in range(6)] for i in range(2)]
    p.dve(lambda e: e.memset(ones[:], 1.0), [], [ones])
    prev = None
    for pc in range(S // PW):
        l = lf[pc % 2]
        cc = c[pc % 2]
        rw = rows[pc % 2]
        load_sel(p, cx, l, l[:], [a[:, pc * PW:(pc + 1) * PW] for a in _L(logf4)], (r1, r1[:]), npart=4)
        init = 0.0 if prev is None else prev[:, PW - 1:PW]
        rd = [ones, l] + ([prev] if prev is not None else [])
        p.dve(lambda e, o=cc[:], d1=l[:], init=init: e.tensor_tensor_scan(out=o, data0=ones[:], data1=d1, initial=init,
                                                                         op0=ALU.mult, op1=ALU.add), rd, [cc])
        prev = cc
        hi, mid, lo, nhi, nmid, nlo = rw
        p.dve(lambda e, o=hi[:], x=cc[:]: e.tensor_copy(out=o, in_=x), [cc], [hi])
        p.dve(lambda e, o=r1[:], x=cc[:], y=hi[:]: e.tensor_tensor(out=o, in0=x, in1=y, op=ALU.subtract), [cc, hi], [r1])
        p.dve(lambda e, o=mid[:]: e.tensor_copy(out=o, in_=r1[:]), [r1], [mid])
        p.dve(lambda e, o=r2[:], y=mid[:]: e.tensor_tensor(out=o, in0=r1[:], in1=y, op=ALU.subtract), [r1, mid], [r2])
        p.dve(lambda e, o=lo[:]: e.tensor_copy(out=o, in_=r2[:]), [r2], [lo])
        for src, dst in ((hi, nhi), (mid, nmid), (lo, nlo)):
            p.dve(lambda e, o=dst[:], x=src[:]: e.tensor_scalar(out=o, in0=x, scalar1=-1.0, scalar2=None, op0=ALU.mult),
                  [src], [dst])
        for k in range(6):
            p.dma(caug[:, k, pc * PW:(pc + 1) * PW], rw[k][:], [rw[k]], [])
    p.barrier()
    p.release(m)


def build_attn(S, layer, njobs=4, only=None):
    nc, stack, p = new_prog()
    with stack:
        dt = lambda n, sh, d=BF16, k="ExternalInput": nc.dram_tensor(n, sh, d, kind=k).ap()
        ident = dt("ident", [128, 128])
        cx = make_ctx(p, ident)
        consts = {"tri": dt("tri", [128, 128])}
        if layer == 0:
            consts["sbmask"] = dt("sbmask", [128, 128])
            consts["onehot"] = dt("onehot", [32, S])
        else:
            consts["dilm"] = dt("dilm", [20, 128, 512])
            logf4 = dt("logf4", [4, S], F32)
            caug = nc.dram_tensor("caug", [4, 6, S], BF16).ap()
            fox_pre(p, cx, S, logf4, caug)
        W = attn_setup(p, cx, S, layer, consts)
        dq, dk, dv = dt("dq", [njobs, DH, S]), dt("dk", [njobs, DH, S]), dt("dv", [njobs, S, DH])
        sq, sk, sv = dt("sq", [njobs, DH, S]), dt("sk", [njobs, DH, S]), dt("sv", [njobs, S, DH])
        od = dt("od", [njobs, S, DH], BF16, "ExternalOutput")
        os_ = dt("os", [njobs, S, DH], BF16, "ExternalOutput")
        for jb in range(njobs):
            if layer == 0:
                if only in (None, "d"):
                    sb_job(p, cx, W, dq[jb], dk[jb], dv[jb], od[jb])
                if only in (None, "s"):
                    softmax_job(p, cx, W, "moba", sq[jb], sk[jb], sv[jb], os_[jb])
            else:
                if only in (None, "d"):
                    softmax_job(p, cx, W, "fox", dq[jb], dk[jb], dv[jb], od[jb], fox_rows=caug[jb])
                if only in (None, "s"):
                    softmax_job(p, cx, W, "dil", sq[jb], sk[jb], sv[jb], os_[jb])
        p.barrier()
        p.emit()
    return nc


def attn_consts_np(S, layer):
    ii = np.arange(128)
    c = {"ident": np.eye(128, dtype=np.float32).astype(NPBF),
         "tri": np.where(ii[:, None] <= ii[None, :], 0.0, NEG).astype(np.float32).astype(NPBF)}
    if layer == 0:
        c["sbmask"] = np.where(ii[None, :] + ii[:, None] >= 128, 0.0, NEG).astype(np.float32).astype(NPBF)
        c["onehot"] = (np.arange(S)[None, :] // 256 == np.arange(32)[:, None]).astype(np.float32).astype(NPBF)
    else:
        dm = np.zeros((20, 128, 512), np.float32)
        ss = np.arange(128)[:, None]
        tt = np.arange(512)[None, :]
        for d in range(-3, 17):
            o = 128 * d + tt - ss
            m = np.zeros_like(o, dtype=np.float32)
            for w, r in ((128, 1), (512, 4), (2048, 16)):
                m += ((o >= 0) & (o <= w) & (o % r == 0)).astype(np.float32)
            dm[d + 3] = m
        c["dilm"] = dm.astype(NPBF)
    return c


def build_tok_launch(T, plan):
    nc, stack, p = new_prog()
    with stack:
        dt = lambda n, sh, d=F32, k="ExternalInput": nc.dram_tensor(n, sh, d, kind=k).ap()
        ident = dt("ident", [128, 128], BF16)
        cx = make_ctx(p, ident)
        cur = dt("x", [T, D])
        for si, st in enumerate(plan):
            last_x = not any(s_ in ("outproj", "ffn") for s_ in plan[si + 1:])
            if st in ("outproj", "ffn"):
                nxt = (dt(f"xo{si}", [T, D], F32, "ExternalOutput") if last_x
                       else nc.dram_tensor(f"xs{si}", [T, D], F32).ap())
            if st == "ffn":
                stage_ffn(p, cx, T, cur, nxt, dt(f"gain{si}", [D]), dt(f"wg{si}", [D, DFF]), dt(f"wu{si}", [D, DFF]),
                          dt(f"wd{si}", [DFF, D]))
                cur = nxt
            elif st == "outproj":
                stage_outproj(p, cx, T, cur, dt(f"attn{si}", [T, D], BF16), nxt, dt(f"wo{si}", [D, D]))
                cur = nxt
            else:
                layer = int(st[-1])
                NW = 3 * D + (8 if layer == 1 else 0)
                ng = 1 if layer == 0 else 2
                gq = [dt(f"gq{j}", [DH]) for j in range(ng)]
                gk = [dt(f"gk{j}", [DH]) for j in range(ng)]
                kw = {}
                if layer == 1:
                    kw = dict(bf=dt("bf", [8]), logf_out=dt("logf", [8, T], F32, "ExternalOutput"))
                stage_qkv(p, cx, T, layer, cur, dt(f"gain{si}", [D]), dt("w_in", [D, NW]), gq, gk,
                          dt("cos", [T, 32]), dt("sin", [T, 32]),
                          dt("qT", [NH, DH, T], BF16, "ExternalOutput"), dt("kT", [NH, DH, T], BF16, "ExternalOutput"),
                          dt("v", [T, D], BF16, "ExternalOutput"), **kw)
        p.emit()
    return nc


def _run(nc, in_maps):
    in_maps = [{k: np.ascontiguousarray(v) for k, v in m.items()} for m in in_maps]
    return run_bass_kernel_spmd(nc, in_maps, core_ids=list(range(len(in_maps)))).results


def kernel_unfused(x, norm_ffn1, ffn1_w_gate, ffn1_w_up, ffn1_w_down, norm_mix, w_in_ab, g_q_b, g_k_b,
           w_in_cd, b_f, g_q_c, g_k_c, g_q_d, g_k_d, w_out, norm_ffn2, ffn2_w_gate, ffn2_w_up,
           ffn2_w_down):
    A = lambda a: np.asarray(a, dtype=np.float32)
    x = A(x)
    B, S, _ = x.shape
    NC = 8
    T = B * S // NC
    half_of = lambda c: (c // 2, c % 2)
    ident = np.eye(128, dtype=np.float32).astype(NPBF)
    inv = (1.0 / (10000.0 ** (np.arange(0, DH, 2, dtype=np.float32) / DH))).astype(np.float32)
    ang = (np.arange(S, dtype=np.float32)[:, None] * inv[None, :]).astype(np.float32)
    cos, sin = np.cos(ang).astype(np.float32), np.sin(ang).astype(np.float32)

    def tok_shard(c, arr):
        b, h = half_of(c)
        return arr[b, h * T:(h + 1) * T]

    def ffn_w(si, norm, wg, wu, wd, l):
        return {f"gain{si}": A(norm[l]), f"wg{si}": A(wg[l]), f"wu{si}": A(wu[l]), f"wd{si}": A(wd[l])}

    def attn_inputs(layer, res):
        consts = attn_consts_np(S, layer)
        maps = []
        for c in range(NC):
            b, h = half_of(c)
            QT = np.concatenate([res[2 * b]["qT"], res[2 * b + 1]["qT"]], axis=2)
            KT = np.concatenate([res[2 * b]["kT"], res[2 * b + 1]["kT"]], axis=2)
            V = np.concatenate([res[2 * b]["v"], res[2 * b + 1]["v"]], axis=0)
            V = V.reshape(S, NH, DH).transpose(1, 0, 2)
            hd = slice(4 * h, 4 * h + 4)
            hs = slice(8 + 4 * h, 12 + 4 * h)
            m = dict(consts)
            if layer == 0:
                m.update(dq=QT[hd], dk=KT[hd][:, :, ::-1], dv=V[hd][:, ::-1, :])
            else:
                LF = np.concatenate([res[2 * b]["logf"], res[2 * b + 1]["logf"]], axis=1)
                m.update(dq=QT[hd], dk=KT[hd], dv=V[hd], logf4=LF[hd])
            m.update(sq=QT[hs], sk=KT[hs], sv=V[hs])
            maps.append(m)
        return maps

    def attn_gather(res):
        outs = []
        for b in range(B):
            full = np.empty((S, NH, DH), dtype=NPBF)
            for h in range(2):
                r = res[2 * b + h]
                full[:, 4 * h:4 * h + 4] = np.asarray(r["od"]).transpose(1, 0, 2)
                full[:, 8 + 4 * h:12 + 4 * h] = np.asarray(r["os"]).transpose(1, 0, 2)
            full = full.reshape(S, D)
            outs += [full[0:T], full[T:2 * T]]
        return outs

    ncA = build_tok_launch(T, ["ffn", "qkv0"])
    mA = []
    for c in range(NC):
        m = {"ident": ident, "x": tok_shard(c, x), "w_in": A(w_in_ab[0]), "gq0": A(g_q_b[0]), "gk0": A(g_k_b[0]),
             "gain1": A(norm_mix[0]), "cos": cos[(c % 2) * T:(c % 2 + 1) * T], "sin": sin[(c % 2) * T:(c % 2 + 1) * T]}
        m.update(ffn_w(0, norm_ffn1, ffn1_w_gate, ffn1_w_up, ffn1_w_down, 0))
        mA.append(m)
    rA = _run(ncA, mA)
    ncB = build_attn(S, 0)
    rB = _run(ncB, attn_inputs(0, rA))
    att0 = attn_gather(rB)
    ncC = build_tok_launch(T, ["outproj", "ffn", "ffn", "qkv1"])
    mC = []
    for c in range(NC):
        m = {"ident": ident, "x": rA[c]["xo0"], "attn0": att0[c], "wo0": A(w_out[0]),
             "w_in": A(w_in_cd[0]), "gq0": A(g_q_c[0]), "gk0": A(g_k_c[0]), "gq1": A(g_q_d[0]), "gk1": A(g_k_d[0]),
             "bf": A(b_f[0]), "gain3": A(norm_mix[1]),
             "cos": cos[(c % 2) * T:(c % 2 + 1) * T], "sin": sin[(c % 2) * T:(c % 2 + 1) * T]}
        m.update(ffn_w(1, norm_ffn2, ffn2_w_gate, ffn2_w_up, ffn2_w_down, 0))
        m.update(ffn_w(2, norm_ffn1, ffn1_w_gate, ffn1_w_up, ffn1_w_down, 1))
        mC.append(m)
    rC = _run(ncC, mC)
    ncD = build_attn(S, 1)
    rD = _run(ncD, attn_inputs(1, rC))
    att1 = attn_gather(rD)
    ncE = build_tok_launch(T, ["outproj", "ffn"])
    mE = []
    for c in range(NC):
        m = {"ident": ident, "x": rC[c]["xo2"], "attn0": att1[c], "wo0": A(w_out[1])}
        m.update(ffn_w(1, norm_ffn2, ffn2_w_gate, ffn2_w_up, ffn2_w_down, 1))
        mE.append(m)
    rE = _run(ncE, mE)
    out = np.empty((B, S, D), dtype=np.float32)
    for c in range(NC):
        b, h = half_of(c)
        out[b, h * T:(h + 1) * T] = rE[c]["xo1"]
    return out


RG_PAIRS = [[0, 1], [2, 3], [4, 5], [6, 7]]


class XBuf:
    def __init__(self, nc, name, nelem, dt, pattern, **axes):
        ce = min(nelem, 128 * 16384)
        self.nch = nelem // ce
        assert self.nch * ce == nelem and ce % 128 == 0
        self.i = nc.dram_tensor(name + "_i", [self.nch, 128, ce // 128], dt)
        self.o = nc.dram_tensor(name + "_o", [self.nch, 128, ce // 128], dt)
        self.vi = self.i.ap().rearrange("c p f -> (c p f)").rearrange(pattern, **axes)
        self.vo = self.o.ap().rearrange("c p f -> (c p f)").rearrange(pattern, **axes)

    def exchange(self, p):
        for c in range(self.nch):
            p.collective(self.i[c], self.o[c], RG_PAIRS)


def build_fused(T, S):
    nc, stack, p = new_prog()
    with stack:
        dt = lambda n, sh, d=F32, k="ExternalInput": nc.dram_tensor(n, sh, d, kind=k).ap()
        ident = dt("ident", [128, 128], BF16)
        cx = make_ctx(p, ident, dt("rank", [128, 2]))
        x = dt("x", [T, D])
        y = dt("y", [T, D], F32, "ExternalOutput")
        cos, sin = dt("cos", [T, 32]), dt("sin", [T, 32])
        consts0 = {"tri": dt("tri", [128, 128], BF16), "sbmask": dt("sbmask", [128, 128], BF16),
                   "onehot": dt("onehot", [32, S], BF16), "antiid": dt("antiid", [128, 128], BF16)}
        consts1 = {"tri": consts0["tri"], "dilm": dt("dilm", [20, 128, 512], BF16)}
        ffn = [dict(gain=dt(f"gain{i}", [D]), wg=dt(f"wg{i}", [D, DFF]), wu=dt(f"wu{i}", [D, DFF]), wd=dt(f"wd{i}", [DFF, D]))
               for i in range(4)]
        nmix = [dt(f"nmix{i}", [D]) for i in range(2)]
        w_in = [dt("w_in0", [D, 3 * D]), dt("w_in1", [D, 3 * D + 8])]
        wo = [dt(f"wo{i}", [D, D]) for i in range(2)]
        gqb, gkb = dt("gqb", [DH]), dt("gkb", [DH])
        gqc, gkc, gqd, gkd = dt("gqc", [DH]), dt("gkc", [DH]), dt("gqd", [DH]), dt("gkd", [DH])
        bf = dt("bf", [8])
        xs = [nc.dram_tensor(f"xs{i}", [T, D], F32).ap() for i in range(5)]
        EQ = XBuf(nc, "eq", NH * DH * S, BF16, "(n d h t) -> n d h t", n=NH, d=DH, h=2)
        EK = XBuf(nc, "ek", NH * DH * S, BF16, "(n d h t) -> n d h t", n=NH, d=DH, h=2)
        EV = XBuf(nc, "ev", S * D, BF16, "(h n p k d) -> h n p k d", h=2, n=NH, p=128, d=DH)
        EKT = XBuf(nc, "ekt", S * 512, BF16, "(h n p k d) -> h n p k d", h=2, n=8, p=128, d=DH)
        ELF = XBuf(nc, "elf", 8 * S, F32, "(n h t) -> n h t", n=8, h=2)
        EA = XBuf(nc, "ea", S * D, BF16, "(h t c) -> h t c", h=2, c=D)
        caug = nc.dram_tensor("caug", [4, 6, S], BF16).ap()

        def exchange(bufs):
            for b_ in bufs:
                b_.exchange(p)
            p.barrier()

        def head_q(E, hh):
            return E.vo[hh].rearrange("d h t -> d (h t)")

        def cols(E, hh, out=False):
            v = (E.vi if out else E.vo).rearrange("h t c -> (h t) c")
            return v[:, hh * DH:(hh + 1) * DH]

        def pmv(E, hh):
            return E.vo[:, hh].rearrange("h p k d -> p h k d")

        def attention(layer):
            m = p.mark()
            if layer == 1:
                lf = ELF.vo.rearrange("n h t -> n (h t)")
                fox_pre(p, cx, S, [lf[0:4], lf[4:8]], caug)
            W = attn_setup(p, cx, S, layer, consts0 if layer == 0 else consts1, fused=True)
            for j in range(4):
                d0, d1, s0, s1 = j, 4 + j, 8 + j, 12 + j
                if layer == 0:
                    sb_job(p, cx, W, [head_q(EQ, d0), head_q(EQ, d1)], None, [pmv(EV, d0), pmv(EV, d1)],
                           [cols(EA, d0, True), cols(EA, d1, True)], ktok_ap=[pmv(EKT, d0), pmv(EKT, d1)])
                    kind = "moba"
                else:
                    softmax_job(p, cx, W, "fox", [head_q(EQ, d0), head_q(EQ, d1)], [head_q(EK, d0), head_q(EK, d1)],
                                [pmv(EV, d0), pmv(EV, d1)], [cols(EA, d0, True), cols(EA, d1, True)], fox_rows=caug[j])
                    kind = "dil"
                softmax_job(p, cx, W, kind, [head_q(EQ, s0), head_q(EQ, s1)], [head_q(EK, s0), head_q(EK, s1)],
                            [pmv(EV, s0), pmv(EV, s1)], [cols(EA, s0, True), cols(EA, s1, True)])
            p.barrier()
            p.release(m)

        def qkv(layer, xin):
            kw = {"pm": True}
            if layer == 0:
                gq, gk = [gqb], [gkb]
                kw["ktok_out"] = [EKT.vi[0], EKT.vi[1]]
            else:
                gq, gk = [gqc, gqd], [gkc, gkd]
                kw.update(bf=bf, logf_out=[ELF.vi[:, 0, :], ELF.vi[:, 1, :]])
            stage_qkv(p, cx, T, layer, xin, nmix[layer], w_in[layer], gq, gk, cos, sin,
                      [EQ.vi[:, :, 0, :], EQ.vi[:, :, 1, :]], [EK.vi[:, :, 0, :], EK.vi[:, :, 1, :]],
                      [EV.vi[0], EV.vi[1]], **kw)

        stage_ffn(p, cx, T, x, xs[0], **ffn[0])
        qkv(0, xs[0])
        exchange([EQ, EK, EV, EKT])
        attention(0)
        exchange([EA])
        stage_outproj(p, cx, T, xs[0], [EA.vo[0], EA.vo[1]], xs[1], wo[0])
        stage_ffn(p, cx, T, xs[1], xs[2], **ffn[1])
        stage_ffn(p, cx, T, xs[2], xs[3], **ffn[2])
        qkv(1, xs[3])
        exchange([EQ, EK, EV, ELF])
        attention(1)
        exchange([EA])
        stage_outproj(p, cx, T, xs[3], [EA.vo[0], EA.vo[1]], xs[4], wo[1])
        stage_ffn(p, cx, T, xs[4], y, **ffn[3])
        p.emit()
    return nc


def kernel(x, norm_ffn1, ffn1_w_gate, ffn1_w_up, ffn1_w_down, norm_mix, w_in_ab, g_q_b, g_k_b,
           w_in_cd, b_f, g_q_c, g_k_c, g_q_d, g_k_d, w_out, norm_ffn2, ffn2_w_gate, ffn2_w_up,
           ffn2_w_down):
    A = lambda a: np.asarray(a, dtype=np.float32)
    x = A(x)
    B, S, _ = x.shape
    NC = 8
    T = B * S // NC
    inv = (1.0 / (10000.0 ** (np.arange(0, DH, 2, dtype=np.float32) / DH))).astype(np.float32)
    ang = (np.arange(S, dtype=np.float32)[:, None] * inv[None, :]).astype(np.float32)
    cos, sin = np.cos(ang).astype(np.float32), np.sin(ang).astype(np.float32)
    c0, c1 = attn_consts_np(S, 0), attn_consts_np(S, 1)
    shared = {"ident": c0["ident"], "tri": c0["tri"], "sbmask": c0["sbmask"], "onehot": c0["onehot"],
              "antiid": np.ascontiguousarray(np.eye(128, dtype=np.float32)[::-1]).astype(NPBF), "dilm": c1["dilm"],
              "nmix0": A(norm_mix[0]), "nmix1": A(norm_mix[1]), "w_in0": A(w_in_ab[0]), "w_in1": A(w_in_cd[0]),
              "wo0": A(w_out[0]), "wo1": A(w_out[1]), "gqb": A(g_q_b[0]), "gkb": A(g_k_b[0]),
              "gqc": A(g_q_c[0]), "gkc": A(g_k_c[0]), "gqd": A(g_q_d[0]), "gkd": A(g_k_d[0]), "bf": A(b_f[0])}
    for i, (nrm, wg, wu, wd, l) in enumerate([(norm_ffn1, ffn1_w_gate, ffn1_w_up, ffn1_w_down, 0),
                                               (norm_ffn2, ffn2_w_gate, ffn2_w_up, ffn2_w_down, 0),
                                               (norm_ffn1, ffn1_w_gate, ffn1_w_up, ffn1_w_down, 1),
                                               (norm_ffn2, ffn2_w_gate, ffn2_w_up, ffn2_w_down, 1)]):
        shared.update({f"gain{i}": A(nrm[l]), f"wg{i}": A(wg[l]), f"wu{i}": A(wu[l]), f"wd{i}": A(wd[l])})
    maps = []
    for c in range(NC):
        b, h = c // 2, c % 2
        m = dict(shared)
        rk = np.zeros((128, 2), np.float32)
        rk[:, h] = 1.0
        m.update(x=x[b, h * T:(h + 1) * T], rank=rk, cos=cos[h * T:(h + 1) * T], sin=sin[h * T:(h + 1) * T])
        maps.append(m)
    nc = build_fused(T, S)
    res = _run(nc, maps)
    out = np.empty((B, S, D), dtype=np.float32)
    for c in range(NC):
        b, h = c // 2, c % 2
        out[b, h * T:(h + 1) * T] = res[c]["y"]
    return out
```

```python
import numpy as np
import ml_dtypes
from contextlib import ExitStack
import concourse.bass as bass
import concourse.mybir as mybir
from concourse.bass_utils import run_bass_kernel_spmd

F32 = mybir.dt.float32
BF16 = mybir.dt.bfloat16
AF = mybir.ActivationFunctionType
ALU = mybir.AluOpType
AX = mybir.AxisListType
NPBF = ml_dtypes.bfloat16

D = 1024
DFF = 2816
NH = 16
DH = 64
S_FULL = 8192
B_FULL = 4
EPS = 1e-6
NEG = -30000.0
STW = 1540


class _State:
    __slots__ = ("last_w", "readers", "sem", "cnt")

    def __init__(self):
        self.last_w = None
        self.readers = []
        self.sem = None
        self.cnt = 0


class Buf:
    __slots__ = ("t", "st", "name")

    def __init__(self, t, name, st=None):
        self.t = t
        self.name = name
        self.st = st if st is not None else _State()

    def __getitem__(self, k):
        return self.t[k]

    last_w = property(lambda s: s.st.last_w, lambda s, v: setattr(s.st, "last_w", v))
    readers = property(lambda s: s.st.readers, lambda s, v: setattr(s.st, "readers", v))
    sem = property(lambda s: s.st.sem, lambda s, v: setattr(s.st, "sem", v))
    cnt = property(lambda s: s.st.cnt, lambda s, v: setattr(s.st, "cnt", v))


class Ins:
    __slots__ = ("eng", "fn", "deps", "needs_inc", "count", "epoch", "dma_tok", "idx", "inc")

    def __init__(self, eng, fn):
        self.eng = eng
        self.fn = fn
        self.deps = []
        self.needs_inc = False
        self.count = 0
        self.epoch = 0
        self.dma_tok = None
        self.idx = 0
        self.inc = 16


ENGS = ("pe", "act", "dve", "pool", "sp")
EPOCH = 20000


class Prog:
    ARENA_BYTES = 212000

    def __init__(self, nc, stack):
        self.nc = nc
        self.stack = stack
        self.ins = {e: [] for e in ENGS}
        self.nsb = 0
        self.arena = stack.enter_context(nc.sbuf_tensor("arena", [128, self.ARENA_BYTES // 2], BF16))
        self.off = 0
        self.live = []
        self.sem_pool = []
        self.nsem = 0
        self.cc_sem = None
        self.cc_cnt = 0
        self.banks = [stack.enter_context(nc.psum_tensor(f"bank{i}", [128, 512], F32)) for i in range(8)]
        self.bank_state = [_State() for _ in range(8)]

    def sb(self, shape, dt, name=None):
        self.nsb += 1
        name = name or f"sb{self.nsb}"
        esz = 4 if dt == F32 else 2
        n = 1
        for d in shape[1:]:
            n *= d
        nbytes = (n * esz + 63) // 64 * 64
        assert self.off + nbytes <= self.ARENA_BYTES, (name, self.off, nbytes)
        v = self.arena[0:shape[0], self.off // 2:(self.off + n * esz) // 2]
        if dt == F32:
            v = v.bitcast(F32)
        if len(shape) == 3:
            v = v.rearrange("p (a b) -> p a b", a=shape[1])
        self.off += nbytes
        b = Buf(v, name)
        self.live.append((self.off - nbytes, b))
        return b

    def mark(self):
        return self.off

    def release(self, m):
        keep = []
        for off, b in self.live:
            if off >= m:
                if b.sem is not None:
                    self.sem_pool.append((b.sem, b.cnt))
                    b.st.sem = None
            else:
                keep.append((off, b))
        self.live = keep
        self.off = m

    def ps(self, bank, dt=F32, name=None):
        v = self.banks[bank][:, :]
        if dt == BF16:
            v = v.bitcast(BF16)
        return Buf(v, name or f"bank{bank}", self.bank_state[bank])

    def _add(self, eng, fn, reads, writes, dma=False):
        i = Ins(eng, fn)
        i.idx = len(self.ins[eng])
        deps = []
        for b in reads:
            if b.last_w is not None:
                deps.append(b.last_w)
        for b in writes:
            if b.last_w is not None:
                deps.append(b.last_w)
            for r in b.readers:
                deps.append(r)
        seen = set()
        for d in deps:
            if id(d) in seen or d is i:
                continue
            seen.add(id(d))
            if d.dma_tok is None and d.eng == eng:
                if eng == "pe":
                    continue
                if not any((b.last_w is d) for b in list(reads) + list(writes)):
                    continue
            d.needs_inc = True
            i.deps.append(d)
        if dma:
            key = None
            for b in list(writes) + list(reads):
                key = b
                break
            if key.sem is None:
                if self.sem_pool:
                    key.sem, key.cnt = self.sem_pool.pop()
                else:
                    self.nsem += 1
                    key.sem = self.stack.enter_context(self.nc.semaphore(f"d{self.nsem}_{key.name}"))
                    key.cnt = 0
            key.cnt += 16
            i.dma_tok = (key.sem, key.cnt)
        for b in reads:
            b.readers.append(i)
        for b in writes:
            b.last_w = i
            b.readers = []
        self.ins[eng].append(i)
        return i

    def pe(self, fn, reads, writes):
        return self._add("pe", fn, reads, writes)

    def act(self, fn, reads, writes):
        return self._add("act", fn, reads, writes)

    def dve(self, fn, reads, writes):
        return self._add("dve", fn, reads, writes)

    def pool(self, fn, reads, writes):
        return self._add("pool", fn, reads, writes)

    def dma(self, out_ap, in_ap, reads, writes, eng=None):
        if eng is None:
            eng = "sp"
        return self._add(eng, lambda e: e.dma_start(out=out_ap, in_=in_ap), reads, writes, dma=True)

    def collective(self, in_ap, out_ap, groups):
        if self.cc_sem is None:
            self.cc_sem = self.stack.enter_context(self.nc.semaphore("cc_sem"))
        i = Ins("pool", lambda e: e.collective_compute("AllReduce", ALU.add, replica_groups=groups,
                                                       ins=[in_ap], outs=[out_ap]))
        self.cc_cnt += 1
        i.dma_tok = (self.cc_sem, self.cc_cnt)
        i.inc = 1
        self.ins["pool"].append(i)
        return i

    def barrier(self):
        lasts = []
        for e in ENGS:
            if e == "sp":
                continue
            for i in reversed(self.ins[e]):
                if i.fn is not None and i.dma_tok is None:
                    lasts.append(i)
                    break
        dmas = {}
        for e in ENGS:
            for i in self.ins[e]:
                if i.dma_tok is not None:
                    dmas[id(i.dma_tok[0])] = i
        for e in ENGS:
            i = Ins(e, None)
            for d in lasts:
                if d.eng != e:
                    d.needs_inc = True
                    i.deps.append(d)
            for d in dmas.values():
                i.deps.append(d)
            self.ins[e].append(i)

    def emit(self):
        nc = self.nc
        sems = {}
        for e in ENGS:
            c = 0
            ep = 0
            for i in self.ins[e]:
                if i.dma_tok is None and i.needs_inc:
                    if c >= EPOCH:
                        ep += 1
                        c = 0
                    c += 1
                    i.count = c
                    i.epoch = ep
                    if (e, ep) not in sems:
                        sems[(e, ep)] = self.stack.enter_context(nc.semaphore(f"s_{e}{ep}"))
        block = self.stack.enter_context(nc.Block())
        engobj = {"pe": block.tensor, "act": block.scalar, "dve": block.vector,
                  "pool": block.gpsimd, "sp": block.sync}

        def make(e):
            def body(eng):
                waited = {}
                for i in self.ins[e]:
                    need = {}
                    for d in i.deps:
                        if d.dma_tok is not None:
                            sem, val = d.dma_tok
                        else:
                            sem, val = sems[(d.eng, d.epoch)], d.count
                        k = id(sem)
                        if waited.get(k, 0) >= val:
                            continue
                        if k not in need or need[k][1] < val:
                            need[k] = (sem, val)
                    for k, (sem, val) in need.items():
                        eng.wait_ge(sem, val)
                        waited[k] = val
                    if i.fn is None:
                        continue
                    r = i.fn(eng)
                    if i.dma_tok is not None:
                        r.then_inc(i.dma_tok[0], i.inc)
                    elif i.needs_inc:
                        r.then_inc(sems[(e, i.epoch)], 1)
            return body

        for e in ENGS:
            if self.ins[e]:
                engobj[e](make(e))


def new_prog():
    nc = bass.Bass("TRN2", target_bir_lowering=False)
    stack = ExitStack()
    return nc, stack, Prog(nc, stack)


class Ctx:
    pass


def load_const(p, ap, shape, dt, name):
    b = p.sb(shape, dt, name)
    p.dma(b[:], ap, [], [b])
    return b


def load_weight_bf16(p, w_ap, K, N, name, stage, eng_rr):
    kc = K // 128
    wv = w_ap.rearrange("(c p) n -> p c n", p=128)
    npiece = (N + STW - 1) // STW
    pw = (N + npiece - 1) // npiece
    chunks = []
    for c in range(kc):
        wb = p.sb([128, N], BF16, f"{name}{c}")
        chunks.append(wb)
        for n0 in range(0, N, pw):
            n1 = min(N, n0 + pw)
            st = stage[eng_rr[0] % len(stage)]
            eng_rr[0] += 1
            p.dma(st[:, 0:n1 - n0], wv[:, c, n0:n1], [], [st])
            src = st[:, 0:n1 - n0]
            dst = wb[:, n0:n1]
            if eng_rr[0] % 2 == 0:
                p.dve(lambda e, d=dst, s=src: e.tensor_copy(out=d, in_=s), [st], [wb])
            else:
                p.act(lambda e, d=dst, s=src: e.copy(out=d, in_=s), [st], [wb])
    return chunks


def rms_to_T(p, cx, x_sb, g_bc, xnT, col0, nsub):
    for i in range(nsub):
        xs = x_sb[i]
        ss = cx.small[cx.rr % len(cx.small)]
        cx.rr += 1
        junk = cx.junk
        p.act(lambda e, o=junk[:], a=xs[:], s=ss[:, 0:1]: e.activation(out=o, in_=a, func=AF.Square, accum_out=s),
              [xs], [junk, ss])
        p.act(lambda e, s=ss: e.activation(out=s[:, 1:2], in_=s[:, 0:1], func=AF.Sqrt, scale=1.0 / D, bias=cx.eps[:, 0:1]),
              [ss, cx.eps], [ss])
        p.dve(lambda e, s=ss: e.reciprocal(out=s[:, 2:3], in_=s[:, 1:2]), [ss], [ss])
        xn = cx.xn[cx.rr % len(cx.xn)]
        p.dve(lambda e, o=xn[:], a=xs[:], s=ss[:, 2:3], g=g_bc[:]: e.scalar_tensor_tensor(
            out=o, in0=a, scalar=s, in1=g, op0=ALU.mult, op1=ALU.mult), [xs, ss, g_bc], [xn])
        transpose_to(p, cx, xn, xnT, 0, col0 + 128 * i, D // 128)


def transpose_to(p, cx, src, dstT, c0, col, nchunk, flat=None):
    tp = cx.tps[cx.rrt % len(cx.tps)]
    cx.rrt += 1
    sv = flat if flat is not None else src[:]
    for c in range(nchunk):
        p.pe(lambda e, o=tp[:, c * 128:(c + 1) * 128], a=sv[:, c * 128:(c + 1) * 128], idn=cx.ident[:]:
             e.transpose(out=o, in_=a, identity=idn), [src, cx.ident], [tp])
    o = dstT[:, c0:c0 + nchunk, col:col + 128]
    i_ = tp[:, 0:nchunk * 128].rearrange("p (c t) -> p c t", c=nchunk)
    if cx.rrt % 2 == 0:
        p.act(lambda e, o=o, i_=i_: e.copy(out=o, in_=i_), [tp], [dstT])
    else:
        p.dve(lambda e, o=o, i_=i_: e.tensor_copy(out=o, in_=i_), [tp], [dstT])


def emit_out(p, cx, src_buf, src_ap, dsts, tmps, npart=128):
    if len(dsts) == 1:
        p.dma(dsts[0], src_ap, [src_buf], [])
        return
    for s_ in range(2):
        t = tmps[s_]
        p.act(lambda e, o=t[:], a=src_ap, m=cx.rk[0:npart, s_:s_ + 1]: e.mul(out=o, in_=a, mul=m), [src_buf, cx.rk], [t])
        p.dma(dsts[s_], t[:], [t], [])


def load_sel(p, cx, dst_buf, dst_ap, srcs, tmp, npart=128):
    if len(srcs) == 1:
        p.dma(dst_ap, srcs[0], [], [dst_buf])
        return
    tb, ta = tmp
    p.dma(ta, srcs[0], [], [tb])
    p.act(lambda e, o=dst_ap, a=ta, m=cx.rk[0:npart, 0:1]: e.mul(out=o, in_=a, mul=m), [tb, cx.rk], [dst_buf])
    p.dma(ta, srcs[1], [], [tb])
    p.dve(lambda e, o=dst_ap, b=ta, m=cx.rk[0:npart, 1:2]: e.scalar_tensor_tensor(
        out=o, in0=b, scalar=m, in1=o, op0=ALU.mult, op1=ALU.add), [tb, cx.rk, dst_buf], [dst_buf])


def make_ctx(p, ident_ap, rank_ap=None):
    cx = Ctx()
    cx.rr = 0
    cx.rrt = 0
    cx.ident = load_const(p, ident_ap, [128, 128], BF16, "ident")
    cx.eps = p.sb([128, 1], F32, "eps")
    p.dve(lambda e: e.memset(cx.eps[:], EPS), [], [cx.eps])
    cx.small = [p.sb([128, 4], F32, f"small{i}") for i in range(4)]
    cx.small8 = [p.sb([128, 24], F32, f"small8_{i}") for i in range(4)]
    cx.junk = p.sb([128, D], BF16, "junk")
    cx.xn = [p.sb([128, D], BF16, f"xn{i}") for i in range(1)]
    cx.tps = [p.ps(6 + i, BF16, f"tps{i}") for i in range(2)]
    cx.mm = [p.ps(i, F32, f"mm{i}") for i in range(6)]
    cx.stage = [p.sb([128, STW], F32, f"stage{i}") for i in range(2)]
    cx.rk = None
    if rank_ap is not None:
        cx.rk = load_const(p, rank_ap, [128, 2], F32, "rank")
    return cx


TG = 512
NSUB = TG // 128
KC = D // 128


def stage_ffn(p, cx, T, x_in, x_out, gain, wg, wu, wd):
    m = p.mark()
    rr = [0]
    mm_ps = cx.mm
    g_bc = p.sb([128, D], F32, "ffn_g")
    p.dma(g_bc[:], gain.partition_broadcast(128), [], [g_bc])
    Wg = load_weight_bf16(p, wg, D, DFF, "Wg", cx.stage, rr)
    Wu = load_weight_bf16(p, wu, D, DFF, "Wu", cx.stage, rr)
    Wd = load_weight_bf16(p, wd, DFF, D, "Wd", cx.stage, rr)
    nf = DFF // 128
    xpool = [p.sb([128, D], F32, f"ffn_x{i}") for i in range(5)]
    xnT = p.sb([128, KC, TG], BF16, "ffn_xnT")
    hT = p.sb([128, nf, TG], BF16, "ffn_hT")
    sg = [p.sb([128, TG], F32, f"ffn_sg{i}") for i in range(1)]
    xv = x_in.rearrange("(n p) d -> n p d", p=128)
    ov = x_out.rearrange("(n p) d -> n p d", p=128)
    for g in range(T // TG):
        xs = [xpool[(g * NSUB + i) % len(xpool)] for i in range(NSUB)]
        for i in range(NSUB):
            p.dma(xs[i][:], xv[g * NSUB + i], [], [xs[i]])
        rms_to_T(p, cx, xs, g_bc, xnT, 0, NSUB)
        for f in range(nf):
            pg = mm_ps[(2 * f) % len(mm_ps)]
            pu = mm_ps[(2 * f + 1) % len(mm_ps)]
            for k in range(KC):
                p.pe(lambda e, o=pg[:], w=Wg[k][:, f * 128:(f + 1) * 128], a=xnT[:, k, :], k=k:
                     e.matmul(o, lhsT=w, rhs=a, start=(k == 0), stop=(k == KC - 1)), [Wg[k], xnT], [pg])
            for k in range(KC):
                p.pe(lambda e, o=pu[:], w=Wu[k][:, f * 128:(f + 1) * 128], a=xnT[:, k, :], k=k:
                     e.matmul(o, lhsT=w, rhs=a, start=(k == 0), stop=(k == KC - 1)), [Wu[k], xnT], [pu])
            s = sg[f % len(sg)]
            p.act(lambda e, o=s[:], a=pg[:]: e.activation(out=o, in_=a, func=AF.Silu), [pg], [s])
            p.dve(lambda e, o=hT[:, f, :], a=s[:], b=pu[:]: e.tensor_tensor(out=o, in0=a, in1=b, op=ALU.mult),
                  [s, pu], [hT])
        for i in range(NSUB):
            for half in range(2):
                py = mm_ps[(2 * i + half) % len(mm_ps)]
                for f in range(nf):
                    p.pe(lambda e, o=py[:], a=hT[:, f, i * 128:(i + 1) * 128], w=Wd[f][:, half * 512:(half + 1) * 512], f=f:
                         e.matmul(o, lhsT=a, rhs=w, start=(f == 0), stop=(f == nf - 1)), [hT, Wd[f]], [py])
                p.dve(lambda e, o=xs[i][:, half * 512:(half + 1) * 512], a=py[:]: e.scalar_tensor_tensor(
                    out=o, in0=a, scalar=0.5, in1=o, op0=ALU.mult, op1=ALU.add), [py, xs[i]], [xs[i]])
            p.dma(ov[g * NSUB + i], xs[i][:], [xs[i]], [])
    p.barrier()
    p.release(m)


def stage_outproj(p, cx, T, x_in, attn_in, x_out, wo):
    attn_list = list(attn_in) if isinstance(attn_in, (list, tuple)) else [attn_in]
    m = p.mark()
    rr = [0]
    mm_ps = cx.mm
    Wo = load_weight_bf16(p, wo, D, D, "Wo", cx.stage, rr)
    xpool = [p.sb([128, D], F32, f"op_x{i}") for i in range(6)]
    apool = [p.sb([128, D], BF16, f"op_a{i}") for i in range(6)]
    aT = p.sb([128, KC, TG], BF16, "op_aT")
    xv = x_in.rearrange("(n p) d -> n p d", p=128)
    avs = [a.rearrange("(n p) d -> n p d", p=128) for a in attn_list]
    acand = [p.sb([128, D], BF16, f"op_c{i}") for i in range(2)] if len(avs) == 2 else None
    ov = x_out.rearrange("(n p) d -> n p d", p=128)
    for g in range(T // TG):
        xs = [xpool[(g * NSUB + i) % len(xpool)] for i in range(NSUB)]
        as_ = [apool[(g * NSUB + i) % len(apool)] for i in range(NSUB)]
        for i in range(NSUB):
            p.dma(xs[i][:], xv[g * NSUB + i], [], [xs[i]])
            load_sel(p, cx, as_[i], as_[i][:], [a[g * NSUB + i] for a in avs],
                     (acand[i % 2], acand[i % 2][:]) if acand else None)
        for i in range(NSUB):
            transpose_to(p, cx, as_[i], aT, 0, 128 * i, KC)
        for i in range(NSUB):
            for half in range(2):
                py = mm_ps[(2 * i + half) % len(mm_ps)]
                for k in range(KC):
                    p.pe(lambda e, o=py[:], a=aT[:, k, i * 128:(i + 1) * 128], w=Wo[k][:, half * 512:(half + 1) * 512], k=k:
                         e.matmul(o, lhsT=a, rhs=w, start=(k == 0), stop=(k == KC - 1)), [aT, Wo[k]], [py])
                p.dve(lambda e, o=xs[i][:, half * 512:(half + 1) * 512], a=py[:]: e.tensor_tensor(
                    out=o, in0=a, in1=o, op=ALU.add), [py, xs[i]], [xs[i]])
            p.dma(ov[g * NSUB + i], xs[i][:], [xs[i]], [])
    p.barrier()
    p.release(m)


def stage_qkv(p, cx, T, layer, x_in, gain, w_in, gq, gk, cos, sin, qT_out, kT_out, v_out, bf=None, logf_out=None,
              ktok_out=None, pm=False):
    m = p.mark()
    rr = [0]
    mm_ps = cx.mm
    NW = 3 * D + (8 if layer == 1 else 0)
    g_bc = p.sb([128, D], F32, "qkv_g")
    p.dma(g_bc[:], gain.partition_broadcast(128), [], [g_bc])
    Win = load_weight_bf16(p, w_in, D, NW, "Win", cx.stage, rr)
    gains = {}
    for nm, ap, sc in [("q", gq, 0.125), ("k", gk, 1.0)]:
        for j, a in enumerate(ap):
            t = p.sb([128, DH], F32, f"gain_{nm}{j}")
            p.dma(t[:], a.partition_broadcast(128), [], [t])
            if sc != 1.0:
                p.dve(lambda e, t=t, sc=sc: e.tensor_scalar(out=t[:], in0=t[:], scalar1=sc, scalar2=None, op0=ALU.mult),
                      [t], [t])
            gains[(nm, j)] = t
    if layer == 1:
        bf_bc = p.sb([128, 8], F32, "bf_bc")
        p.dma(bf_bc[:], bf.partition_broadcast(128), [], [bf_bc])
        lf = [p.sb([128, 32], F32, f"lf{i}") for i in range(2)]
    cs_pool = [(p.sb([128, 32], F32, f"cos{i}"), p.sb([128, 32], F32, f"sin{i}")) for i in range(2)]
    xpool = [p.sb([128, D], F32, f"qkv_x{i}") for i in range(6)]
    xnT = p.sb([128, KC, TG], BF16, "qkv_xnT")
    qT = p.sb([128, 8, TG], BF16, "qkv_qT")
    kT = p.sb([128, 8, TG], BF16, "qkv_kT")
    vb = [p.sb([128, D], BF16, f"qkv_v{i}") for i in range(2)]
    sq_l = [p.sb([128, 8, DH], F32, f"qkv_sq{i}") for i in range(2)]
    qn_l = [p.sb([128, 8, DH], F32, f"qkv_qn{i}") for i in range(2)]
    tmp_l = [[p.sb([128, 8, 32], F32, f"qkv_t{j}_{i}") for i in range(4)] for j in range(2)]
    qb = [p.sb([128, 8, DH], BF16, f"qkv_qb{i}") for i in range(4)]
    L = lambda a: list(a) if isinstance(a, (list, tuple)) else [a]
    xv = x_in.rearrange("(n p) d -> n p d", p=128)
    if pm:
        vvs = [[a[:, :, n_, :].rearrange("n p d -> p n d") for n_ in range(T // 128)] for a in L(v_out)]
    else:
        vvs = [a.rearrange("(n p) d -> n p d", p=128) for a in L(v_out)]
    cv = cos.rearrange("(n p) d -> n p d", p=128)
    sv = sin.rearrange("(n p) d -> n p d", p=128)
    qTvs = [a.rearrange("(hp two) d t -> (two d) hp t", two=2) for a in L(qT_out)]
    kTvs = [a.rearrange("(hp two) d t -> (two d) hp t", two=2) for a in L(kT_out)]
    two = len(qTvs) == 2
    tq = [p.sb([128, 8, TG], BF16, f"qkv_tq{i}") for i in range(2)] if two else None
    tv = [p.sb([128, NH, DH], BF16, f"qkv_tv{i}") for i in range(2)] if two else None
    tk = [p.sb([128, 8, DH], BF16, f"qkv_tk{i}") for i in range(2)] if two else None
    tl = [p.sb([128, 8], F32, f"qkv_tl{i}") for i in range(2)] if two else None
    if ktok_out is None:
        ktvs = None
    elif pm:
        ktvs = [[a[:, :, n_, :].rearrange("n p d -> p n d") for n_ in range(T // 128)] for a in L(ktok_out)]
    else:
        ktvs = [a.rearrange("(n p) d -> n p d", p=128) for a in L(ktok_out)]
    if layer == 1:
        lvs = L(logf_out)
        lfT = [p.sb([8, TG], F32, f"qkv_lfT{i}") for i in range(2)]
        tlT = [p.sb([8, TG], F32, f"qkv_tlT{i}") for i in range(2)] if two else None
        ident32 = p.sb([128, 128], F32, "ident32")
        p.dve(lambda e: e.tensor_copy(out=ident32[:], in_=cx.ident[:]), [cx.ident], [ident32])
    nb = 0
    for g in range(T // TG):
        xs = [xpool[(g * NSUB + i) % len(xpool)] for i in range(NSUB)]
        for i in range(NSUB):
            p.dma(xs[i][:], xv[g * NSUB + i], [], [xs[i]])
        rms_to_T(p, cx, xs, g_bc, xnT, 0, NSUB)
        for i in range(NSUB):
            n = g * NSUB + i
            cosb, sinb = cs_pool[n % 2]
            p.dma(cosb[:], cv[n], [], [cosb])
            p.dma(sinb[:], sv[n], [], [sinb])
            vbuf = vb[n % 2]
            for cg in range(6):
                ps = mm_ps[(nb) % len(mm_ps)]
                nb += 1
                for k in range(KC):
                    p.pe(lambda e, o=ps[:], a=xnT[:, k, i * 128:(i + 1) * 128], w=Win[k][:, cg * 512:(cg + 1) * 512], k=k:
                         e.matmul(o, lhsT=a, rhs=w, start=(k == 0), stop=(k == KC - 1)), [xnT, Win[k]], [ps])
                if cg >= 4:
                    p.act(lambda e, o=vbuf[:, (cg - 4) * 512:(cg - 3) * 512], a=ps[:]: e.copy(out=o, in_=a), [ps], [vbuf])
                    continue
                isq = cg < 2
                hg = cg % 2
                normed = (layer == 1) or (hg == 1)
                roped = (hg == 1)
                qbuf = qb[nb % 4]
                sq, qn, tmp = sq_l[nb % 2], qn_l[nb % 2], tmp_l[nb % 2]
                ps3 = ps[:].rearrange("p (h d) -> p h d", h=8)
                if not normed:
                    p.act(lambda e, o=qbuf[:], a=ps3, sc=(0.125 if isq else 1.0): e.mul(out=o, in_=a, mul=sc), [ps], [qbuf])
                else:
                    gj = 0 if layer == 0 else hg
                    gt = gains[("q" if isq else "k", gj)]
                    ss = cx.small8[cx.rr % len(cx.small8)]
                    cx.rr += 1
                    p.act(lambda e, o=sq[:], a=ps3: e.activation(out=o, in_=a, func=AF.Square), [ps], [sq])
                    p.dve(lambda e, o=ss[:, 0:8], a=sq[:]: e.tensor_reduce(out=o, in_=a, axis=AX.X, op=ALU.add), [sq], [ss])
                    p.act(lambda e, s=ss: e.activation(out=s[:, 8:16], in_=s[:, 0:8], func=AF.Sqrt, scale=1.0 / DH,
                                                       bias=cx.eps[:, 0:1]), [ss, cx.eps], [ss])
                    p.dve(lambda e, s=ss: e.reciprocal(out=s[:, 16:24], in_=s[:, 8:16]), [ss], [ss])
                    p.dve(lambda e, o=qn[:], a=ps3, s=ss: e.tensor_tensor(
                        out=o, in0=a, in1=s[:, 16:24].unsqueeze(2).to_broadcast([128, 8, DH]), op=ALU.mult), [ps, ss], [qn])
                    gbc = gt[:].unsqueeze(1).to_broadcast([128, 8, DH])
                    if not roped:
                        p.dve(lambda e, o=qbuf[:], a=qn[:], g_=gbc: e.tensor_tensor(out=o, in0=a, in1=g_, op=ALU.mult),
                              [qn, gt], [qbuf])
                    else:
                        p.dve(lambda e, o=qn[:], a=qn[:], g_=gbc: e.tensor_tensor(out=o, in0=a, in1=g_, op=ALU.mult),
                              [qn, gt], [qn])
                        cb = cosb[:].unsqueeze(1).to_broadcast([128, 8, 32])
                        sb_ = sinb[:].unsqueeze(1).to_broadcast([128, 8, 32])
                        x1 = qn[:, :, 0:32]
                        x2 = qn[:, :, 32:64]
                        t0, t1, t2, t3 = tmp
                        p.pool(lambda e, o=t0[:], a=x1, b=cb: e.tensor_tensor(out=o, in0=a, in1=b, op=ALU.mult), [qn, cosb], [t0])
                        p.pool(lambda e, o=t1[:], a=x2, b=sb_: e.tensor_tensor(out=o, in0=a, in1=b, op=ALU.mult), [qn, sinb], [t1])
                        p.dve(lambda e, o=t2[:], a=x2, b=cb: e.tensor_tensor(out=o, in0=a, in1=b, op=ALU.mult), [qn, cosb], [t2])
                        p.dve(lambda e, o=t3[:], a=x1, b=sb_: e.tensor_tensor(out=o, in0=a, in1=b, op=ALU.mult), [qn, sinb], [t3])
                        p.dve(lambda e, o=qbuf[:, :, 0:32], a=t0[:], b=t1[:]: e.tensor_tensor(out=o, in0=a, in1=b, op=ALU.subtract),
                              [t0, t1], [qbuf])
                        p.dve(lambda e, o=qbuf[:, :, 32:64], a=t2[:], b=t3[:]: e.tensor_tensor(out=o, in0=a, in1=b, op=ALU.add),
                              [t2, t3], [qbuf])
                transpose_to(p, cx, qbuf, qT if isq else kT, hg * 4, 128 * i, 4,
                             flat=qbuf.t.rearrange("p h d -> p (h d)"))
                if ktvs is not None and (not isq) and hg == 0:
                    emit_out(p, cx, qbuf, qbuf[:] if pm else qbuf.t.rearrange("p h d -> p (h d)"), [a[n] for a in ktvs], tk)
            if layer == 1:
                ps = mm_ps[(nb) % len(mm_ps)]
                nb += 1
                for k in range(KC):
                    p.pe(lambda e, o=ps[:, 0:8], a=xnT[:, k, i * 128:(i + 1) * 128], w=Win[k][:, 3 * D:3 * D + 8], k=k:
                         e.matmul(o, lhsT=a, rhs=w, start=(k == 0), stop=(k == KC - 1)), [xnT, Win[k]], [ps])
                l = lf[n % 2]
                p.dve(lambda e, o=l[:, 0:8], a=ps[:, 0:8], b=bf_bc[:]: e.tensor_tensor(out=o, in0=a, in1=b, op=ALU.add),
                      [ps, bf_bc], [l])
                p.act(lambda e, l=l: e.activation(out=l[:, 8:16], in_=l[:, 0:8], func=AF.Exp, scale=-1.0), [l], [l])
                p.act(lambda e, l=l: e.activation(out=l[:, 16:24], in_=l[:, 8:16], func=AF.Ln, bias=1.0), [l], [l])
                p.dve(lambda e, l=l: e.tensor_scalar(out=l[:, 24:32], in0=l[:, 16:24], scalar1=-1.0, scalar2=None, op0=ALU.mult),
                      [l], [l])
                pst = mm_ps[(nb) % len(mm_ps)]
                nb += 1
                p.pe(lambda e, o=pst[0:8, 0:128], a=l[:, 24:32]: e.transpose(out=o, in_=a, identity=ident32[:]),
                     [l, ident32], [pst])
                lt = lfT[g % 2]
                p.act(lambda e, o=lt[:, i * 128:(i + 1) * 128], a=pst[0:8, 0:128]: e.copy(out=o, in_=a), [pst], [lt])
                if i == NSUB - 1:
                    emit_out(p, cx, lt, lt[:], [a[:, g * TG:(g + 1) * TG] for a in lvs], tlT, npart=8)
            emit_out(p, cx, vbuf, vbuf[:].rearrange("p (h d) -> p h d", h=NH) if pm else vbuf[:], [a[n] for a in vvs], tv)
        emit_out(p, cx, qT, qT[:], [a[:, :, g * TG:(g + 1) * TG] for a in qTvs], tq)
        emit_out(p, cx, kT, kT[:], [a[:, :, g * TG:(g + 1) * TG] for a in kTvs], tq)
    p.barrier()
    p.release(m)


def build_test_tok(T, layer, which):
    nc, stack, p = new_prog()
    with stack:
        dt = lambda n, sh, d=F32, k="ExternalInput": nc.dram_tensor(n, sh, d, kind=k).ap()
        ident = dt("ident", [128, 128], BF16)
        cx = make_ctx(p, ident)
        x = dt("x", [T, D])
        if which == "ffn":
            y = dt("y", [T, D], F32, "ExternalOutput")
            stage_ffn(p, cx, T, x, y, dt("gain", [D]), dt("wg", [D, DFF]), dt("wu", [D, DFF]), dt("wd", [DFF, D]))
        elif which == "outproj":
            y = dt("y", [T, D], F32, "ExternalOutput")
            stage_outproj(p, cx, T, x, dt("attn", [T, D], BF16), y, dt("wo", [D, D]))
        elif which == "qkv":
            NW = 3 * D + (8 if layer == 1 else 0)
            ng = 1 if layer == 0 else 2
            gq = [dt(f"gq{j}", [DH]) for j in range(ng)]
            gk = [dt(f"gk{j}", [DH]) for j in range(ng)]
            qT = dt("qT", [NH, DH, T], BF16, "ExternalOutput")
            kT = dt("kT", [NH, DH, T], BF16, "ExternalOutput")
            v = dt("v", [T, D], BF16, "ExternalOutput")
            kw = {}
            if layer == 1:
                kw = dict(bf=dt("bf", [8]), logf_out=dt("logf", [8, T], F32, "ExternalOutput"))
            stage_qkv(p, cx, T, layer, x, dt("gain", [D]), dt("w_in", [D, NW]), gq, gk,
                      dt("cos", [T, 32]), dt("sin", [T, 32]), qT, kT, v, **kw)
        p.emit()
    return nc


class AttnWork:
    pass


def attn_setup(p, cx, S, layer, consts, fused=False):
    W = AttnWork()
    W.S = S
    W.fused = fused
    if fused:
        W.cq = p.sb([64, S], BF16, "a_cq")
        W.cv = p.sb([128, S // 128, DH], BF16, "a_cv")
        W.otmp = [p.sb([128, 4, DH], BF16, f"a_otmp{i}") for i in range(2)]
        if layer == 0:
            W.J = load_const(p, consts["antiid"], [128, 128], BF16, "antiid")
            W.ktok = p.sb([128, S // 128, DH], BF16, "a_ktok")
            W.vnat = p.sb([128, S // 128, DH], BF16, "a_vnat")
    W.qT = p.sb([64, S], BF16, "a_qT")
    W.kT = p.sb([64, S], BF16, "a_kT")
    W.v = p.sb([128, S // 128, 65], BF16, "a_v")
    p.dve(lambda e: e.memset(W.v[:, :, 64:65], 1.0), [], [W.v])
    W.tri = load_const(p, consts["tri"], [128, 128], BF16, "tri")
    W.osb = [p.sb([128, 4, DH], BF16, f"a_o{i}") for i in range(2)]
    W.rinv = [p.sb([128, 1], F32, f"a_rinv{i}") for i in range(4)]
    W.pt = [p.sb([128, 512], BF16, f"a_pt{i}") for i in range(4)]
    if layer == 0:
        W.sbmask = load_const(p, consts["sbmask"], [128, 128], BF16, "sbmask")
        W.onehot = load_const(p, consts["onehot"], [32, S], BF16, "onehot")
        W.QA = p.sb([32, S], BF16, "a_QA")
        W.ones = p.sb([128, 512], F32, "a_ones")
        p.dve(lambda e: e.memset(W.ones[:], 1.0), [], [W.ones])
        W.e = [p.sb([128, 512], F32, f"a_e{i}") for i in range(4)]
        W.sp = [p.sb([128, 512], F32, f"a_sp{i}") for i in range(4)]
        W.cs = [p.sb([128, 512], F32, f"a_cs{i}") for i in range(4)]
        W.ecs = [p.sb([128, 512], F32, f"a_ecs{i}") for i in range(4)]
        W.a = [p.sb([128, 512], BF16, f"a_a{i}") for i in range(4)]
        W.at = [p.sb([128, 512], BF16, f"a_at{i}") for i in range(2)]
        W.km = p.sb([64, 32], F32, "a_km")
        W.kmh = p.sb([64, 32], BF16, "a_kmh")
        W.kml = p.sb([64, 32], BF16, "a_kml")
        W.gm = p.sb([128, 32], F32, "a_gm")
        W.top8 = [p.sb([128, 8], F32, f"a_top{i}") for i in range(2)]
        W.sel = [p.sb([128, 32], F32, f"a_sel{i}") for i in range(2)]
        W.bsel = [p.sb([128, 32], BF16, f"a_bsel{i}") for i in range(2)]
    else:
        W.dilm = load_const(p, consts["dilm"].rearrange("k p t -> p k t"), [128, 20, 512], BF16, "dilm")
        W.QA = p.sb([6, S], BF16, "a_QA")
        W.KA = p.sb([6, S], BF16, "a_KA")
    return W


def _L(a):
    return list(a) if isinstance(a, (list, tuple)) else [a]


def load_job(p, cx, W, q_ap, k_ap, v_ap):
    tq = (W.cq, W.cq[:]) if W.fused else None
    tv = (W.cv, W.cv[:]) if W.fused else None
    load_sel(p, cx, W.qT, W.qT[:], _L(q_ap), tq, npart=64)
    if k_ap is not None:
        load_sel(p, cx, W.kT, W.kT[:], _L(k_ap), tq, npart=64)
    if v_ap is not None:
        _load_tok(p, cx, W, W.v, W.v[:, :, 0:DH], v_ap)


def _load_tok(p, cx, W, dst_buf, dst_ap, src):
    srcs = _L(src)
    if len(srcs[0].shape) == 2:
        load_sel(p, cx, dst_buf, dst_ap, [a.rearrange("(k p) d -> p k d", p=128) for a in srcs],
                 (W.cv, W.cv[:]) if W.fused else None)
    else:
        r = lambda ap: ap.rearrange("p (h k) d -> p h k d", h=2)
        load_sel(p, cx, dst_buf, r(dst_ap), srcs, (W.cv, r(W.cv[:])))


def store_out(p, cx, W, osb, dsts):
    emit_out(p, cx, osb, osb[:], dsts, W.otmp if W.fused else None)


def sb_reverse(p, cx, W, ktok_ap, v_ap):
    S = W.S
    nb = S // 128
    _load_tok(p, cx, W, W.ktok, W.ktok[:], ktok_ap)
    _load_tok(p, cx, W, W.vnat, W.vnat[:], v_ap)
    for r0 in range(0, nb, 4):
        ps = p.ps(r0 // 4 % 2, F32)
        for rb in range(r0, r0 + 4):
            k = nb - 1 - rb
            p.pe(lambda e, o=ps[0:DH, (rb - r0) * 128:(rb - r0 + 1) * 128], a=W.ktok[:, k, :]:
                 e.matmul(o, lhsT=a, rhs=W.J[:], start=True, stop=True), [W.ktok, W.J], [ps])
        p.act(lambda e, o=W.kT[:, r0 * 128:(r0 + 4) * 128], a=ps[0:DH, :]: e.copy(out=o, in_=a), [ps], [W.kT])
    for r0 in range(0, nb, 8):
        ps = p.ps(2 + r0 // 8 % 2, F32)
        for rb in range(r0, r0 + 8):
            k = nb - 1 - rb
            p.pe(lambda e, o=ps[:, (rb - r0) * DH:(rb - r0 + 1) * DH], a=W.vnat[:, k, :]:
                 e.matmul(o, lhsT=W.J[:], rhs=a, start=True, stop=True), [W.vnat, W.J], [ps])
        p.dve(lambda e, o=W.v[:, r0:r0 + 8, 0:DH], a=ps[:].rearrange("p (k d) -> p k d", d=DH): e.tensor_copy(out=o, in_=a),
              [ps], [W.v])


def sb_job(p, cx, W, q_ap, k_ap, v_ap, o_ap, ktok_ap=None):
    S = W.S
    if ktok_ap is None:
        load_job(p, cx, W, q_ap, k_ap, v_ap)
    else:
        load_job(p, cx, W, q_ap, None, None)
        sb_reverse(p, cx, W, ktok_ap, v_ap)
    nb = S // 128
    ovs = [a.rearrange("(g q p) d -> g p q d", p=128, q=4) for a in _L(o_ap)]
    NS = len(W.e)
    LAG = NS - 1
    units = []
    for i in range(nb):
        rb0 = nb - 1 - i
        nun = (128 * (i + 1) + 511) // 512
        for u in range(nun):
            c0 = rb0 * 128 + 512 * u
            units.append(dict(i=i, u=u, nun=nun, c0=c0, N=min(512, S - c0), idx=len(units)))

    def stage_a(t):
        i, u, c0, N, k = t["i"], t["u"], t["c0"], t["N"], t["idx"]
        z = p.ps(k % 2, F32)
        p.pe(lambda e, o=z[:, 0:N], a=W.qT[:, i * 128:(i + 1) * 128], b=W.kT[:, c0:c0 + N], u=u:
             e.matmul(o, lhsT=a, rhs=b, start=True, stop=(u != 0)), [W.qT, W.kT], [z])
        if u == 0:
            p.pe(lambda e, o=z[:, 0:128]: e.matmul(o, lhsT=cx.ident[:], rhs=W.sbmask[:], start=False, stop=True),
                 [cx.ident, W.sbmask], [z])
        e_sb, sp, cs, ecs, a = (W.e[k % NS], W.sp[k % NS], W.cs[k % NS], W.ecs[k % NS], W.a[k % NS])
        p.act(lambda e, o=e_sb[:, 0:N], i_=z[:, 0:N]: e.activation(out=o, in_=i_, func=AF.Exp), [z], [e_sb])
        p.act(lambda e, o=sp[:, 0:N], i_=e_sb[:, 0:N]: e.activation(out=o, in_=i_, func=AF.Ln, bias=1.0), [e_sb], [sp])
        if u == 0:
            init, rd = 0.0, [W.ones, sp]
        else:
            cprev = W.cs[(k - 1) % NS]
            pN = units[k - 1]["N"]
            init, rd = cprev[:, pN - 1:pN], [W.ones, sp, cprev]
        p.dve(lambda e, o=cs[:, 0:N], d0=W.ones[:, 0:N], d1=sp[:, 0:N], init=init: e.tensor_tensor_scan(
            out=o, data0=d0, data1=d1, initial=init, op0=ALU.mult, op1=ALU.add), rd, [cs])

    def stage_a2(t):
        N, k = t["N"], t["idx"]
        e_sb, cs, ecs, a = (W.e[k % NS], W.cs[k % NS], W.ecs[k % NS], W.a[k % NS])
        p.act(lambda e, o=ecs[:, 0:N], i_=cs[:, 0:N]: e.activation(out=o, in_=i_, func=AF.Exp, scale=-1.0), [cs], [ecs])
        p.dve(lambda e, o=a[:, 0:N], x=e_sb[:, 0:N], y=ecs[:, 0:N]: e.tensor_tensor(out=o, in0=x, in1=y, op=ALU.mult),
              [e_sb, ecs], [a])

    def stage_b(t):
        i, u, nun, c0, N, k = t["i"], t["u"], t["nun"], t["c0"], t["N"], t["idx"]
        nk = N // 128
        a = W.a[k % NS]
        at = W.at[k % 2]
        o_ps = p.ps(4 + i % 2, F32)
        atp = p.ps(2 + k % 2, BF16)
        for kb in range(nk):
            p.pe(lambda e, o=atp[:, kb * 128:(kb + 1) * 128], x=a[:, kb * 128:(kb + 1) * 128]:
                 e.transpose(out=o, in_=x, identity=cx.ident[:]), [a, cx.ident], [atp])
        p.act(lambda e, o=at[:, 0:N], x=atp[:, 0:N]: e.copy(out=o, in_=x), [atp], [at])
        for kb in range(nk):
            p.pe(lambda e, o=o_ps[:, 0:DH], x=at[:, kb * 128:(kb + 1) * 128], vv=W.v[:, c0 // 128 + kb, 0:DH],
                 st=(u == 0 and kb == 0), sp_=(u == nun - 1 and kb == nk - 1):
                 e.matmul(o, lhsT=x, rhs=vv, start=st, stop=sp_), [at, W.v], [o_ps])
        if u == nun - 1:
            osb = W.osb[(i // 4) % 2]
            p.act(lambda e, o=osb[:, i % 4, :], x=o_ps[:, 0:DH]: e.copy(out=o, in_=x), [o_ps], [osb])
            if i % 4 == 3:
                store_out(p, cx, W, osb, [a_[i // 4] for a_ in ovs])

    nu = len(units)
    for k in range(nu + LAG):
        if k < nu:
            stage_a(units[k])
        if 0 <= k - 1 < nu:
            stage_a2(units[k - 1])
        if 0 <= k - LAG < nu:
            stage_b(units[k - LAG])


def moba_pre(p, cx, W):
    S = W.S
    nkb = S // 256
    p.dve(lambda e: e.tensor_reduce(out=W.km[:, 0:nkb], in_=W.kT[:].rearrange("p (n s) -> p n s", s=256),
                                    axis=AX.X, op=ALU.add), [W.kT], [W.km])
    p.dve(lambda e: e.tensor_scalar(out=W.km[:, 0:nkb], in0=W.km[:, 0:nkb], scalar1=1.0 / 256, scalar2=None, op0=ALU.mult),
          [W.km], [W.km])
    p.dve(lambda e: e.tensor_copy(out=W.kmh[:, 0:nkb], in_=W.km[:, 0:nkb]), [W.km], [W.kmh])
    p.dve(lambda e: e.tensor_tensor(out=W.kml[:, 0:nkb], in0=W.km[:, 0:nkb], in1=W.kmh[:, 0:nkb], op=ALU.subtract),
          [W.km, W.kmh], [W.kml])
    p.dve(lambda e: e.memset(W.gm[:], -1e30), [], [W.gm])
    for i in range(S // 128):
        own = i // 2
        bsel = W.bsel[i % 2]
        if own == 0:
            p.dve(lambda e, b=bsel: e.memset(b[:], 0.0), [], [bsel])
        else:
            gp = p.ps(4 + i % 2, F32)
            p.pe(lambda e, o=gp[:, 0:nkb], a=W.qT[:, i * 128:(i + 1) * 128]: e.matmul(o, lhsT=a, rhs=W.kmh[:, 0:nkb], start=True, stop=False),
                 [W.qT, W.kmh], [gp])
            p.pe(lambda e, o=gp[:, 0:nkb], a=W.qT[:, i * 128:(i + 1) * 128]: e.matmul(o, lhsT=a, rhs=W.kml[:, 0:nkb], start=False, stop=True),
                 [W.qT, W.kml], [gp])
            p.dve(lambda e, o=W.gm[:, 0:own], a=gp[:, 0:own]: e.tensor_copy(out=o, in_=a), [gp], [W.gm])
            top = W.top8[i % 2]
            sel = W.sel[i % 2]
            p.dve(lambda e, t=top: e.max(out=t[:], in_=W.gm[:]), [W.gm], [top])
            p.dve(lambda e, t=top: e.tensor_scalar(out=t[:, 3:4], in0=t[:, 2:3], scalar1=-1e29, scalar2=None, op0=ALU.max),
                  [top], [top])
            p.dve(lambda e, s=sel, t=top: e.tensor_scalar(out=s[:], in0=W.gm[:], scalar1=t[:, 3:4], scalar2=None, op0=ALU.is_ge),
                  [W.gm, top], [sel])
            p.dve(lambda e, s=sel, b=bsel: e.tensor_scalar(out=b[:], in0=s[:], scalar1=-NEG, scalar2=NEG, op0=ALU.mult, op1=ALU.add),
                  [sel], [bsel])
            p.dve(lambda e, b=bsel, own=own: e.memset(b[:, own:own + 1], 0.0), [], [bsel])
        tp = cx.tps[i % 2]
        p.pe(lambda e, o=tp[0:32, 0:128], b=bsel: e.transpose(out=o, in_=b[:], identity=cx.ident[:]), [bsel, cx.ident], [tp])
        p.act(lambda e, o=W.QA[:, i * 128:(i + 1) * 128], x=tp[0:32, 0:128]: e.copy(out=o, in_=x), [tp], [W.QA])


def softmax_job(p, cx, W, kind, q_ap, k_ap, v_ap, o_ap, fox_rows=None):
    S = W.S
    load_job(p, cx, W, q_ap, k_ap, v_ap)
    QA = KA = None
    if kind == "moba":
        moba_pre(p, cx, W)
        QA, KA = W.QA, W.onehot
    elif kind == "fox":
        p.dve(lambda e: e.memset(W.QA[:], 1.0), [], [W.QA])
        p.dve(lambda e: e.memset(W.KA[:], 1.0), [], [W.KA])
        p.dma(W.QA[0:3, :], fox_rows[0:3, :], [], [W.QA])
        p.dma(W.KA[3:6, :], fox_rows[3:6, :], [], [W.KA])
        QA, KA = W.QA, W.KA
    ovs = [a.rearrange("(g q p) d -> g p q d", p=128, q=4) for a in _L(o_ap)]
    units = []
    for g in range(S // 512):
        jlo = max(0, 4 * g - 16) if kind == "dil" else 0
        for j in range(jlo, 4 * g + 4):
            units.append(dict(g=g, j=j, jlo=jlo, idx=len(units)))
    NP = len(W.pt)

    def stage_a(t):
        g, j, k = t["g"], t["j"], t["idx"]
        r = max(0, j - 4 * g)
        c0 = 128 * r
        N = 512 - c0
        sps = p.ps(k % 2, F32)
        q0 = g * 512 + c0
        diag = (kind != "dil") and j >= 4 * g
        p.pe(lambda e, o=sps[:, 0:N], a=W.kT[:, j * 128:(j + 1) * 128], b=W.qT[:, q0:q0 + N], sp_=(QA is None and not diag):
             e.matmul(o, lhsT=a, rhs=b, start=True, stop=sp_), [W.kT, W.qT], [sps])
        if QA is not None:
            p.pe(lambda e, o=sps[:, 0:N], a=KA[:, j * 128:(j + 1) * 128], b=QA[:, q0:q0 + N], sp_=(not diag):
                 e.matmul(o, lhsT=a, rhs=b, start=False, stop=sp_), [KA, QA], [sps])
        if diag:
            p.pe(lambda e, o=sps[:, 0:128]: e.matmul(o, lhsT=cx.ident[:], rhs=W.tri[:], start=False, stop=True),
                 [cx.ident, W.tri], [sps])
        pt = W.pt[k % NP]
        p.act(lambda e, o=pt[:, 0:N], x=sps[:, 0:N]: e.activation(out=o, in_=x, func=AF.Exp), [sps], [pt])
        if kind == "dil":
            dm = W.dilm[:, 4 * g - j + 3, c0:512]
            p.dve(lambda e, o=pt[:, 0:N], m_=dm: e.tensor_tensor(out=o, in0=o, in1=m_, op=ALU.mult), [pt, W.dilm], [pt])

    def stage_b(t):
        g, j, jlo, k = t["g"], t["j"], t["jlo"], t["idx"]
        r = max(0, j - 4 * g)
        pt = W.pt[k % NP]
        accs = [p.ps(2 + qb, F32) for qb in range(4)]
        osb = W.osb[g % 2]
        for qb in range(r, 4):
            lc = (qb - r) * 128
            last = (j == 4 * g + qb)
            p.pe(lambda e, o=accs[qb][:, 0:65], x=pt[:, lc:lc + 128], vv=W.v[:, j, :], st=(j == jlo), sp_=last:
                 e.matmul(o, lhsT=x, rhs=vv, start=st, stop=sp_), [pt, W.v], [accs[qb]])
            if last:
                rv = W.rinv[qb]
                p.dve(lambda e, o=rv[:], x=accs[qb][:, 64:65]: e.reciprocal(out=o, in_=x), [accs[qb]], [rv])
                p.dve(lambda e, o=osb[:, qb, :], x=accs[qb][:, 0:DH], s_=rv[:, 0:1]: e.tensor_scalar(
                    out=o, in0=x, scalar1=s_, scalar2=None, op0=ALU.mult), [accs[qb], rv], [osb])
        if j == 4 * g + 3:
            store_out(p, cx, W, osb, [a_[g] for a_ in ovs])

    LAG = 2
    for k, t in enumerate(units):
        stage_a(t)
        if k >= LAG:
            stage_b(units[k - LAG])
    for t in units[max(0, len(units) - LAG):]:
        stage_b(t)


def fox_pre(p, cx, S, logf4, caug):
    m = p.mark()
    PW = min(2048, S)
    lf = [p.sb([4, PW], F32, f"fx_lf{i}") for i in range(2)]
    ones = p.sb([4, PW], F32, "fx_ones")
    c = [p.sb([4, PW], F32, f"fx_c{i}") for i in range(2)]
    r1 = p.sb([4, PW], F32, "fx_r1")
    r2 = p.sb([4, PW], F32, "fx_r2")
    rows = [[p.sb([4, PW], BF16, f"fx_row{k}_{i}") for k in range(6)] for i in range(2)]
    p.dve(lambda e: e.memset(ones[:], 1.0), [], [ones])
    prev = None
    for pc in range(S // PW):
        l = lf[pc % 2]
        cc = c[pc % 2]
        rw = rows[pc % 2]
        load_sel(p, cx, l, l[:], [a[:, pc * PW:(pc + 1) * PW] for a in _L(logf4)], (r1, r1[:]), npart=4)
        init = 0.0 if prev is None else prev[:, PW - 1:PW]
        rd = [ones, l] + ([prev] if prev is not None else [])
        p.dve(lambda e, o=cc[:], d1=l[:], init=init: e.tensor_tensor_scan(out=o, data0=ones[:], data1=d1, initial=init,
                                                                         op0=ALU.mult, op1=ALU.add), rd, [cc])
        prev = cc
        hi, mid, lo, nhi, nmid, nlo = rw
        p.dve(lambda e, o=hi[:], x=cc[:]: e.tensor_copy(out=o, in_=x), [cc], [hi])
        p.dve(lambda e, o=r1[:], x=cc[:], y=hi[:]: e.tensor_tensor(out=o, in0=x, in1=y, op=ALU.subtract), [cc, hi], [r1])
        p.dve(lambda e, o=mid[:]: e.tensor_copy(out=o, in_=r1[:]), [r1], [mid])
        p.dve(lambda e, o=r2[:], y=mid[:]: e.tensor_tensor(out=o, in0=r1[:], in1=y, op=ALU.subtract), [r1, mid], [r2])
        p.dve(lambda e, o=lo[:]: e.tensor_copy(out=o, in_=r2[:]), [r2], [lo])
        for src, dst in ((hi, nhi), (mid, nmid), (lo, nlo)):
            p.dve(lambda e, o=dst[:], x=src[:]: e.tensor_scalar(out=o, in0=x, scalar1=-1.0, scalar2=None, op0=ALU.mult),
                  [src], [dst])
        for k in range(6):
            p.dma(caug[:, k, pc * PW:(pc + 1) * PW], rw[k][:], [rw[k]], [])
    p.barrier()
    p.release(m)


def build_attn(S, layer, njobs=4, only=None):
    nc, stack, p = new_prog()
    with stack:
        dt = lambda n, sh, d=BF16, k="ExternalInput": nc.dram_tensor(n, sh, d, kind=k).ap()
        ident = dt("ident", [128, 128])
        cx = make_ctx(p, ident)
        consts = {"tri": dt("tri", [128, 128])}
        if layer == 0:
            consts["sbmask"] = dt("sbmask", [128, 128])
            consts["onehot"] = dt("onehot", [32, S])
        else:
            consts["dilm"] = dt("dilm", [20, 128, 512])
            logf4 = dt("logf4", [4, S], F32)
            caug = nc.dram_tensor("caug", [4, 6, S], BF16).ap()
            fox_pre(p, cx, S, logf4, caug)
        W = attn_setup(p, cx, S, layer, consts)
        dq, dk, dv = dt("dq", [njobs, DH, S]), dt("dk", [njobs, DH, S]), dt("dv", [njobs, S, DH])
        sq, sk, sv = dt("sq", [njobs, DH, S]), dt("sk", [njobs, DH, S]), dt("sv", [njobs, S, DH])
        od = dt("od", [njobs, S, DH], BF16, "ExternalOutput")
        os_ = dt("os", [njobs, S, DH], BF16, "ExternalOutput")
        for jb in range(njobs):
            if layer == 0:
                if only in (None, "d"):
                    sb_job(p, cx, W, dq[jb], dk[jb], dv[jb], od[jb])
                if only in (None, "s"):
                    softmax_job(p, cx, W, "moba", sq[jb], sk[jb], sv[jb], os_[jb])
            else:
                if only in (None, "d"):
                    softmax_job(p, cx, W, "fox", dq[jb], dk[jb], dv[jb], od[jb], fox_rows=caug[jb])
                if only in (None, "s"):
                    softmax_job(p, cx, W, "dil", sq[jb], sk[jb], sv[jb], os_[jb])
        p.barrier()
        p.emit()
    return nc


def attn_consts_np(S, layer):
    ii = np.arange(128)
    c = {"ident": np.eye(128, dtype=np.float32).astype(NPBF),
         "tri": np.where(ii[:, None] <= ii[None, :], 0.0, NEG).astype(np.float32).astype(NPBF)}
    if layer == 0:
        c["sbmask"] = np.where(ii[None, :] + ii[:, None] >= 128, 0.0, NEG).astype(np.float32).astype(NPBF)
        c["onehot"] = (np.arange(S)[None, :] // 256 == np.arange(32)[:, None]).astype(np.float32).astype(NPBF)
    else:
        dm = np.zeros((20, 128, 512), np.float32)
        ss = np.arange(128)[:, None]
        tt = np.arange(512)[None, :]
        for d in range(-3, 17):
            o = 128 * d + tt - ss
            m = np.zeros_like(o, dtype=np.float32)
            for w, r in ((128, 1), (512, 4), (2048, 16)):
                m += ((o >= 0) & (o <= w) & (o % r == 0)).astype(np.float32)
            dm[d + 3] = m
        c["dilm"] = dm.astype(NPBF)
    return c


def build_tok_launch(T, plan):
    nc, stack, p = new_prog()
    with stack:
        dt = lambda n, sh, d=F32, k="ExternalInput": nc.dram_tensor(n, sh, d, kind=k).ap()
        ident = dt("ident", [128, 128], BF16)
        cx = make_ctx(p, ident)
        cur = dt("x", [T, D])
        for si, st in enumerate(plan):
            last_x = not any(s_ in ("outproj", "ffn") for s_ in plan[si + 1:])
            if st in ("outproj", "ffn"):
                nxt = (dt(f"xo{si}", [T, D], F32, "ExternalOutput") if last_x
                       else nc.dram_tensor(f"xs{si}", [T, D], F32).ap())
            if st == "ffn":
                stage_ffn(p, cx, T, cur, nxt, dt(f"gain{si}", [D]), dt(f"wg{si}", [D, DFF]), dt(f"wu{si}", [D, DFF]),
                          dt(f"wd{si}", [DFF, D]))
                cur = nxt
            elif st == "outproj":
                stage_outproj(p, cx, T, cur, dt(f"attn{si}", [T, D], BF16), nxt, dt(f"wo{si}", [D, D]))
                cur = nxt
            else:
                layer = int(st[-1])
                NW = 3 * D + (8 if layer == 1 else 0)
                ng = 1 if layer == 0 else 2
                gq = [dt(f"gq{j}", [DH]) for j in range(ng)]
                gk = [dt(f"gk{j}", [DH]) for j in range(ng)]
                kw = {}
                if layer == 1:
                    kw = dict(bf=dt("bf", [8]), logf_out=dt("logf", [8, T], F32, "ExternalOutput"))
                stage_qkv(p, cx, T, layer, cur, dt(f"gain{si}", [D]), dt("w_in", [D, NW]), gq, gk,
                          dt("cos", [T, 32]), dt("sin", [T, 32]),
                          dt("qT", [NH, DH, T], BF16, "ExternalOutput"), dt("kT", [NH, DH, T], BF16, "ExternalOutput"),
                          dt("v", [T, D], BF16, "ExternalOutput"), **kw)
        p.emit()
    return nc


def _run(nc, in_maps):
    in_maps = [{k: np.ascontiguousarray(v) for k, v in m.items()} for m in in_maps]
    return run_bass_kernel_spmd(nc, in_maps, core_ids=list(range(len(in_maps)))).results


def kernel_unfused(x, norm_ffn1, ffn1_w_gate, ffn1_w_up, ffn1_w_down, norm_mix, w_in_ab, g_q_b, g_k_b,
           w_in_cd, b_f, g_q_c, g_k_c, g_q_d, g_k_d, w_out, norm_ffn2, ffn2_w_gate, ffn2_w_up,
           ffn2_w_down):
    A = lambda a: np.asarray(a, dtype=np.float32)
    x = A(x)
    B, S, _ = x.shape
    NC = 8
    T = B * S // NC
    half_of = lambda c: (c // 2, c % 2)
    ident = np.eye(128, dtype=np.float32).astype(NPBF)
    inv = (1.0 / (10000.0 ** (np.arange(0, DH, 2, dtype=np.float32) / DH))).astype(np.float32)
    ang = (np.arange(S, dtype=np.float32)[:, None] * inv[None, :]).astype(np.float32)
    cos, sin = np.cos(ang).astype(np.float32), np.sin(ang).astype(np.float32)

    def tok_shard(c, arr):
        b, h = half_of(c)
        return arr[b, h * T:(h + 1) * T]

    def ffn_w(si, norm, wg, wu, wd, l):
        return {f"gain{si}": A(norm[l]), f"wg{si}": A(wg[l]), f"wu{si}": A(wu[l]), f"wd{si}": A(wd[l])}

    def attn_inputs(layer, res):
        consts = attn_consts_np(S, layer)
        maps = []
        for c in range(NC):
            b, h = half_of(c)
            QT = np.concatenate([res[2 * b]["qT"], res[2 * b + 1]["qT"]], axis=2)
            KT = np.concatenate([res[2 * b]["kT"], res[2 * b + 1]["kT"]], axis=2)
            V = np.concatenate([res[2 * b]["v"], res[2 * b + 1]["v"]], axis=0)
            V = V.reshape(S, NH, DH).transpose(1, 0, 2)
            hd = slice(4 * h, 4 * h + 4)
            hs = slice(8 + 4 * h, 12 + 4 * h)
            m = dict(consts)
            if layer == 0:
                m.update(dq=QT[hd], dk=KT[hd][:, :, ::-1], dv=V[hd][:, ::-1, :])
            else:
                LF = np.concatenate([res[2 * b]["logf"], res[2 * b + 1]["logf"]], axis=1)
                m.update(dq=QT[hd], dk=KT[hd], dv=V[hd], logf4=LF[hd])
            m.update(sq=QT[hs], sk=KT[hs], sv=V[hs])
            maps.append(m)
        return maps

    def attn_gather(res):
        outs = []
        for b in range(B):
            full = np.empty((S, NH, DH), dtype=NPBF)
            for h in range(2):
                r = res[2 * b + h]
                full[:, 4 * h:4 * h + 4] = np.asarray(r["od"]).transpose(1, 0, 2)
                full[:, 8 + 4 * h:12 + 4 * h] = np.asarray(r["os"]).transpose(1, 0, 2)
            full = full.reshape(S, D)
            outs += [full[0:T], full[T:2 * T]]
        return outs

    ncA = build_tok_launch(T, ["ffn", "qkv0"])
    mA = []
    for c in range(NC):
        m = {"ident": ident, "x": tok_shard(c, x), "w_in": A(w_in_ab[0]), "gq0": A(g_q_b[0]), "gk0": A(g_k_b[0]),
             "gain1": A(norm_mix[0]), "cos": cos[(c % 2) * T:(c % 2 + 1) * T], "sin": sin[(c % 2) * T:(c % 2 + 1) * T]}
        m.update(ffn_w(0, norm_ffn1, ffn1_w_gate, ffn1_w_up, ffn1_w_down, 0))
        mA.append(m)
    rA = _run(ncA, mA)
    ncB = build_attn(S, 0)
    rB = _run(ncB, attn_inputs(0, rA))
    att0 = attn_gather(rB)
    ncC = build_tok_launch(T, ["outproj", "ffn", "ffn", "qkv1"])
    mC = []
    for c in range(NC):
        m = {"ident": ident, "x": rA[c]["xo0"], "attn0": att0[c], "wo0": A(w_out[0]),
             "w_in": A(w_in_cd[0]), "gq0": A(g_q_c[0]), "gk0": A(g_k_c[0]), "gq1": A(g_q_d[0]), "gk1": A(g_k_d[0]),
             "bf": A(b_f[0]), "gain3": A(norm_mix[1]),
             "cos": cos[(c % 2) * T:(c % 2 + 1) * T], "sin": sin[(c % 2) * T:(c % 2 + 1) * T]}
        m.update(ffn_w(1, norm_ffn2, ffn2_w_gate, ffn2_w_up, ffn2_w_down, 0))
        m.update(ffn_w(2, norm_ffn1, ffn1_w_gate, ffn1_w_up, ffn1_w_down, 1))
        mC.append(m)
    rC = _run(ncC, mC)
    ncD = build_attn(S, 1)
    rD = _run(ncD, attn_inputs(1, rC))
    att1 = attn_gather(rD)
    ncE = build_tok_launch(T, ["outproj", "ffn"])
    mE = []
    for c in range(NC):
        m = {"ident": ident, "x": rC[c]["xo2"], "attn0": att1[c], "wo0": A(w_out[1])}
        m.update(ffn_w(1, norm_ffn2, ffn2_w_gate, ffn2_w_up, ffn2_w_down, 1))
        mE.append(m)
    rE = _run(ncE, mE)
    out = np.empty((B, S, D), dtype=np.float32)
    for c in range(NC):
        b, h = half_of(c)
        out[b, h * T:(h + 1) * T] = rE[c]["xo1"]
    return out


RG_PAIRS = [[0, 1], [2, 3], [4, 5], [6, 7]]


class XBuf:
    def __init__(self, nc, name, nelem, dt, pattern, **axes):
        ce = min(nelem, 128 * 16384)
        self.nch = nelem // ce
        assert self.nch * ce == nelem and ce % 128 == 0
        self.i = nc.dram_tensor(name + "_i", [self.nch, 128, ce // 128], dt)
        self.o = nc.dram_tensor(name + "_o", [self.nch, 128, ce // 128], dt)
        self.vi = self.i.ap().rearrange("c p f -> (c p f)").rearrange(pattern, **axes)
        self.vo = self.o.ap().rearrange("c p f -> (c p f)").rearrange(pattern, **axes)

    def exchange(self, p):
        for c in range(self.nch):
            p.collective(self.i[c], self.o[c], RG_PAIRS)


def build_fused(T, S):
    nc, stack, p = new_prog()
    with stack:
        dt = lambda n, sh, d=F32, k="ExternalInput": nc.dram_tensor(n, sh, d, kind=k).ap()
        ident = dt("ident", [128, 128], BF16)
        cx = make_ctx(p, ident, dt("rank", [128, 2]))
        x = dt("x", [T, D])
        y = dt("y", [T, D], F32, "ExternalOutput")
        cos, sin = dt("cos", [T, 32]), dt("sin", [T, 32])
        consts0 = {"tri": dt("tri", [128, 128], BF16), "sbmask": dt("sbmask", [128, 128], BF16),
                   "onehot": dt("onehot", [32, S], BF16), "antiid": dt("antiid", [128, 128], BF16)}
        consts1 = {"tri": consts0["tri"], "dilm": dt("dilm", [20, 128, 512], BF16)}
        ffn = [dict(gain=dt(f"gain{i}", [D]), wg=dt(f"wg{i}", [D, DFF]), wu=dt(f"wu{i}", [D, DFF]), wd=dt(f"wd{i}", [DFF, D]))
               for i in range(4)]
        nmix = [dt(f"nmix{i}", [D]) for i in range(2)]
        w_in = [dt("w_in0", [D, 3 * D]), dt("w_in1", [D, 3 * D + 8])]
        wo = [dt(f"wo{i}", [D, D]) for i in range(2)]
        gqb, gkb = dt("gqb", [DH]), dt("gkb", [DH])
        gqc, gkc, gqd, gkd = dt("gqc", [DH]), dt("gkc", [DH]), dt("gqd", [DH]), dt("gkd", [DH])
        bf = dt("bf", [8])
        xs = [nc.dram_tensor(f"xs{i}", [T, D], F32).ap() for i in range(5)]
        EQ = XBuf(nc, "eq", NH * DH * S, BF16, "(n d h t) -> n d h t", n=NH, d=DH, h=2)
        EK = XBuf(nc, "ek", NH * DH * S, BF16, "(n d h t) -> n d h t", n=NH, d=DH, h=2)
        EV = XBuf(nc, "ev", S * D, BF16, "(h n p k d) -> h n p k d", h=2, n=NH, p=128, d=DH)
        EKT = XBuf(nc, "ekt", S * 512, BF16, "(h n p k d) -> h n p k d", h=2, n=8, p=128, d=DH)
        ELF = XBuf(nc, "elf", 8 * S, F32, "(n h t) -> n h t", n=8, h=2)
        EA = XBuf(nc, "ea", S * D, BF16, "(h t c) -> h t c", h=2, c=D)
        caug = nc.dram_tensor("caug", [4, 6, S], BF16).ap()

        def exchange(bufs):
            for b_ in bufs:
                b_.exchange(p)
            p.barrier()

        def head_q(E, hh):
            return E.vo[hh].rearrange("d h t -> d (h t)")

        def cols(E, hh, out=False):
            v = (E.vi if out else E.vo).rearrange("h t c -> (h t) c")
            return v[:, hh * DH:(hh + 1) * DH]

        def pmv(E, hh):
            return E.vo[:, hh].rearrange("h p k d -> p h k d")

        def attention(layer):
            m = p.mark()
            if layer == 1:
                lf = ELF.vo.rearrange("n h t -> n (h t)")
                fox_pre(p, cx, S, [lf[0:4], lf[4:8]], caug)
            W = attn_setup(p, cx, S, layer, consts0 if layer == 0 else consts1, fused=True)
            for j in range(4):
                d0, d1, s0, s1 = j, 4 + j, 8 + j, 12 + j
                if layer == 0:
                    sb_job(p, cx, W, [head_q(EQ, d0), head_q(EQ, d1)], None, [pmv(EV, d0), pmv(EV, d1)],
                           [cols(EA, d0, True), cols(EA, d1, True)], ktok_ap=[pmv(EKT, d0), pmv(EKT, d1)])
                    kind = "moba"
                else:
                    softmax_job(p, cx, W, "fox", [head_q(EQ, d0), head_q(EQ, d1)], [head_q(EK, d0), head_q(EK, d1)],
                                [pmv(EV, d0), pmv(EV, d1)], [cols(EA, d0, True), cols(EA, d1, True)], fox_rows=caug[j])
                    kind = "dil"
                softmax_job(p, cx, W, kind, [head_q(EQ, s0), head_q(EQ, s1)], [head_q(EK, s0), head_q(EK, s1)],
                            [pmv(EV, s0), pmv(EV, s1)], [cols(EA, s0, True), cols(EA, s1, True)])
            p.barrier()
            p.release(m)

        def qkv(layer, xin):
            kw = {"pm": True}
            if layer == 0:
                gq, gk = [gqb], [gkb]
                kw["ktok_out"] = [EKT.vi[0], EKT.vi[1]]
            else:
                gq, gk = [gqc, gqd], [gkc, gkd]
                kw.update(bf=bf, logf_out=[ELF.vi[:, 0, :], ELF.vi[:, 1, :]])
            stage_qkv(p, cx, T, layer, xin, nmix[layer], w_in[layer], gq, gk, cos, sin,
                      [EQ.vi[:, :, 0, :], EQ.vi[:, :, 1, :]], [EK.vi[:, :, 0, :], EK.vi[:, :, 1, :]],
                      [EV.vi[0], EV.vi[1]], **kw)

        stage_ffn(p, cx, T, x, xs[0], **ffn[0])
        qkv(0, xs[0])
        exchange([EQ, EK, EV, EKT])
        attention(0)
        exchange([EA])
        stage_outproj(p, cx, T, xs[0], [EA.vo[0], EA.vo[1]], xs[1], wo[0])
        stage_ffn(p, cx, T, xs[1], xs[2], **ffn[1])
        stage_ffn(p, cx, T, xs[2], xs[3], **ffn[2])
        qkv(1, xs[3])
        exchange([EQ, EK, EV, ELF])
        attention(1)
        exchange([EA])
        stage_outproj(p, cx, T, xs[3], [EA.vo[0], EA.vo[1]], xs[4], wo[1])
        stage_ffn(p, cx, T, xs[4], y, **ffn[3])
        p.emit()
    return nc


def kernel(x, norm_ffn1, ffn1_w_gate, ffn1_w_up, ffn1_w_down, norm_mix, w_in_ab, g_q_b, g_k_b,
           w_in_cd, b_f, g_q_c, g_k_c, g_q_d, g_k_d, w_out, norm_ffn2, ffn2_w_gate, ffn2_w_up,
           ffn2_w_down):
    A = lambda a: np.asarray(a, dtype=np.float32)
    x = A(x)
    B, S, _ = x.shape
    NC = 8
    T = B * S // NC
    inv = (1.0 / (10000.0 ** (np.arange(0, DH, 2, dtype=np.float32) / DH))).astype(np.float32)
    ang = (np.arange(S, dtype=np.float32)[:, None] * inv[None, :]).astype(np.float32)
    cos, sin = np.cos(ang).astype(np.float32), np.sin(ang).astype(np.float32)
    c0, c1 = attn_consts_np(S, 0), attn_consts_np(S, 1)
    shared = {"ident": c0["ident"], "tri": c0["tri"], "sbmask": c0["sbmask"], "onehot": c0["onehot"],
              "antiid": np.ascontiguousarray(np.eye(128, dtype=np.float32)[::-1]).astype(NPBF), "dilm": c1["dilm"],
              "nmix0": A(norm_mix[0]), "nmix1": A(norm_mix[1]), "w_in0": A(w_in_ab[0]), "w_in1": A(w_in_cd[0]),
              "wo0": A(w_out[0]), "wo1": A(w_out[1]), "gqb": A(g_q_b[0]), "gkb": A(g_k_b[0]),
              "gqc": A(g_q_c[0]), "gkc": A(g_k_c[0]), "gqd": A(g_q_d[0]), "gkd": A(g_k_d[0]), "bf": A(b_f[0])}
    for i, (nrm, wg, wu, wd, l) in enumerate([(norm_ffn1, ffn1_w_gate, ffn1_w_up, ffn1_w_down, 0),
                                               (norm_ffn2, ffn2_w_gate, ffn2_w_up, ffn2_w_down, 0),
                                               (norm_ffn1, ffn1_w_gate, ffn1_w_up, ffn1_w_down, 1),
                                               (norm_ffn2, ffn2_w_gate, ffn2_w_up, ffn2_w_down, 1)]):
        shared.update({f"gain{i}": A(nrm[l]), f"wg{i}": A(wg[l]), f"wu{i}": A(wu[l]), f"wd{i}": A(wd[l])})
    maps = []
    for c in range(NC):
        b, h = c // 2, c % 2
        m = dict(shared)
        rk = np.zeros((128, 2), np.float32)
        rk[:, h] = 1.0
        m.update(x=x[b, h * T:(h + 1) * T], rank=rk, cos=cos[h * T:(h + 1) * T], sin=sin[h * T:(h + 1) * T])
        maps.append(m)
    nc = build_fused(T, S)
    res = _run(nc, maps)
    out = np.empty((B, S, D), dtype=np.float32)
    for c in range(NC):
        b, h = c // 2, c % 2
        out[b, h * T:(h + 1) * T] = res[c]["y"]
    return out
```

```python
import numpy as np
import ml_dtypes
from contextlib import ExitStack
import concourse.bass as bass
import concourse.mybir as mybir
from concourse.bass_utils import run_bass_kernel_spmd

F32 = mybir.dt.float32
BF16 = mybir.dt.bfloat16
AF = mybir.ActivationFunctionType
ALU = mybir.AluOpType
AX = mybir.AxisListType
NPBF = ml_dtypes.bfloat16

D = 1024
DFF = 2816
NH = 16
DH = 64
S_FULL = 8192
B_FULL = 4
EPS = 1e-6
NEG = -30000.0
STW = 1540


class _State:
    __slots__ = ("last_w", "readers", "sem", "cnt")

    def __init__(self):
        self.last_w = None
        self.readers = []
        self.sem = None
        self.cnt = 0


class Buf:
    __slots__ = ("t", "st", "name")

    def __init__(self, t, name, st=None):
        self.t = t
        self.name = name
        self.st = st if st is not None else _State()

    def __getitem__(self, k):
        return self.t[k]

    last_w = property(lambda s: s.st.last_w, lambda s, v: setattr(s.st, "last_w", v))
    readers = property(lambda s: s.st.readers, lambda s, v: setattr(s.st, "readers", v))
    sem = property(lambda s: s.st.sem, lambda s, v: setattr(s.st, "sem", v))
    cnt = property(lambda s: s.st.cnt, lambda s, v: setattr(s.st, "cnt", v))


class Ins:
    __slots__ = ("eng", "fn", "deps", "needs_inc", "count", "epoch", "dma_tok", "idx", "inc")

    def __init__(self, eng, fn):
        self.eng = eng
        self.fn = fn
        self.deps = []
        self.needs_inc = False
        self.count = 0
        self.epoch = 0
        self.dma_tok = None
        self.idx = 0
        self.inc = 16


ENGS = ("pe", "act", "dve", "pool", "sp")
EPOCH = 20000


class Prog:
    ARENA_BYTES = 212000

    def __init__(self, nc, stack):
        self.nc = nc
        self.stack = stack
        self.ins = {e: [] for e in ENGS}
        self.nsb = 0
        self.arena = stack.enter_context(nc.sbuf_tensor("arena", [128, self.ARENA_BYTES // 2], BF16))
        self.off = 0
        self.live = []
        self.sem_pool = []
        self.nsem = 0
        self.cc_sem = None
        self.cc_cnt = 0
        self.banks = [stack.enter_context(nc.psum_tensor(f"bank{i}", [128, 512], F32)) for i in range(8)]
        self.bank_state = [_State() for _ in range(8)]

    def sb(self, shape, dt, name=None):
        self.nsb += 1
        name = name or f"sb{self.nsb}"
        esz = 4 if dt == F32 else 2
        n = 1
        for d in shape[1:]:
            n *= d
        nbytes = (n * esz + 63) // 64 * 64
        assert self.off + nbytes <= self.ARENA_BYTES, (name, self.off, nbytes)
        v = self.arena[0:shape[0], self.off // 2:(self.off + n * esz) // 2]
        if dt == F32:
            v = v.bitcast(F32)
        if len(shape) == 3:
            v = v.rearrange("p (a b) -> p a b", a=shape[1])
        self.off += nbytes
        b = Buf(v, name)
        self.live.append((self.off - nbytes, b))
        return b

    def mark(self):
        return self.off

    def release(self, m):
        keep = []
        for off, b in self.live:
            if off >= m:
                if b.sem is not None:
                    self.sem_pool.append((b.sem, b.cnt))
                    b.st.sem = None
            else:
                keep.append((off, b))
        self.live = keep
        self.off = m

    def ps(self, bank, dt=F32, name=None):
        v = self.banks[bank][:, :]
        if dt == BF16:
            v = v.bitcast(BF16)
        return Buf(v, name or f"bank{bank}", self.bank_state[bank])

    def _add(self, eng, fn, reads, writes, dma=False):
        i = Ins(eng, fn)
        i.idx = len(self.ins[eng])
        deps = []
        for b in reads:
            if b.last_w is not None:
                deps.append(b.last_w)
        for b in writes:
            if b.last_w is not None:
                deps.append(b.last_w)
            for r in b.readers:
                deps.append(r)
        seen = set()
        for d in deps:
            if id(d) in seen or d is i:
                continue
            seen.add(id(d))
            if d.dma_tok is None and d.eng == eng:
                if eng == "pe":
                    continue
                if not any((b.last_w is d) for b in list(reads) + list(writes)):
                    continue
            d.needs_inc = True
            i.deps.append(d)
        if dma:
            key = None
            for b in list(writes) + list(reads):
                key = b
                break
            if key.sem is None:
                if self.sem_pool:
                    key.sem, key.cnt = self.sem_pool.pop()
                else:
                    self.nsem += 1
                    key.sem = self.stack.enter_context(self.nc.semaphore(f"d{self.nsem}_{key.name}"))
                    key.cnt = 0
            key.cnt += 16
            i.dma_tok = (key.sem, key.cnt)
        for b in reads:
            b.readers.append(i)
        for b in writes:
            b.last_w = i
            b.readers = []
        self.ins[eng].append(i)
        return i

    def pe(self, fn, reads, writes):
        return self._add("pe", fn, reads, writes)

    def act(self, fn, reads, writes):
        return self._add("act", fn, reads, writes)

    def dve(self, fn, reads, writes):
        return self._add("dve", fn, reads, writes)

    def pool(self, fn, reads, writes):
        return self._add("pool", fn, reads, writes)

    def dma(self, out_ap, in_ap, reads, writes, eng=None):
        if eng is None:
            eng = "sp"
        return self._add(eng, lambda e: e.dma_start(out=out_ap, in_=in_ap), reads, writes, dma=True)

    def collective(self, in_ap, out_ap, groups):
        if self.cc_sem is None:
            self.cc_sem = self.stack.enter_context(self.nc.semaphore("cc_sem"))
        i = Ins("pool", lambda e: e.collective_compute("AllReduce", ALU.add, replica_groups=groups,
                                                       ins=[in_ap], outs=[out_ap]))
        self.cc_cnt += 1
        i.dma_tok = (self.cc_sem, self.cc_cnt)
        i.inc = 1
        self.ins["pool"].append(i)
        return i

    def barrier(self):
        lasts = []
        for e in ENGS:
            if e == "sp":
                continue
            for i in reversed(self.ins[e]):
                if i.fn is not None and i.dma_tok is None:
                    lasts.append(i)
                    break
        dmas = {}
        for e in ENGS:
            for i in self.ins[e]:
                if i.dma_tok is not None:
                    dmas[id(i.dma_tok[0])] = i
        for e in ENGS:
            i = Ins(e, None)
            for d in lasts:
                if d.eng != e:
                    d.needs_inc = True
                    i.deps.append(d)
            for d in dmas.values():
                i.deps.append(d)
            self.ins[e].append(i)

    def emit(self):
        nc = self.nc
        sems = {}
        for e in ENGS:
            c = 0
            ep = 0
            for i in self.ins[e]:
                if i.dma_tok is None and i.needs_inc:
                    if c >= EPOCH:
                        ep += 1
                        c = 0
                    c += 1
                    i.count = c
                    i.epoch = ep
                    if (e, ep) not in sems:
                        sems[(e, ep)] = self.stack.enter_context(nc.semaphore(f"s_{e}{ep}"))
        block = self.stack.enter_context(nc.Block())
        engobj = {"pe": block.tensor, "act": block.scalar, "dve": block.vector,
                  "pool": block.gpsimd, "sp": block.sync}

        def make(e):
            def body(eng):
                waited = {}
                for i in self.ins[e]:
                    need = {}
                    for d in i.deps:
                        if d.dma_tok is not None:
                            sem, val = d.dma_tok
                        else:
                            sem, val = sems[(d.eng, d.epoch)], d.count
                        k = id(sem)
                        if waited.get(k, 0) >= val:
                            continue
                        if k not in need or need[k][1] < val:
                            need[k] = (sem, val)
                    for k, (sem, val) in need.items():
                        eng.wait_ge(sem, val)
                        waited[k] = val
                    if i.fn is None:
                        continue
                    r = i.fn(eng)
                    if i.dma_tok is not None:
                        r.then_inc(i.dma_tok[0], i.inc)
                    elif i.needs_inc:
                        r.then_inc(sems[(e, i.epoch)], 1)
            return body

        for e in ENGS:
            if self.ins[e]:
                engobj[e](make(e))


def new_prog():
    nc = bass.Bass("TRN2", target_bir_lowering=False)
    stack = ExitStack()
    return nc, stack, Prog(nc, stack)


class Ctx:
    pass


def load_const(p, ap, shape, dt, name):
    b = p.sb(shape, dt, name)
    p.dma(b[:], ap, [], [b])
    return b


def load_weight_bf16(p, w_ap, K, N, name, stage, eng_rr):
    kc = K // 128
    wv = w_ap.rearrange("(c p) n -> p c n", p=128)
    npiece = (N + STW - 1) // STW
    pw = (N + npiece - 1) // npiece
    chunks = []
    for c in range(kc):
        wb = p.sb([128, N], BF16, f"{name}{c}")
        chunks.append(wb)
        for n0 in range(0, N, pw):
            n1 = min(N, n0 + pw)
            st = stage[eng_rr[0] % len(stage)]
            eng_rr[0] += 1
            p.dma(st[:, 0:n1 - n0], wv[:, c, n0:n1], [], [st])
            src = st[:, 0:n1 - n0]
            dst = wb[:, n0:n1]
            if eng_rr[0] % 2 == 0:
                p.dve(lambda e, d=dst, s=src: e.tensor_copy(out=d, in_=s), [st], [wb])
            else:
                p.act(lambda e, d=dst, s=src: e.copy(out=d, in_=s), [st], [wb])
    return chunks


def rms_to_T(p, cx, x_sb, g_bc, xnT, col0, nsub):
    for i in range(nsub):
        xs = x_sb[i]
        ss = cx.small[cx.rr % len(cx.small)]
        cx.rr += 1
        junk = cx.junk
        p.act(lambda e, o=junk[:], a=xs[:], s=ss[:, 0:1]: e.activation(out=o, in_=a, func=AF.Square, accum_out=s),
              [xs], [junk, ss])
        p.act(lambda e, s=ss: e.activation(out=s[:, 1:2], in_=s[:, 0:1], func=AF.Sqrt, scale=1.0 / D, bias=cx.eps[:, 0:1]),
              [ss, cx.eps], [ss])
        p.dve(lambda e, s=ss: e.reciprocal(out=s[:, 2:3], in_=s[:, 1:2]), [ss], [ss])
        xn = cx.xn[cx.rr % len(cx.xn)]
        p.dve(lambda e, o=xn[:], a=xs[:], s=ss[:, 2:3], g=g_bc[:]: e.scalar_tensor_tensor(
            out=o, in0=a, scalar=s, in1=g, op0=ALU.mult, op1=ALU.mult), [xs, ss, g_bc], [xn])
        transpose_to(p, cx, xn, xnT, 0, col0 + 128 * i, D // 128)


def transpose_to(p, cx, src, dstT, c0, col, nchunk, flat=None):
    tp = cx.tps[cx.rrt % len(cx.tps)]
    cx.rrt += 1
    sv = flat if flat is not None else src[:]
    for c in range(nchunk):
        p.pe(lambda e, o=tp[:, c * 128:(c + 1) * 128], a=sv[:, c * 128:(c + 1) * 128], idn=cx.ident[:]:
             e.transpose(out=o, in_=a, identity=idn), [src, cx.ident], [tp])
    o = dstT[:, c0:c0 + nchunk, col:col + 128]
    i_ = tp[:, 0:nchunk * 128].rearrange("p (c t) -> p c t", c=nchunk)
    if cx.rrt % 2 == 0:
        p.act(lambda e, o=o, i_=i_: e.copy(out=o, in_=i_), [tp], [dstT])
    else:
        p.dve(lambda e, o=o, i_=i_: e.tensor_copy(out=o, in_=i_), [tp], [dstT])


def emit_out(p, cx, src_buf, src_ap, dsts, tmps, npart=128):
    if len(dsts) == 1:
        p.dma(dsts[0], src_ap, [src_buf], [])
        return
    for s_ in range(2):
        t = tmps[s_]
        p.act(lambda e, o=t[:], a=src_ap, m=cx.rk[0:npart, s_:s_ + 1]: e.mul(out=o, in_=a, mul=m), [src_buf, cx.rk], [t])
        p.dma(dsts[s_], t[:], [t], [])


def load_sel(p, cx, dst_buf, dst_ap, srcs, tmp, npart=128):
    if len(srcs) == 1:
        p.dma(dst_ap, srcs[0], [], [dst_buf])
        return
    tb, ta = tmp
    p.dma(ta, srcs[0], [], [tb])
    p.act(lambda e, o=dst_ap, a=ta, m=cx.rk[0:npart, 0:1]: e.mul(out=o, in_=a, mul=m), [tb, cx.rk], [dst_buf])
    p.dma(ta, srcs[1], [], [tb])
    p.dve(lambda e, o=dst_ap, b=ta, m=cx.rk[0:npart, 1:2]: e.scalar_tensor_tensor(
        out=o, in0=b, scalar=m, in1=o, op0=ALU.mult, op1=ALU.add), [tb, cx.rk, dst_buf], [dst_buf])


def make_ctx(p, ident_ap, rank_ap=None):
    cx = Ctx()
    cx.rr = 0
    cx.rrt = 0
    cx.ident = load_const(p, ident_ap, [128, 128], BF16, "ident")
    cx.eps = p.sb([128, 1], F32, "eps")
    p.dve(lambda e: e.memset(cx.eps[:], EPS), [], [cx.eps])
    cx.small = [p.sb([128, 4], F32, f"small{i}") for i in range(4)]
    cx.small8 = [p.sb([128, 24], F32, f"small8_{i}") for i in range(4)]
    cx.junk = p.sb([128, D], BF16, "junk")
    cx.xn = [p.sb([128, D], BF16, f"xn{i}") for i in range(1)]
    cx.tps = [p.ps(6 + i, BF16, f"tps{i}") for i in range(2)]
    cx.mm = [p.ps(i, F32, f"mm{i}") for i in range(6)]
    cx.stage = [p.sb([128, STW], F32, f"stage{i}") for i in range(2)]
    cx.rk = None
    if rank_ap is not None:
        cx.rk = load_const(p, rank_ap, [128, 2], F32, "rank")
    return cx


TG = 512
NSUB = TG // 128
KC = D // 128


def stage_ffn(p, cx, T, x_in, x_out, gain, wg, wu, wd):
    m = p.mark()
    rr = [0]
    mm_ps = cx.mm
    g_bc = p.sb([128, D], F32, "ffn_g")
    p.dma(g_bc[:], gain.partition_broadcast(128), [], [g_bc])
    Wg = load_weight_bf16(p, wg, D, DFF, "Wg", cx.stage, rr)
    Wu = load_weight_bf16(p, wu, D, DFF, "Wu", cx.stage, rr)
    Wd = load_weight_bf16(p, wd, DFF, D, "Wd", cx.stage, rr)
    nf = DFF // 128
    xpool = [p.sb([128, D], F32, f"ffn_x{i}") for i in range(5)]
    xnT = p.sb([128, KC, TG], BF16, "ffn_xnT")
    hT = p.sb([128, nf, TG], BF16, "ffn_hT")
    sg = [p.sb([128, TG], F32, f"ffn_sg{i}") for i in range(1)]
    xv = x_in.rearrange("(n p) d -> n p d", p=128)
    ov = x_out.rearrange("(n p) d -> n p d", p=128)
    for g in range(T // TG):
        xs = [xpool[(g * NSUB + i) % len(xpool)] for i in range(NSUB)]
        for i in range(NSUB):
            p.dma(xs[i][:], xv[g * NSUB + i], [], [xs[i]])
        rms_to_T(p, cx, xs, g_bc, xnT, 0, NSUB)
        for f in range(nf):
            pg = mm_ps[(2 * f) % len(mm_ps)]
            pu = mm_ps[(2 * f + 1) % len(mm_ps)]
            for k in range(KC):
                p.pe(lambda e, o=pg[:], w=Wg[k][:, f * 128:(f + 1) * 128], a=xnT[:, k, :], k=k:
                     e.matmul(o, lhsT=w, rhs=a, start=(k == 0), stop=(k == KC - 1)), [Wg[k], xnT], [pg])
            for k in range(KC):
                p.pe(lambda e, o=pu[:], w=Wu[k][:, f * 128:(f + 1) * 128], a=xnT[:, k, :], k=k:
                     e.matmul(o, lhsT=w, rhs=a, start=(k == 0), stop=(k == KC - 1)), [Wu[k], xnT], [pu])
            s = sg[f % len(sg)]
            p.act(lambda e, o=s[:], a=pg[:]: e.activation(out=o, in_=a, func=AF.Silu), [pg], [s])
            p.dve(lambda e, o=hT[:, f, :], a=s[:], b=pu[:]: e.tensor_tensor(out=o, in0=a, in1=b, op=ALU.mult),
                  [s, pu], [hT])
        for i in range(NSUB):
            for half in range(2):
                py = mm_ps[(2 * i + half) % len(mm_ps)]
                for f in range(nf):
                    p.pe(lambda e, o=py[:], a=hT[:, f, i * 128:(i + 1) * 128], w=Wd[f][:, half * 512:(half + 1) * 512], f=f:
                         e.matmul(o, lhsT=a, rhs=w, start=(f == 0), stop=(f == nf - 1)), [hT, Wd[f]], [py])
                p.dve(lambda e, o=xs[i][:, half * 512:(half + 1) * 512], a=py[:]: e.scalar_tensor_tensor(
                    out=o, in0=a, scalar=0.5, in1=o, op0=ALU.mult, op1=ALU.add), [py, xs[i]], [xs[i]])
            p.dma(ov[g * NSUB + i], xs[i][:], [xs[i]], [])
    p.barrier()
    p.release(m)


def stage_outproj(p, cx, T, x_in, attn_in, x_out, wo):
    attn_list = list(attn_in) if isinstance(attn_in, (list, tuple)) else [attn_in]
    m = p.mark()
    rr = [0]
    mm_ps = cx.mm
    Wo = load_weight_bf16(p, wo, D, D, "Wo", cx.stage, rr)
    xpool = [p.sb([128, D], F32, f"op_x{i}") for i in range(6)]
    apool = [p.sb([128, D], BF16, f"op_a{i}") for i in range(6)]
    aT = p.sb([128, KC, TG], BF16, "op_aT")
    xv = x_in.rearrange("(n p) d -> n p d", p=128)
    avs = [a.rearrange("(n p) d -> n p d", p=128) for a in attn_list]
    acand = [p.sb([128, D], BF16, f"op_c{i}") for i in range(2)] if len(avs) == 2 else None
    ov = x_out.rearrange("(n p) d -> n p d", p=128)
    for g in range(T // TG):
        xs = [xpool[(g * NSUB + i) % len(xpool)] for i in range(NSUB)]
        as_ = [apool[(g * NSUB + i) % len(apool)] for i in range(NSUB)]
        for i in range(NSUB):
            p.dma(xs[i][:], xv[g * NSUB + i], [], [xs[i]])
            load_sel(p, cx, as_[i], as_[i][:], [a[g * NSUB + i] for a in avs],
                     (acand[i % 2], acand[i % 2][:]) if acand else None)
        for i in range(NSUB):
            transpose_to(p, cx, as_[i], aT, 0, 128 * i, KC)
        for i in range(NSUB):
            for half in range(2):
                py = mm_ps[(2 * i + half) % len(mm_ps)]
                for k in range(KC):
                    p.pe(lambda e, o=py[:], a=aT[:, k, i * 128:(i + 1) * 128], w=Wo[k][:, half * 512:(half + 1) * 512], k=k:
                         e.matmul(o, lhsT=a, rhs=w, start=(k == 0), stop=(k == KC - 1)), [aT, Wo[k]], [py])
                p.dve(lambda e, o=xs[i][:, half * 512:(half + 1) * 512], a=py[:]: e.tensor_tensor(
                    out=o, in0=a, in1=o, op=ALU.add), [py, xs[i]], [xs[i]])
            p.dma(ov[g * NSUB + i], xs[i][:], [xs[i]], [])
    p.barrier()
    p.release(m)


def stage_qkv(p, cx, T, layer, x_in, gain, w_in, gq, gk, cos, sin, qT_out, kT_out, v_out, bf=None, logf_out=None,
              ktok_out=None, pm=False):
    m = p.mark()
    rr = [0]
    mm_ps = cx.mm
    NW = 3 * D + (8 if layer == 1 else 0)
    g_bc = p.sb([128, D], F32, "qkv_g")
    p.dma(g_bc[:], gain.partition_broadcast(128), [], [g_bc])
    Win = load_weight_bf16(p, w_in, D, NW, "Win", cx.stage, rr)
    gains = {}
    for nm, ap, sc in [("q", gq, 0.125), ("k", gk, 1.0)]:
        for j, a in enumerate(ap):
            t = p.sb([128, DH], F32, f"gain_{nm}{j}")
            p.dma(t[:], a.partition_broadcast(128), [], [t])
            if sc != 1.0:
                p.dve(lambda e, t=t, sc=sc: e.tensor_scalar(out=t[:], in0=t[:], scalar1=sc, scalar2=None, op0=ALU.mult),
                      [t], [t])
            gains[(nm, j)] = t
    if layer == 1:
        bf_bc = p.sb([128, 8], F32, "bf_bc")
        p.dma(bf_bc[:], bf.partition_broadcast(128), [], [bf_bc])
        lf = [p.sb([128, 32], F32, f"lf{i}") for i in range(2)]
    cs_pool = [(p.sb([128, 32], F32, f"cos{i}"), p.sb([128, 32], F32, f"sin{i}")) for i in range(2)]
    xpool = [p.sb([128, D], F32, f"qkv_x{i}") for i in range(6)]
    xnT = p.sb([128, KC, TG], BF16, "qkv_xnT")
    qT = p.sb([128, 8, TG], BF16, "qkv_qT")
    kT = p.sb([128, 8, TG], BF16, "qkv_kT")
    vb = [p.sb([128, D], BF16, f"qkv_v{i}") for i in range(2)]
    sq_l = [p.sb([128, 8, DH], F32, f"qkv_sq{i}") for i in range(2)]
    qn_l = [p.sb([128, 8, DH], F32, f"qkv_qn{i}") for i in range(2)]
    tmp_l = [[p.sb([128, 8, 32], F32, f"qkv_t{j}_{i}") for i in range(4)] for j in range(2)]
    qb = [p.sb([128, 8, DH], BF16, f"qkv_qb{i}") for i in range(4)]
    L = lambda a: list(a) if isinstance(a, (list, tuple)) else [a]
    xv = x_in.rearrange("(n p) d -> n p d", p=128)
    if pm:
        vvs = [[a[:, :, n_, :].rearrange("n p d -> p n d") for n_ in range(T // 128)] for a in L(v_out)]
    else:
        vvs = [a.rearrange("(n p) d -> n p d", p=128) for a in L(v_out)]
    cv = cos.rearrange("(n p) d -> n p d", p=128)
    sv = sin.rearrange("(n p) d -> n p d", p=128)
    qTvs = [a.rearrange("(hp two) d t -> (two d) hp t", two=2) for a in L(qT_out)]
    kTvs = [a.rearrange("(hp two) d t -> (two d) hp t", two=2) for a in L(kT_out)]
    two = len(qTvs) == 2
    tq = [p.sb([128, 8, TG], BF16, f"qkv_tq{i}") for i in range(2)] if two else None
    tv = [p.sb([128, NH, DH], BF16, f"qkv_tv{i}") for i in range(2)] if two else None
    tk = [p.sb([128, 8, DH], BF16, f"qkv_tk{i}") for i in range(2)] if two else None
    tl = [p.sb([128, 8], F32, f"qkv_tl{i}") for i in range(2)] if two else None
    if ktok_out is None:
        ktvs = None
    elif pm:
        ktvs = [[a[:, :, n_, :].rearrange("n p d -> p n d") for n_ in range(T // 128)] for a in L(ktok_out)]
    else:
        ktvs = [a.rearrange("(n p) d -> n p d", p=128) for a in L(ktok_out)]
    if layer == 1:
        lvs = L(logf_out)
        lfT = [p.sb([8, TG], F32, f"qkv_lfT{i}") for i in range(2)]
        tlT = [p.sb([8, TG], F32, f"qkv_tlT{i}") for i in range(2)] if two else None
        ident32 = p.sb([128, 128], F32, "ident32")
        p.dve(lambda e: e.tensor_copy(out=ident32[:], in_=cx.ident[:]), [cx.ident], [ident32])
    nb = 0
    nqk = 0
    pending = []
    for g in range(T // TG):
        xs = [xpool[(g * NSUB + i) % len(xpool)] for i in range(NSUB)]
        for i in range(NSUB):
            p.dma(xs[i][:], xv[g * NSUB + i], [], [xs[i]])
        rms_to_T(p, cx, xs, g_bc, xnT, 0, NSUB)
        for i in range(NSUB):
            n = g * NSUB + i
            cosb, sinb = cs_pool[n % 2]
            p.dma(cosb[:], cv[n], [], [cosb])
            p.dma(sinb[:], sv[n], [], [sinb])
            vbuf = vb[n % 2]
            for cg in range(6):
                ps = mm_ps[(nb) % len(mm_ps)]
                nb += 1
                for k in range(KC):
                    p.pe(lambda e, o=ps[:], a=xnT[:, k, i * 128:(i + 1) * 128], w=Win[k][:, cg * 512:(cg + 1) * 512], k=k:
                         e.matmul(o, lhsT=a, rhs=w, start=(k == 0), stop=(k == KC - 1)), [xnT, Win[k]], [ps])
                if cg >= 4:
                    p.act(lambda e, o=vbuf[:, (cg - 4) * 512:(cg - 3) * 512], a=ps[:]: e.copy(out=o, in_=a), [ps], [vbuf])
                    continue
                isq = cg < 2
                hg = cg % 2
                normed = (layer == 1) or (hg == 1)
                roped = (hg == 1)
                qbuf = qb[nqk % 4]
                sq, qn, tmp = sq_l[nqk % 2], qn_l[nqk % 2], tmp_l[nqk % 2]
                nqk += 1
                ps3 = ps[:].rearrange("p (h d) -> p h d", h=8)
                if not normed:
                    p.act(lambda e, o=qbuf[:], a=ps3, sc=(0.125 if isq else 1.0): e.mul(out=o, in_=a, mul=sc), [ps], [qbuf])
                else:
                    gj = 0 if layer == 0 else hg
                    gt = gains[("q" if isq else "k", gj)]
                    ss = cx.small8[cx.rr % len(cx.small8)]
                    cx.rr += 1
                    p.act(lambda e, o=sq[:], a=ps3: e.activation(out=o, in_=a, func=AF.Square), [ps], [sq])
                    p.dve(lambda e, o=ss[:, 0:8], a=sq[:]: e.tensor_reduce(out=o, in_=a, axis=AX.X, op=ALU.add), [sq], [ss])
                    p.act(lambda e, s=ss: e.activation(out=s[:, 8:16], in_=s[:, 0:8], func=AF.Sqrt, scale=1.0 / DH,
                                                       bias=cx.eps[:, 0:1]), [ss, cx.eps], [ss])
                    p.dve(lambda e, s=ss: e.reciprocal(out=s[:, 16:24], in_=s[:, 8:16]), [ss], [ss])
                    p.dve(lambda e, o=qn[:], a=ps3, s=ss: e.tensor_tensor(
                        out=o, in0=a, in1=s[:, 16:24].unsqueeze(2).to_broadcast([128, 8, DH]), op=ALU.mult), [ps, ss], [qn])
                    gbc = gt[:].unsqueeze(1).to_broadcast([128, 8, DH])
                    if not roped:
                        p.dve(lambda e, o=qbuf[:], a=qn[:], g_=gbc: e.tensor_tensor(out=o, in0=a, in1=g_, op=ALU.mult),
                              [qn, gt], [qbuf])
                    else:
                        p.dve(lambda e, o=qn[:], a=qn[:], g_=gbc: e.tensor_tensor(out=o, in0=a, in1=g_, op=ALU.mult),
                              [qn, gt], [qn])
                        cb = cosb[:].unsqueeze(1).to_broadcast([128, 8, 32])
                        sb_ = sinb[:].unsqueeze(1).to_broadcast([128, 8, 32])
                        x1 = qn[:, :, 0:32]
                        x2 = qn[:, :, 32:64]
                        t0, t1, t2, t3 = tmp
                        p.pool(lambda e, o=t0[:], a=x1, b=cb: e.tensor_tensor(out=o, in0=a, in1=b, op=ALU.mult), [qn, cosb], [t0])
                        p.pool(lambda e, o=t1[:], a=x2, b=sb_: e.tensor_tensor(out=o, in0=a, in1=b, op=ALU.mult), [qn, sinb], [t1])
                        p.dve(lambda e, o=t2[:], a=x2, b=cb: e.tensor_tensor(out=o, in0=a, in1=b, op=ALU.mult), [qn, cosb], [t2])
                        p.dve(lambda e, o=t3[:], a=x1, b=sb_: e.tensor_tensor(out=o, in0=a, in1=b, op=ALU.mult), [qn, sinb], [t3])
                        p.dve(lambda e, o=qbuf[:, :, 0:32], a=t0[:], b=t1[:]: e.tensor_tensor(out=o, in0=a, in1=b, op=ALU.subtract),
                              [t0, t1], [qbuf])
                        p.dve(lambda e, o=qbuf[:, :, 32:64], a=t2[:], b=t3[:]: e.tensor_tensor(out=o, in0=a, in1=b, op=ALU.add),
                              [t2, t3], [qbuf])
                def fin(qbuf=qbuf, isq=isq, hg=hg, i=i, n=n):
                    transpose_to(p, cx, qbuf, qT if isq else kT, hg * 4, 128 * i, 4,
                                 flat=qbuf.t.rearrange("p h d -> p (h d)"))
                    if ktvs is not None and (not isq) and hg == 0:
                        emit_out(p, cx, qbuf, qbuf[:] if pm else qbuf.t.rearrange("p h d -> p (h d)"), [a[n] for a in ktvs], tk)
                pending.append(fin)
                while len(pending) > 2:
                    pending.pop(0)()
            if layer == 1:
                ps = mm_ps[(nb) % len(mm_ps)]
                nb += 1
                for k in range(KC):
                    p.pe(lambda e, o=ps[:, 0:8], a=xnT[:, k, i * 128:(i + 1) * 128], w=Win[k][:, 3 * D:3 * D + 8], k=k:
                         e.matmul(o, lhsT=a, rhs=w, start=(k == 0), stop=(k == KC - 1)), [xnT, Win[k]], [ps])
                l = lf[n % 2]
                p.dve(lambda e, o=l[:, 0:8], a=ps[:, 0:8], b=bf_bc[:]: e.tensor_tensor(out=o, in0=a, in1=b, op=ALU.add),
                      [ps, bf_bc], [l])
                p.act(lambda e, l=l: e.activation(out=l[:, 8:16], in_=l[:, 0:8], func=AF.Exp, scale=-1.0), [l], [l])
                p.act(lambda e, l=l: e.activation(out=l[:, 16:24], in_=l[:, 8:16], func=AF.Ln, bias=1.0), [l], [l])
                p.dve(lambda e, l=l: e.tensor_scalar(out=l[:, 24:32], in0=l[:, 16:24], scalar1=-1.0, scalar2=None, op0=ALU.mult),
                      [l], [l])
                pst = mm_ps[(nb) % len(mm_ps)]
                nb += 1
                p.pe(lambda e, o=pst[0:8, 0:128], a=l[:, 24:32]: e.transpose(out=o, in_=a, identity=ident32[:]),
                     [l, ident32], [pst])
                lt = lfT[g % 2]
                p.act(lambda e, o=lt[:, i * 128:(i + 1) * 128], a=pst[0:8, 0:128]: e.copy(out=o, in_=a), [pst], [lt])
                if i == NSUB - 1:
                    emit_out(p, cx, lt, lt[:], [a[:, g * TG:(g + 1) * TG] for a in lvs], tlT, npart=8)
            emit_out(p, cx, vbuf, vbuf[:].rearrange("p (h d) -> p h d", h=NH) if pm else vbuf[:], [a[n] for a in vvs], tv)
        while pending:
            pending.pop(0)()
        emit_out(p, cx, qT, qT[:], [a[:, :, g * TG:(g + 1) * TG] for a in qTvs], tq)
        emit_out(p, cx, kT, kT[:], [a[:, :, g * TG:(g + 1) * TG] for a in kTvs], tq)
    p.barrier()
    p.release(m)


def build_test_tok(T, layer, which):
    nc, stack, p = new_prog()
    with stack:
        dt = lambda n, sh, d=F32, k="ExternalInput": nc.dram_tensor(n, sh, d, kind=k).ap()
        ident = dt("ident", [128, 128], BF16)
        cx = make_ctx(p, ident)
        x = dt("x", [T, D])
        if which == "ffn":
            y = dt("y", [T, D], F32, "ExternalOutput")
            stage_ffn(p, cx, T, x, y, dt("gain", [D]), dt("wg", [D, DFF]), dt("wu", [D, DFF]), dt("wd", [DFF, D]))
        elif which == "outproj":
            y = dt("y", [T, D], F32, "ExternalOutput")
            stage_outproj(p, cx, T, x, dt("attn", [T, D], BF16), y, dt("wo", [D, D]))
        elif which == "qkv":
            NW = 3 * D + (8 if layer == 1 else 0)
            ng = 1 if layer == 0 else 2
            gq = [dt(f"gq{j}", [DH]) for j in range(ng)]
            gk = [dt(f"gk{j}", [DH]) for j in range(ng)]
            qT = dt("qT", [NH, DH, T], BF16, "ExternalOutput")
            kT = dt("kT", [NH, DH, T], BF16, "ExternalOutput")
            v = dt("v", [T, D], BF16, "ExternalOutput")
            kw = {}
            if layer == 1:
                kw = dict(bf=dt("bf", [8]), logf_out=dt("logf", [8, T], F32, "ExternalOutput"))
            stage_qkv(p, cx, T, layer, x, dt("gain", [D]), dt("w_in", [D, NW]), gq, gk,
                      dt("cos", [T, 32]), dt("sin", [T, 32]), qT, kT, v, **kw)
        p.emit()
    return nc


class AttnWork:
    pass


def attn_setup(p, cx, S, layer, consts, fused=False):
    W = AttnWork()
    W.S = S
    W.fused = fused
    if fused:
        W.cq = p.sb([64, S], BF16, "a_cq")
        W.cv = p.sb([128, S // 128, DH], BF16, "a_cv")
        W.otmp = [p.sb([128, 4, DH], BF16, f"a_otmp{i}") for i in range(2)]
        if layer == 0:
            W.J = load_const(p, consts["antiid"], [128, 128], BF16, "antiid")
            W.ktok = p.sb([128, S // 128, DH], BF16, "a_ktok")
            W.vnat = p.sb([128, S // 128, DH], BF16, "a_vnat")
    W.qT = p.sb([64, S], BF16, "a_qT")
    W.kT = p.sb([64, S], BF16, "a_kT")
    W.v = p.sb([128, S // 128, 65], BF16, "a_v")
    p.dve(lambda e: e.memset(W.v[:, :, 64:65], 1.0), [], [W.v])
    W.tri = load_const(p, consts["tri"], [128, 128], BF16, "tri")
    W.osb = [p.sb([128, 4, DH], BF16, f"a_o{i}") for i in range(2)]
    W.rinv = [p.sb([128, 1], F32, f"a_rinv{i}") for i in range(4)]
    W.pt = [p.sb([128, 512], BF16, f"a_pt{i}") for i in range(4)]
    if layer == 0:
        W.sbmask = load_const(p, consts["sbmask"], [128, 128], BF16, "sbmask")
        W.onehot = load_const(p, consts["onehot"], [32, S], BF16, "onehot")
        W.QA = p.sb([32, S], BF16, "a_QA")
        W.ones = p.sb([128, 512], F32, "a_ones")
        p.dve(lambda e: e.memset(W.ones[:], 1.0), [], [W.ones])
        W.e = [p.sb([128, 512], F32, f"a_e{i}") for i in range(4)]
        W.sp = [p.sb([128, 512], F32, f"a_sp{i}") for i in range(4)]
        W.cs = [p.sb([128, 512], F32, f"a_cs{i}") for i in range(4)]
        W.ecs = [p.sb([128, 512], F32, f"a_ecs{i}") for i in range(4)]
        W.a = [p.sb([128, 512], BF16, f"a_a{i}") for i in range(4)]
        W.at = [p.sb([128, 512], BF16, f"a_at{i}") for i in range(2)]
        W.km = p.sb([64, 32], F32, "a_km")
        W.kmh = p.sb([64, 32], BF16, "a_kmh")
        W.kml = p.sb([64, 32], BF16, "a_kml")
        W.gm = p.sb([128, 32], F32, "a_gm")
        W.top8 = [p.sb([128, 8], F32, f"a_top{i}") for i in range(2)]
        W.sel = [p.sb([128, 32], F32, f"a_sel{i}") for i in range(2)]
        W.bsel = [p.sb([128, 32], BF16, f"a_bsel{i}") for i in range(2)]
    else:
        W.dilm = load_const(p, consts["dilm"].rearrange("k p t -> p k t"), [128, 20, 512], BF16, "dilm")
        W.QA = p.sb([6, S], BF16, "a_QA")
        W.KA = p.sb([6, S], BF16, "a_KA")
    return W


def _L(a):
    return list(a) if isinstance(a, (list, tuple)) else [a]


def load_job(p, cx, W, q_ap, k_ap, v_ap):
    tq = (W.cq, W.cq[:]) if W.fused else None
    tv = (W.cv, W.cv[:]) if W.fused else None
    load_sel(p, cx, W.qT, W.qT[:], _L(q_ap), tq, npart=64)
    if k_ap is not None:
        load_sel(p, cx, W.kT, W.kT[:], _L(k_ap), tq, npart=64)
    if v_ap is not None:
        _load_tok(p, cx, W, W.v, W.v[:, :, 0:DH], v_ap)


def _load_tok(p, cx, W, dst_buf, dst_ap, src):
    srcs = _L(src)
    if len(srcs[0].shape) == 2:
        load_sel(p, cx, dst_buf, dst_ap, [a.rearrange("(k p) d -> p k d", p=128) for a in srcs],
                 (W.cv, W.cv[:]) if W.fused else None)
    else:
        r = lambda ap: ap.rearrange("p (h k) d -> p h k d", h=2)
        load_sel(p, cx, dst_buf, r(dst_ap), srcs, (W.cv, r(W.cv[:])))


def store_out(p, cx, W, osb, dsts):
    emit_out(p, cx, osb, osb[:], dsts, W.otmp if W.fused else None)


def sb_reverse(p, cx, W, ktok_ap, v_ap):
    S = W.S
    nb = S // 128
    _load_tok(p, cx, W, W.ktok, W.ktok[:], ktok_ap)
    _load_tok(p, cx, W, W.vnat, W.vnat[:], v_ap)
    for r0 in range(0, nb, 4):
        ps = p.ps(r0 // 4 % 2, F32)
        for rb in range(r0, r0 + 4):
            k = nb - 1 - rb
            p.pe(lambda e, o=ps[0:DH, (rb - r0) * 128:(rb - r0 + 1) * 128], a=W.ktok[:, k, :]:
                 e.matmul(o, lhsT=a, rhs=W.J[:], start=True, stop=True), [W.ktok, W.J], [ps])
        p.act(lambda e, o=W.kT[:, r0 * 128:(r0 + 4) * 128], a=ps[0:DH, :]: e.copy(out=o, in_=a), [ps], [W.kT])
    for r0 in range(0, nb, 8):
        ps = p.ps(2 + r0 // 8 % 2, F32)
        for rb in range(r0, r0 + 8):
            k = nb - 1 - rb
            p.pe(lambda e, o=ps[:, (rb - r0) * DH:(rb - r0 + 1) * DH], a=W.vnat[:, k, :]:
                 e.matmul(o, lhsT=W.J[:], rhs=a, start=True, stop=True), [W.vnat, W.J], [ps])
        p.dve(lambda e, o=W.v[:, r0:r0 + 8, 0:DH], a=ps[:].rearrange("p (k d) -> p k d", d=DH): e.tensor_copy(out=o, in_=a),
              [ps], [W.v])


def sb_job(p, cx, W, q_ap, k_ap, v_ap, o_ap, ktok_ap=None):
    S = W.S
    if ktok_ap is None:
        load_job(p, cx, W, q_ap, k_ap, v_ap)
    else:
        load_job(p, cx, W, q_ap, None, None)
        sb_reverse(p, cx, W, ktok_ap, v_ap)
    nb = S // 128
    ovs = [a.rearrange("(g q p) d -> g p q d", p=128, q=4) for a in _L(o_ap)]
    NS = len(W.e)
    LAG = NS - 1
    units = []
    for i in range(nb):
        rb0 = nb - 1 - i
        nun = (128 * (i + 1) + 511) // 512
        for u in range(nun):
            c0 = rb0 * 128 + 512 * u
            units.append(dict(i=i, u=u, nun=nun, c0=c0, N=min(512, S - c0), idx=len(units)))

    def stage_a(t):
        i, u, c0, N, k = t["i"], t["u"], t["c0"], t["N"], t["idx"]
        z = p.ps(k % 2, F32)
        p.pe(lambda e, o=z[:, 0:N], a=W.qT[:, i * 128:(i + 1) * 128], b=W.kT[:, c0:c0 + N], u=u:
             e.matmul(o, lhsT=a, rhs=b, start=True, stop=(u != 0)), [W.qT, W.kT], [z])
        if u == 0:
            p.pe(lambda e, o=z[:, 0:128]: e.matmul(o, lhsT=cx.ident[:], rhs=W.sbmask[:], start=False, stop=True),
                 [cx.ident, W.sbmask], [z])
        e_sb, sp, cs, ecs, a = (W.e[k % NS], W.sp[k % NS], W.cs[k % NS], W.ecs[k % NS], W.a[k % NS])
        p.act(lambda e, o=e_sb[:, 0:N], i_=z[:, 0:N]: e.activation(out=o, in_=i_, func=AF.Exp), [z], [e_sb])
        p.act(lambda e, o=sp[:, 0:N], i_=e_sb[:, 0:N]: e.activation(out=o, in_=i_, func=AF.Ln, bias=1.0), [e_sb], [sp])
        if u == 0:
            init, rd = 0.0, [W.ones, sp]
        else:
            cprev = W.cs[(k - 1) % NS]
            pN = units[k - 1]["N"]
            init, rd = cprev[:, pN - 1:pN], [W.ones, sp, cprev]
        p.dve(lambda e, o=cs[:, 0:N], d0=W.ones[:, 0:N], d1=sp[:, 0:N], init=init: e.tensor_tensor_scan(
            out=o, data0=d0, data1=d1, initial=init, op0=ALU.mult, op1=ALU.add), rd, [cs])

    def stage_a2(t):
        N, k = t["N"], t["idx"]
        e_sb, cs, ecs, a = (W.e[k % NS], W.cs[k % NS], W.ecs[k % NS], W.a[k % NS])
        p.act(lambda e, o=ecs[:, 0:N], i_=cs[:, 0:N]: e.activation(out=o, in_=i_, func=AF.Exp, scale=-1.0), [cs], [ecs])
        p.dve(lambda e, o=a[:, 0:N], x=e_sb[:, 0:N], y=ecs[:, 0:N]: e.tensor_tensor(out=o, in0=x, in1=y, op=ALU.mult),
              [e_sb, ecs], [a])

    def stage_b(t):
        i, u, nun, c0, N, k = t["i"], t["u"], t["nun"], t["c0"], t["N"], t["idx"]
        nk = N // 128
        a = W.a[k % NS]
        at = W.at[k % 2]
        o_ps = p.ps(4 + i % 2, F32)
        atp = p.ps(2 + k % 2, BF16)
        for kb in range(nk):
            p.pe(lambda e, o=atp[:, kb * 128:(kb + 1) * 128], x=a[:, kb * 128:(kb + 1) * 128]:
                 e.transpose(out=o, in_=x, identity=cx.ident[:]), [a, cx.ident], [atp])
        p.act(lambda e, o=at[:, 0:N], x=atp[:, 0:N]: e.copy(out=o, in_=x), [atp], [at])
        for kb in range(nk):
            p.pe(lambda e, o=o_ps[:, 0:DH], x=at[:, kb * 128:(kb + 1) * 128], vv=W.v[:, c0 // 128 + kb, 0:DH],
                 st=(u == 0 and kb == 0), sp_=(u == nun - 1 and kb == nk - 1):
                 e.matmul(o, lhsT=x, rhs=vv, start=st, stop=sp_), [at, W.v], [o_ps])
        if u == nun - 1:
            osb = W.osb[(i // 4) % 2]
            p.act(lambda e, o=osb[:, i % 4, :], x=o_ps[:, 0:DH]: e.copy(out=o, in_=x), [o_ps], [osb])
            if i % 4 == 3:
                store_out(p, cx, W, osb, [a_[i // 4] for a_ in ovs])

    nu = len(units)
    for k in range(nu + LAG):
        if k < nu:
            stage_a(units[k])
        if 0 <= k - 1 < nu:
            stage_a2(units[k - 1])
        if 0 <= k - LAG < nu:
            stage_b(units[k - LAG])


def moba_pre(p, cx, W):
    S = W.S
    nkb = S // 256
    p.dve(lambda e: e.tensor_reduce(out=W.km[:, 0:nkb], in_=W.kT[:].rearrange("p (n s) -> p n s", s=256),
                                    axis=AX.X, op=ALU.add), [W.kT], [W.km])
    p.dve(lambda e: e.tensor_scalar(out=W.km[:, 0:nkb], in0=W.km[:, 0:nkb], scalar1=1.0 / 256, scalar2=None, op0=ALU.mult),
          [W.km], [W.km])
    p.dve(lambda e: e.tensor_copy(out=W.kmh[:, 0:nkb], in_=W.km[:, 0:nkb]), [W.km], [W.kmh])
    p.dve(lambda e: e.tensor_tensor(out=W.kml[:, 0:nkb], in0=W.km[:, 0:nkb], in1=W.kmh[:, 0:nkb], op=ALU.subtract),
          [W.km, W.kmh], [W.kml])
    p.dve(lambda e: e.memset(W.gm[:], -1e30), [], [W.gm])
    for i in range(S // 128):
        own = i // 2
        bsel = W.bsel[i % 2]
        if own == 0:
            p.dve(lambda e, b=bsel: e.memset(b[:], 0.0), [], [bsel])
        else:
            gp = p.ps(4 + i % 2, F32)
            p.pe(lambda e, o=gp[:, 0:nkb], a=W.qT[:, i * 128:(i + 1) * 128]: e.matmul(o, lhsT=a, rhs=W.kmh[:, 0:nkb], start=True, stop=False),
                 [W.qT, W.kmh], [gp])
            p.pe(lambda e, o=gp[:, 0:nkb], a=W.qT[:, i * 128:(i + 1) * 128]: e.matmul(o, lhsT=a, rhs=W.kml[:, 0:nkb], start=False, stop=True),
                 [W.qT, W.kml], [gp])
            p.dve(lambda e, o=W.gm[:, 0:own], a=gp[:, 0:own]: e.tensor_copy(out=o, in_=a), [gp], [W.gm])
            top = W.top8[i % 2]
            sel = W.sel[i % 2]
            p.dve(lambda e, t=top: e.max(out=t[:], in_=W.gm[:]), [W.gm], [top])
            p.dve(lambda e, t=top: e.tensor_scalar(out=t[:, 3:4], in0=t[:, 2:3], scalar1=-1e29, scalar2=None, op0=ALU.max),
                  [top], [top])
            p.dve(lambda e, s=sel, t=top: e.tensor_scalar(out=s[:], in0=W.gm[:], scalar1=t[:, 3:4], scalar2=None, op0=ALU.is_ge),
                  [W.gm, top], [sel])
            p.dve(lambda e, s=sel, b=bsel: e.tensor_scalar(out=b[:], in0=s[:], scalar1=-NEG, scalar2=NEG, op0=ALU.mult, op1=ALU.add),
                  [sel], [bsel])
            p.dve(lambda e, b=bsel, own=own: e.memset(b[:, own:own + 1], 0.0), [], [bsel])
        tp = cx.tps[i % 2]
        p.pe(lambda e, o=tp[0:32, 0:128], b=bsel: e.transpose(out=o, in_=b[:], identity=cx.ident[:]), [bsel, cx.ident], [tp])
        p.act(lambda e, o=W.QA[:, i * 128:(i + 1) * 128], x=tp[0:32, 0:128]: e.copy(out=o, in_=x), [tp], [W.QA])


def softmax_job(p, cx, W, kind, q_ap, k_ap, v_ap, o_ap, fox_rows=None):
    S = W.S
    load_job(p, cx, W, q_ap, k_ap, v_ap)
    QA = KA = None
    if kind == "moba":
        moba_pre(p, cx, W)
        QA, KA = W.QA, W.onehot
    elif kind == "fox":
        p.dve(lambda e: e.memset(W.QA[:], 1.0), [], [W.QA])
        p.dve(lambda e: e.memset(W.KA[:], 1.0), [], [W.KA])
        p.dma(W.QA[0:3, :], fox_rows[0:3, :], [], [W.QA])
        p.dma(W.KA[3:6, :], fox_rows[3:6, :], [], [W.KA])
        QA, KA = W.QA, W.KA
    ovs = [a.rearrange("(g q p) d -> g p q d", p=128, q=4) for a in _L(o_ap)]
    units = []
    for g in range(S // 512):
        jlo = max(0, 4 * g - 16) if kind == "dil" else 0
        for j in range(jlo, 4 * g + 4):
            units.append(dict(g=g, j=j, jlo=jlo, idx=len(units)))
    NP = len(W.pt)

    def stage_a(t):
        g, j, k = t["g"], t["j"], t["idx"]
        r = max(0, j - 4 * g)
        c0 = 128 * r
        N = 512 - c0
        sps = p.ps(k % 2, F32)
        q0 = g * 512 + c0
        diag = (kind != "dil") and j >= 4 * g
        p.pe(lambda e, o=sps[:, 0:N], a=W.kT[:, j * 128:(j + 1) * 128], b=W.qT[:, q0:q0 + N], sp_=(QA is None and not diag):
             e.matmul(o, lhsT=a, rhs=b, start=True, stop=sp_), [W.kT, W.qT], [sps])
        if QA is not None:
            p.pe(lambda e, o=sps[:, 0:N], a=KA[:, j * 128:(j + 1) * 128], b=QA[:, q0:q0 + N], sp_=(not diag):
                 e.matmul(o, lhsT=a, rhs=b, start=False, stop=sp_), [KA, QA], [sps])
        if diag:
            p.pe(lambda e, o=sps[:, 0:128]: e.matmul(o, lhsT=cx.ident[:], rhs=W.tri[:], start=False, stop=True),
                 [cx.ident, W.tri], [sps])
        pt = W.pt[k % NP]
        p.act(lambda e, o=pt[:, 0:N], x=sps[:, 0:N]: e.activation(out=o, in_=x, func=AF.Exp), [sps], [pt])
        if kind == "dil":
            dm = W.dilm[:, 4 * g - j + 3, c0:512]
            p.dve(lambda e, o=pt[:, 0:N], m_=dm: e.tensor_tensor(out=o, in0=o, in1=m_, op=ALU.mult), [pt, W.dilm], [pt])

    def stage_b(t):
        g, j, jlo, k = t["g"], t["j"], t["jlo"], t["idx"]
        r = max(0, j - 4 * g)
        pt = W.pt[k % NP]
        accs = [p.ps(2 + qb, F32) for qb in range(4)]
        osb = W.osb[g % 2]
        for qb in range(r, 4):
            lc = (qb - r) * 128
            last = (j == 4 * g + qb)
            p.pe(lambda e, o=accs[qb][:, 0:65], x=pt[:, lc:lc + 128], vv=W.v[:, j, :], st=(j == jlo), sp_=last:
                 e.matmul(o, lhsT=x, rhs=vv, start=st, stop=sp_), [pt, W.v], [accs[qb]])
            if last:
                rv = W.rinv[qb]
                p.dve(lambda e, o=rv[:], x=accs[qb][:, 64:65]: e.reciprocal(out=o, in_=x), [accs[qb]], [rv])
                p.dve(lambda e, o=osb[:, qb, :], x=accs[qb][:, 0:DH], s_=rv[:, 0:1]: e.tensor_scalar(
                    out=o, in0=x, scalar1=s_, scalar2=None, op0=ALU.mult), [accs[qb], rv], [osb])
        if j == 4 * g + 3:
            store_out(p, cx, W, osb, [a_[g] for a_ in ovs])

    LAG = 2
    for k, t in enumerate(units):
        stage_a(t)
        if k >= LAG:
            stage_b(units[k - LAG])
    for t in units[max(0, len(units) - LAG):]:
        stage_b(t)


def fox_pre(p, cx, S, logf4, caug):
    m = p.mark()
    PW = min(2048, S)
    lf = [p.sb([4, PW], F32, f"fx_lf{i}") for i in range(2)]
    ones = p.sb([4, PW], F32, "fx_ones")
    c = [p.sb([4, PW], F32, f"fx_c{i}") for i in range(2)]
    r1 = p.sb([4, PW], F32, "fx_r1")
    r2 = p.sb([4, PW], F32, "fx_r2")
    rows = [[p.sb([4, PW], BF16, f"fx_row{k}_{i}") for k in range(6)] for i in range(2)]
    p.dve(lambda e: e.memset(ones[:], 1.0), [], [ones])
    prev = None
    for pc in range(S // PW):
        l = lf[pc % 2]
        cc = c[pc % 2]
        rw = rows[pc % 2]
        load_sel(p, cx, l, l[:], [a[:, pc * PW:(pc + 1) * PW] for a in _L(logf4)], (r1, r1[:]), npart=4)
        init = 0.0 if prev is None else prev[:, PW - 1:PW]
        rd = [ones, l] + ([prev] if prev is not None else [])
        p.dve(lambda e, o=cc[:], d1=l[:], init=init: e.tensor_tensor_scan(out=o, data0=ones[:], data1=d1, initial=init,
                                                                         op0=ALU.mult, op1=ALU.add), rd, [cc])
        prev = cc
        hi, mid, lo, nhi, nmid, nlo = rw
        p.dve(lambda e, o=hi[:], x=cc[:]: e.tensor_copy(out=o, in_=x), [cc], [hi])
        p.dve(lambda e, o=r1[:], x=cc[:], y=hi[:]: e.tensor_tensor(out=o, in0=x, in1=y, op=ALU.subtract), [cc, hi], [r1])
        p.dve(lambda e, o=mid[:]: e.tensor_copy(out=o, in_=r1[:]), [r1], [mid])
        p.dve(lambda e, o=r2[:], y=mid[:]: e.tensor_tensor(out=o, in0=r1[:], in1=y, op=ALU.subtract), [r1, mid], [r2])
        p.dve(lambda e, o=lo[:]: e.tensor_copy(out=o, in_=r2[:]), [r2], [lo])
        for src, dst in ((hi, nhi), (mid, nmid), (lo, nlo)):
            p.dve(lambda e, o=dst[:], x=src[:]: e.tensor_scalar(out=o, in0=x, scalar1=-1.0, scalar2=None, op0=ALU.mult),
                  [src], [dst])
        for k in range(6):
            p.dma(caug[:, k, pc * PW:(pc + 1) * PW], rw[k][:], [rw[k]], [])
    p.barrier()
    p.release(m)


def build_attn(S, layer, njobs=4, only=None):
    nc, stack, p = new_prog()
    with stack:
        dt = lambda n, sh, d=BF16, k="ExternalInput": nc.dram_tensor(n, sh, d, kind=k).ap()
        ident = dt("ident", [128, 128])
        cx = make_ctx(p, ident)
        consts = {"tri": dt("tri", [128, 128])}
        if layer == 0:
            consts["sbmask"] = dt("sbmask", [128, 128])
            consts["onehot"] = dt("onehot", [32, S])
        else:
            consts["dilm"] = dt("dilm", [20, 128, 512])
            logf4 = dt("logf4", [4, S], F32)
            caug = nc.dram_tensor("caug", [4, 6, S], BF16).ap()
            fox_pre(p, cx, S, logf4, caug)
        W = attn_setup(p, cx, S, layer, consts)
        dq, dk, dv = dt("dq", [njobs, DH, S]), dt("dk", [njobs, DH, S]), dt("dv", [njobs, S, DH])
        sq, sk, sv = dt("sq", [njobs, DH, S]), dt("sk", [njobs, DH, S]), dt("sv", [njobs, S, DH])
        od = dt("od", [njobs, S, DH], BF16, "ExternalOutput")
        os_ = dt("os", [njobs, S, DH], BF16, "ExternalOutput")
        for jb in range(njobs):
            if layer == 0:
                if only in (None, "d"):
                    sb_job(p, cx, W, dq[jb], dk[jb], dv[jb], od[jb])
                if only in (None, "s"):
                    softmax_job(p, cx, W, "moba", sq[jb], sk[jb], sv[jb], os_[jb])
            else:
                if only in (None, "d"):
                    softmax_job(p, cx, W, "fox", dq[jb], dk[jb], dv[jb], od[jb], fox_rows=caug[jb])
                if only in (None, "s"):
                    softmax_job(p, cx, W, "dil", sq[jb], sk[jb], sv[jb], os_[jb])
        p.barrier()
        p.emit()
    return nc


def attn_consts_np(S, layer):
    ii = np.arange(128)
    c = {"ident": np.eye(128, dtype=np.float32).astype(NPBF),
         "tri": np.where(ii[:, None] <= ii[None, :], 0.0, NEG).astype(np.float32).astype(NPBF)}
    if layer == 0:
        c["sbmask"] = np.where(ii[None, :] + ii[:, None] >= 128, 0.0, NEG).astype(np.float32).astype(NPBF)
        c["onehot"] = (np.arange(S)[None, :] // 256 == np.arange(32)[:, None]).astype(np.float32).astype(NPBF)
    else:
        dm = np.zeros((20, 128, 512), np.float32)
        ss = np.arange(128)[:, None]
        tt = np.arange(512)[None, :]
        for d in range(-3, 17):
            o = 128 * d + tt - ss
            m = np.zeros_like(o, dtype=np.float32)
            for w, r in ((128, 1), (512, 4), (2048, 16)):
                m += ((o >= 0) & (o <= w) & (o % r == 0)).astype(np.float32)
            dm[d + 3] = m
        c["dilm"] = dm.astype(NPBF)
    return c


def build_tok_launch(T, plan):
    nc, stack, p = new_prog()
    with stack:
        dt = lambda n, sh, d=F32, k="ExternalInput": nc.dram_tensor(n, sh, d, kind=k).ap()
        ident = dt("ident", [128, 128], BF16)
        cx = make_ctx(p, ident)
        cur = dt("x", [T, D])
        for si, st in enumerate(plan):
            last_x = not any(s_ in ("outproj", "ffn") for s_ in plan[si + 1:])
            if st in ("outproj", "ffn"):
                nxt = (dt(f"xo{si}", [T, D], F32, "ExternalOutput") if last_x
                       else nc.dram_tensor(f"xs{si}", [T, D], F32).ap())
            if st == "ffn":
                stage_ffn(p, cx, T, cur, nxt, dt(f"gain{si}", [D]), dt(f"wg{si}", [D, DFF]), dt(f"wu{si}", [D, DFF]),
                          dt(f"wd{si}", [DFF, D]))
                cur = nxt
            elif st == "outproj":
                stage_outproj(p, cx, T, cur, dt(f"attn{si}", [T, D], BF16), nxt, dt(f"wo{si}", [D, D]))
                cur = nxt
            else:
                layer = int(st[-1])
                NW = 3 * D + (8 if layer == 1 else 0)
                ng = 1 if layer == 0 else 2
                gq = [dt(f"gq{j}", [DH]) for j in range(ng)]
                gk = [dt(f"gk{j}", [DH]) for j in range(ng)]
                kw = {}
                if layer == 1:
                    kw = dict(bf=dt("bf", [8]), logf_out=dt("logf", [8, T], F32, "ExternalOutput"))
                stage_qkv(p, cx, T, layer, cur, dt(f"gain{si}", [D]), dt("w_in", [D, NW]), gq, gk,
                          dt("cos", [T, 32]), dt("sin", [T, 32]),
                          dt("qT", [NH, DH, T], BF16, "ExternalOutput"), dt("kT", [NH, DH, T], BF16, "ExternalOutput"),
                          dt("v", [T, D], BF16, "ExternalOutput"), **kw)
        p.emit()
    return nc


def _run(nc, in_maps):
    in_maps = [{k: np.ascontiguousarray(v) for k, v in m.items()} for m in in_maps]
    return run_bass_kernel_spmd(nc, in_maps, core_ids=list(range(len(in_maps)))).results


def kernel_unfused(x, norm_ffn1, ffn1_w_gate, ffn1_w_up, ffn1_w_down, norm_mix, w_in_ab, g_q_b, g_k_b,
           w_in_cd, b_f, g_q_c, g_k_c, g_q_d, g_k_d, w_out, norm_ffn2, ffn2_w_gate, ffn2_w_up,
           ffn2_w_down):
    A = lambda a: np.asarray(a, dtype=np.float32)
    x = A(x)
    B, S, _ = x.shape
    NC = 8
    T = B * S // NC
    half_of = lambda c: (c // 2, c % 2)
    ident = np.eye(128, dtype=np.float32).astype(NPBF)
    inv = (1.0 / (10000.0 ** (np.arange(0, DH, 2, dtype=np.float32) / DH))).astype(np.float32)
    ang = (np.arange(S, dtype=np.float32)[:, None] * inv[None, :]).astype(np.float32)
    cos, sin = np.cos(ang).astype(np.float32), np.sin(ang).astype(np.float32)

    def tok_shard(c, arr):
        b, h = half_of(c)
        return arr[b, h * T:(h + 1) * T]

    def ffn_w(si, norm, wg, wu, wd, l):
        return {f"gain{si}": A(norm[l]), f"wg{si}": A(wg[l]), f"wu{si}": A(wu[l]), f"wd{si}": A(wd[l])}

    def attn_inputs(layer, res):
        consts = attn_consts_np(S, layer)
        maps = []
        for c in range(NC):
            b, h = half_of(c)
            QT = np.concatenate([res[2 * b]["qT"], res[2 * b + 1]["qT"]], axis=2)
            KT = np.concatenate([res[2 * b]["kT"], res[2 * b + 1]["kT"]], axis=2)
            V = np.concatenate([res[2 * b]["v"], res[2 * b + 1]["v"]], axis=0)
            V = V.reshape(S, NH, DH).transpose(1, 0, 2)
            hd = slice(4 * h, 4 * h + 4)
            hs = slice(8 + 4 * h, 12 + 4 * h)
            m = dict(consts)
            if layer == 0:
                m.update(dq=QT[hd], dk=KT[hd][:, :, ::-1], dv=V[hd][:, ::-1, :])
            else:
                LF = np.concatenate([res[2 * b]["logf"], res[2 * b + 1]["logf"]], axis=1)
                m.update(dq=QT[hd], dk=KT[hd], dv=V[hd], logf4=LF[hd])
            m.update(sq=QT[hs], sk=KT[hs], sv=V[hs])
            maps.append(m)
        return maps

    def attn_gather(res):
        outs = []
        for b in range(B):
            full = np.empty((S, NH, DH), dtype=NPBF)
            for h in range(2):
                r = res[2 * b + h]
                full[:, 4 * h:4 * h + 4] = np.asarray(r["od"]).transpose(1, 0, 2)
                full[:, 8 + 4 * h:12 + 4 * h] = np.asarray(r["os"]).transpose(1, 0, 2)
            full = full.reshape(S, D)
            outs += [full[0:T], full[T:2 * T]]
        return outs

    ncA = build_tok_launch(T, ["ffn", "qkv0"])
    mA = []
    for c in range(NC):
        m = {"ident": ident, "x": tok_shard(c, x), "w_in": A(w_in_ab[0]), "gq0": A(g_q_b[0]), "gk0": A(g_k_b[0]),
             "gain1": A(norm_mix[0]), "cos": cos[(c % 2) * T:(c % 2 + 1) * T], "sin": sin[(c % 2) * T:(c % 2 + 1) * T]}
        m.update(ffn_w(0, norm_ffn1, ffn1_w_gate, ffn1_w_up, ffn1_w_down, 0))
        mA.append(m)
    rA = _run(ncA, mA)
    ncB = build_attn(S, 0)
    rB = _run(ncB, attn_inputs(0, rA))
    att0 = attn_gather(rB)
    ncC = build_tok_launch(T, ["outproj", "ffn", "ffn", "qkv1"])
    mC = []
    for c in range(NC):
        m = {"ident": ident, "x": rA[c]["xo0"], "attn0": att0[c], "wo0": A(w_out[0]),
             "w_in": A(w_in_cd[0]), "gq0": A(g_q_c[0]), "gk0": A(g_k_c[0]), "gq1": A(g_q_d[0]), "gk1": A(g_k_d[0]),
             "bf": A(b_f[0]), "gain3": A(norm_mix[1]),
             "cos": cos[(c % 2) * T:(c % 2 + 1) * T], "sin": sin[(c % 2) * T:(c % 2 + 1) * T]}
        m.update(ffn_w(1, norm_ffn2, ffn2_w_gate, ffn2_w_up, ffn2_w_down, 0))
        m.update(ffn_w(2, norm_ffn1, ffn1_w_gate, ffn1_w_up, ffn1_w_down, 1))
        mC.append(m)
    rC = _run(ncC, mC)
    ncD = build_attn(S, 1)
    rD = _run(ncD, attn_inputs(1, rC))
    att1 = attn_gather(rD)
    ncE = build_tok_launch(T, ["outproj", "ffn"])
    mE = []
    for c in range(NC):
        m = {"ident": ident, "x": rC[c]["xo2"], "attn0": att1[c], "wo0": A(w_out[1])}
        m.update(ffn_w(1, norm_ffn2, ffn2_w_gate, ffn2_w_up, ffn2_w_down, 1))
        mE.append(m)
    rE = _run(ncE, mE)
    out = np.empty((B, S, D), dtype=np.float32)
    for c in range(NC):
        b, h = half_of(c)
        out[b, h * T:(h + 1) * T] = rE[c]["xo1"]
    return out


RG_PAIRS = [[0, 1], [2, 3], [4, 5], [6, 7]]


class XBuf:
    def __init__(self, nc, name, nelem, dt, pattern, **axes):
        ce = min(nelem, 128 * 16384)
        self.nch = nelem // ce
        assert self.nch * ce == nelem and ce % 128 == 0
        self.i = nc.dram_tensor(name + "_i", [self.nch, 128, ce // 128], dt)
        self.o = nc.dram_tensor(name + "_o", [self.nch, 128, ce // 128], dt)
        self.vi = self.i.ap().rearrange("c p f -> (c p f)").rearrange(pattern, **axes)
        self.vo = self.o.ap().rearrange("c p f -> (c p f)").rearrange(pattern, **axes)

    def exchange(self, p):
        for c in range(self.nch):
            p.collective(self.i[c], self.o[c], RG_PAIRS)


def build_fused(T, S):
    nc, stack, p = new_prog()
    with stack:
        dt = lambda n, sh, d=F32, k="ExternalInput": nc.dram_tensor(n, sh, d, kind=k).ap()
        ident = dt("ident", [128, 128], BF16)
        cx = make_ctx(p, ident, dt("rank", [128, 2]))
        x = dt("x", [T, D])
        y = dt("y", [T, D], F32, "ExternalOutput")
        cos, sin = dt("cos", [T, 32]), dt("sin", [T, 32])
        consts0 = {"tri": dt("tri", [128, 128], BF16), "sbmask": dt("sbmask", [128, 128], BF16),
                   "onehot": dt("onehot", [32, S], BF16), "antiid": dt("antiid", [128, 128], BF16)}
        consts1 = {"tri": consts0["tri"], "dilm": dt("dilm", [20, 128, 512], BF16)}
        ffn = [dict(gain=dt(f"gain{i}", [D]), wg=dt(f"wg{i}", [D, DFF]), wu=dt(f"wu{i}", [D, DFF]), wd=dt(f"wd{i}", [DFF, D]))
               for i in range(4)]
        nmix = [dt(f"nmix{i}", [D]) for i in range(2)]
        w_in = [dt("w_in0", [D, 3 * D]), dt("w_in1", [D, 3 * D + 8])]
        wo = [dt(f"wo{i}", [D, D]) for i in range(2)]
        gqb, gkb = dt("gqb", [DH]), dt("gkb", [DH])
        gqc, gkc, gqd, gkd = dt("gqc", [DH]), dt("gkc", [DH]), dt("gqd", [DH]), dt("gkd", [DH])
        bf = dt("bf", [8])
        xs = [nc.dram_tensor(f"xs{i}", [T, D], F32).ap() for i in range(5)]
        EQ = XBuf(nc, "eq", NH * DH * S, BF16, "(n d h t) -> n d h t", n=NH, d=DH, h=2)
        EK = XBuf(nc, "ek", NH * DH * S, BF16, "(n d h t) -> n d h t", n=NH, d=DH, h=2)
        EV = XBuf(nc, "ev", S * D, BF16, "(h n p k d) -> h n p k d", h=2, n=NH, p=128, d=DH)
        EKT = XBuf(nc, "ekt", S * 512, BF16, "(h n p k d) -> h n p k d", h=2, n=8, p=128, d=DH)
        ELF = XBuf(nc, "elf", 8 * S, F32, "(n h t) -> n h t", n=8, h=2)
        EA = XBuf(nc, "ea", S * D, BF16, "(h t c) -> h t c", h=2, c=D)
        caug = nc.dram_tensor("caug", [4, 6, S], BF16).ap()

        def exchange(bufs):
            for b_ in bufs:
                b_.exchange(p)
            p.barrier()

        def head_q(E, hh):
            return E.vo[hh].rearrange("d h t -> d (h t)")

        def cols(E, hh, out=False):
            v = (E.vi if out else E.vo).rearrange("h t c -> (h t) c")
            return v[:, hh * DH:(hh + 1) * DH]

        def pmv(E, hh):
            return E.vo[:, hh].rearrange("h p k d -> p h k d")

        def attention(layer):
            m = p.mark()
            if layer == 1:
                lf = ELF.vo.rearrange("n h t -> n (h t)")
                fox_pre(p, cx, S, [lf[0:4], lf[4:8]], caug)
            W = attn_setup(p, cx, S, layer, consts0 if layer == 0 else consts1, fused=True)
            for j in range(4):
                d0, d1, s0, s1 = j, 4 + j, 8 + j, 12 + j
                if layer == 0:
                    sb_job(p, cx, W, [head_q(EQ, d0), head_q(EQ, d1)], None, [pmv(EV, d0), pmv(EV, d1)],
                           [cols(EA, d0, True), cols(EA, d1, True)], ktok_ap=[pmv(EKT, d0), pmv(EKT, d1)])
                    kind = "moba"
                else:
                    softmax_job(p, cx, W, "fox", [head_q(EQ, d0), head_q(EQ, d1)], [head_q(EK, d0), head_q(EK, d1)],
                                [pmv(EV, d0), pmv(EV, d1)], [cols(EA, d0, True), cols(EA, d1, True)], fox_rows=caug[j])
                    kind = "dil"
                softmax_job(p, cx, W, kind, [head_q(EQ, s0), head_q(EQ, s1)], [head_q(EK, s0), head_q(EK, s1)],
                            [pmv(EV, s0), pmv(EV, s1)], [cols(EA, s0, True), cols(EA, s1, True)])
            p.barrier()
            p.release(m)

        def qkv(layer, xin):
            kw = {"pm": True}
            if layer == 0:
                gq, gk = [gqb], [gkb]
                kw["ktok_out"] = [EKT.vi[0], EKT.vi[1]]
            else:
                gq, gk = [gqc, gqd], [gkc, gkd]
                kw.update(bf=bf, logf_out=[ELF.vi[:, 0, :], ELF.vi[:, 1, :]])
            stage_qkv(p, cx, T, layer, xin, nmix[layer], w_in[layer], gq, gk, cos, sin,
                      [EQ.vi[:, :, 0, :], EQ.vi[:, :, 1, :]], [EK.vi[:, :, 0, :], EK.vi[:, :, 1, :]],
                      [EV.vi[0], EV.vi[1]], **kw)

        stage_ffn(p, cx, T, x, xs[0], **ffn[0])
        qkv(0, xs[0])
        exchange([EQ, EK, EV, EKT])
        attention(0)
        exchange([EA])
        stage_outproj(p, cx, T, xs[0], [EA.vo[0], EA.vo[1]], xs[1], wo[0])
        stage_ffn(p, cx, T, xs[1], xs[2], **ffn[1])
        stage_ffn(p, cx, T, xs[2], xs[3], **ffn[2])
        qkv(1, xs[3])
        exchange([EQ, EK, EV, ELF])
        attention(1)
        exchange([EA])
        stage_outproj(p, cx, T, xs[3], [EA.vo[0], EA.vo[1]], xs[4], wo[1])
        stage_ffn(p, cx, T, xs[4], y, **ffn[3])
        p.emit()
    return nc


def kernel(x, norm_ffn1, ffn1_w_gate, ffn1_w_up, ffn1_w_down, norm_mix, w_in_ab, g_q_b, g_k_b,
           w_in_cd, b_f, g_q_c, g_k_c, g_q_d, g_k_d, w_out, norm_ffn2, ffn2_w_gate, ffn2_w_up,
           ffn2_w_down):
    A = lambda a: np.asarray(a, dtype=np.float32)
    x = A(x)
    B, S, _ = x.shape
    NC = 8
    T = B * S // NC
    inv = (1.0 / (10000.0 ** (np.arange(0, DH, 2, dtype=np.float32) / DH))).astype(np.float32)
    ang = (np.arange(S, dtype=np.float32)[:, None] * inv[None, :]).astype(np.float32)
    cos, sin = np.cos(ang).astype(np.float32), np.sin(ang).astype(np.float32)
    c0, c1 = attn_consts_np(S, 0), attn_consts_np(S, 1)
    shared = {"ident": c0["ident"], "tri": c0["tri"], "sbmask": c0["sbmask"], "onehot": c0["onehot"],
              "antiid": np.ascontiguousarray(np.eye(128, dtype=np.float32)[::-1]).astype(NPBF), "dilm": c1["dilm"],
              "nmix0": A(norm_mix[0]), "nmix1": A(norm_mix[1]), "w_in0": A(w_in_ab[0]), "w_in1": A(w_in_cd[0]),
              "wo0": A(w_out[0]), "wo1": A(w_out[1]), "gqb": A(g_q_b[0]), "gkb": A(g_k_b[0]),
              "gqc": A(g_q_c[0]), "gkc": A(g_k_c[0]), "gqd": A(g_q_d[0]), "gkd": A(g_k_d[0]), "bf": A(b_f[0])}
    for i, (nrm, wg, wu, wd, l) in enumerate([(norm_ffn1, ffn1_w_gate, ffn1_w_up, ffn1_w_down, 0),
                                               (norm_ffn2, ffn2_w_gate, ffn2_w_up, ffn2_w_down, 0),
                                               (norm_ffn1, ffn1_w_gate, ffn1_w_up, ffn1_w_down, 1),
                                               (norm_ffn2, ffn2_w_gate, ffn2_w_up, ffn2_w_down, 1)]):
        shared.update({f"gain{i}": A(nrm[l]), f"wg{i}": A(wg[l]), f"wu{i}": A(wu[l]), f"wd{i}": A(wd[l])})
    maps = []
    for c in range(NC):
        b, h = c // 2, c % 2
        m = dict(shared)
        rk = np.zeros((128, 2), np.float32)
        rk[:, h] = 1.0
        m.update(x=x[b, h * T:(h + 1) * T], rank=rk, cos=cos[h * T:(h + 1) * T], sin=sin[h * T:(h + 1) * T])
        maps.append(m)
    nc = build_fused(T, S)
    res = _run(nc, maps)
    out = np.empty((B, S, D), dtype=np.float32)
    for c in range(NC):
        b, h = c // 2, c % 2
        out[b, h * T:(h + 1) * T] = res[c]["y"]
    return out
```

```python
import numpy as np
import ml_dtypes
from contextlib import ExitStack
import concourse.bass as bass
import concourse.mybir as mybir
from concourse.bass_utils import run_bass_kernel_spmd

F32 = mybir.dt.float32
BF16 = mybir.dt.bfloat16
AF = mybir.ActivationFunctionType
ALU = mybir.AluOpType
AX = mybir.AxisListType
NPBF = ml_dtypes.bfloat16

D = 1024
DFF = 2816
NH = 16
DH = 64
S_FULL = 8192
B_FULL = 4
EPS = 1e-6
NEG = -30000.0
STW = 1540


class _State:
    __slots__ = ("last_w", "readers", "sem", "cnt", "sw")

    def __init__(self):
        self.last_w = None
        self.readers = []
        self.sem = None
        self.cnt = 0
        self.sw = False


class Buf:
    __slots__ = ("t", "st", "name")

    def __init__(self, t, name, st=None):
        self.t = t
        self.name = name
        self.st = st if st is not None else _State()

    def __getitem__(self, k):
        return self.t[k]

    last_w = property(lambda s: s.st.last_w, lambda s, v: setattr(s.st, "last_w", v))
    readers = property(lambda s: s.st.readers, lambda s, v: setattr(s.st, "readers", v))
    sem = property(lambda s: s.st.sem, lambda s, v: setattr(s.st, "sem", v))
    cnt = property(lambda s: s.st.cnt, lambda s, v: setattr(s.st, "cnt", v))


class Ins:
    __slots__ = ("eng", "fn", "deps", "needs_inc", "count", "epoch", "dma_tok", "idx", "inc")

    def __init__(self, eng, fn):
        self.eng = eng
        self.fn = fn
        self.deps = []
        self.needs_inc = False
        self.count = 0
        self.epoch = 0
        self.dma_tok = None
        self.idx = 0
        self.inc = 16


ENGS = ("pe", "act", "dve", "pool", "sp")
EPOCH = 20000


class Prog:
    ARENA_BYTES = 212000

    def __init__(self, nc, stack):
        self.nc = nc
        self.stack = stack
        self.ins = {e: [] for e in ENGS}
        self.nsb = 0
        self.arena = stack.enter_context(nc.sbuf_tensor("arena", [128, self.ARENA_BYTES // 2], BF16))
        self.off = 0
        self.live = []
        self.sem_pool = []
        self.sem_pool_sw = []
        self.nsem = 0
        self.cc_sem = None
        self.cc_cnt = 0
        self.banks = [stack.enter_context(nc.psum_tensor(f"bank{i}", [128, 512], F32)) for i in range(8)]
        self.bank_state = [_State() for _ in range(8)]

    def sb(self, shape, dt, name=None):
        self.nsb += 1
        name = name or f"sb{self.nsb}"
        esz = 4 if dt == F32 else 2
        n = 1
        for d in shape[1:]:
            n *= d
        nbytes = (n * esz + 63) // 64 * 64
        assert self.off + nbytes <= self.ARENA_BYTES, (name, self.off, nbytes)
        v = self.arena[0:shape[0], self.off // 2:(self.off + n * esz) // 2]
        if dt == F32:
            v = v.bitcast(F32)
        if len(shape) == 3:
            v = v.rearrange("p (a b) -> p a b", a=shape[1])
        self.off += nbytes
        b = Buf(v, name)
        self.live.append((self.off - nbytes, b))
        return b

    def mark(self):
        return self.off

    def release(self, m):
        keep = []
        for off, b in self.live:
            if off >= m:
                if b.sem is not None:
                    (self.sem_pool_sw if b.st.sw else self.sem_pool).append((b.sem, b.cnt))
                    b.st.sem = None
            else:
                keep.append((off, b))
        self.live = keep
        self.off = m

    def ps(self, bank, dt=F32, name=None):
        v = self.banks[bank][:, :]
        if dt == BF16:
            v = v.bitcast(BF16)
        return Buf(v, name or f"bank{bank}", self.bank_state[bank])

    def _add(self, eng, fn, reads, writes, dma=False):
        i = Ins(eng, fn)
        i.idx = len(self.ins[eng])
        deps = []
        for b in reads:
            if b.last_w is not None:
                deps.append(b.last_w)
        for b in writes:
            if b.last_w is not None:
                deps.append(b.last_w)
            for r in b.readers:
                deps.append(r)
        seen = set()
        for d in deps:
            if id(d) in seen or d is i:
                continue
            seen.add(id(d))
            if d.dma_tok is None and d.eng == eng:
                if eng == "pe":
                    continue
                if not any((b.last_w is d) for b in list(reads) + list(writes)):
                    continue
            d.needs_inc = True
            i.deps.append(d)
        if dma:
            key = None
            for b in list(writes) + list(reads):
                key = b
                break
            if key.sem is None:
                pool_ = self.sem_pool_sw if eng == "pool" else self.sem_pool
                key.st.sw = (eng == "pool")
                if pool_:
                    key.sem, key.cnt = pool_.pop()
                else:
                    self.nsem += 1
                    key.sem = self.stack.enter_context(self.nc.semaphore(f"d{self.nsem}_{key.name}"))
                    key.cnt = 0
            key.cnt += 16
            i.dma_tok = (key.sem, key.cnt)
        for b in reads:
            b.readers.append(i)
        for b in writes:
            b.last_w = i
            b.readers = []
        self.ins[eng].append(i)
        return i

    def pe(self, fn, reads, writes):
        return self._add("pe", fn, reads, writes)

    def act(self, fn, reads, writes):
        return self._add("act", fn, reads, writes)

    def dve(self, fn, reads, writes):
        return self._add("dve", fn, reads, writes)

    def pool(self, fn, reads, writes):
        return self._add("pool", fn, reads, writes)

    def dma(self, out_ap, in_ap, reads, writes, eng=None):
        if eng is None:
            eng = "sp"
        return self._add(eng, lambda e: e.dma_start(out=out_ap, in_=in_ap), reads, writes, dma=True)

    def collective(self, in_ap, out_ap, groups):
        if self.cc_sem is None:
            self.cc_sem = self.stack.enter_context(self.nc.semaphore("cc_sem"))
        i = Ins("pool", lambda e: e.collective_compute("AllReduce", ALU.add, replica_groups=groups,
                                                       ins=[in_ap], outs=[out_ap]))
        self.cc_cnt += 1
        i.dma_tok = (self.cc_sem, self.cc_cnt)
        i.inc = 1
        self.ins["pool"].append(i)
        return i

    def barrier(self):
        lasts = []
        for e in ENGS:
            if e == "sp":
                continue
            for i in reversed(self.ins[e]):
                if i.fn is not None and i.dma_tok is None:
                    lasts.append(i)
                    break
        dmas = {}
        for e in ENGS:
            for i in self.ins[e]:
                if i.dma_tok is not None:
                    dmas[id(i.dma_tok[0])] = i
        for e in ENGS:
            i = Ins(e, None)
            for d in lasts:
                if d.eng != e:
                    d.needs_inc = True
                    i.deps.append(d)
            for d in dmas.values():
                i.deps.append(d)
            self.ins[e].append(i)

    def emit(self):
        nc = self.nc
        sems = {}
        for e in ENGS:
            c = 0
            ep = 0
            for i in self.ins[e]:
                if i.dma_tok is None and i.needs_inc:
                    if c >= EPOCH:
                        ep += 1
                        c = 0
                    c += 1
                    i.count = c
                    i.epoch = ep
                    if (e, ep) not in sems:
                        sems[(e, ep)] = self.stack.enter_context(nc.semaphore(f"s_{e}{ep}"))
        block = self.stack.enter_context(nc.Block())
        engobj = {"pe": block.tensor, "act": block.scalar, "dve": block.vector,
                  "pool": block.gpsimd, "sp": block.sync}

        def make(e):
            def body(eng):
                waited = {}
                for i in self.ins[e]:
                    need = {}
                    for d in i.deps:
                        if d.dma_tok is not None:
                            sem, val = d.dma_tok
                        else:
                            sem, val = sems[(d.eng, d.epoch)], d.count
                        k = id(sem)
                        if waited.get(k, 0) >= val:
                            continue
                        if k not in need or need[k][1] < val:
                            need[k] = (sem, val)
                    for k, (sem, val) in need.items():
                        eng.wait_ge(sem, val)
                        waited[k] = val
                    if i.fn is None:
                        continue
                    r = i.fn(eng)
                    if i.dma_tok is not None:
                        r.then_inc(i.dma_tok[0], i.inc)
                    elif i.needs_inc:
                        r.then_inc(sems[(e, i.epoch)], 1)
            return body

        for e in ENGS:
            if self.ins[e]:
                engobj[e](make(e))


def new_prog():
    nc = bass.Bass("TRN2", target_bir_lowering=False)
    stack = ExitStack()
    return nc, stack, Prog(nc, stack)


class Ctx:
    pass


def load_const(p, ap, shape, dt, name):
    b = p.sb(shape, dt, name)
    p.dma(b[:], ap, [], [b])
    return b


def load_weight_bf16(p, w_ap, K, N, name, stage, eng_rr):
    kc = K // 128
    wv = w_ap.rearrange("(c p) n -> p c n", p=128)
    chunks = []
    for c in range(kc):
        wb = p.sb([128, N], BF16, f"{name}{c}")
        chunks.append(wb)
        p.dma(wb[:], wv[:, c, :], [], [wb], eng="pool")
    return chunks


def rms_to_T(p, cx, x_sb, g_bc, xnT, col0, nsub):
    for i in range(nsub):
        xs = x_sb[i]
        ss = cx.small[cx.rr % len(cx.small)]
        cx.rr += 1
        junk = cx.junk
        p.act(lambda e, o=junk[:], a=xs[:], s=ss[:, 0:1]: e.activation(out=o, in_=a, func=AF.Square, accum_out=s),
              [xs], [junk, ss])
        p.act(lambda e, s=ss: e.activation(out=s[:, 1:2], in_=s[:, 0:1], func=AF.Sqrt, scale=1.0 / D, bias=cx.eps[:, 0:1]),
              [ss, cx.eps], [ss])
        p.dve(lambda e, s=ss: e.reciprocal(out=s[:, 2:3], in_=s[:, 1:2]), [ss], [ss])
        xn = cx.xn[cx.rr % len(cx.xn)]
        p.dve(lambda e, o=xn[:], a=xs[:], s=ss[:, 2:3], g=g_bc[:]: e.scalar_tensor_tensor(
            out=o, in0=a, scalar=s, in1=g, op0=ALU.mult, op1=ALU.mult), [xs, ss, g_bc], [xn])
        transpose_to(p, cx, xn, xnT, 0, col0 + 128 * i, D // 128)


def transpose_to(p, cx, src, dstT, c0, col, nchunk, flat=None):
    tp = cx.tps[cx.rrt % len(cx.tps)]
    cx.rrt += 1
    sv = flat if flat is not None else src[:]
    for c in range(nchunk):
        p.pe(lambda e, o=tp[:, c * 128:(c + 1) * 128], a=sv[:, c * 128:(c + 1) * 128], idn=cx.ident[:]:
             e.transpose(out=o, in_=a, identity=idn), [src, cx.ident], [tp])
    o = dstT[:, c0:c0 + nchunk, col:col + 128]
    i_ = tp[:, 0:nchunk * 128].rearrange("p (c t) -> p c t", c=nchunk)
    if cx.rrt % 2 == 0:
        p.act(lambda e, o=o, i_=i_: e.copy(out=o, in_=i_), [tp], [dstT])
    else:
        p.dve(lambda e, o=o, i_=i_: e.tensor_copy(out=o, in_=i_), [tp], [dstT])


def emit_out(p, cx, src_buf, src_ap, dsts, tmps, npart=128):
    if len(dsts) == 1:
        p.dma(dsts[0], src_ap, [src_buf], [])
        return
    for s_ in range(2):
        t = tmps[s_]
        p.act(lambda e, o=t[:], a=src_ap, m=cx.rk[0:npart, s_:s_ + 1]: e.mul(out=o, in_=a, mul=m), [src_buf, cx.rk], [t])
        p.dma(dsts[s_], t[:], [t], [])


def load_sel(p, cx, dst_buf, dst_ap, srcs, tmp, npart=128):
    if len(srcs) == 1:
        p.dma(dst_ap, srcs[0], [], [dst_buf])
        return
    tb, ta = tmp
    if len(dst_ap.shape) == 4:
        for h_ in range(dst_ap.shape[1]):
            p.dma(dst_ap[:, h_], srcs[0][:, h_], [], [dst_buf])
    else:
        p.dma(dst_ap, srcs[0], [], [dst_buf])
    p.dma(ta, srcs[1], [], [tb])
    p.act(lambda e, o=dst_ap, m=cx.rk[0:npart, 0:1]: e.mul(out=o, in_=o, mul=m), [dst_buf, cx.rk], [dst_buf])
    p.dve(lambda e, o=dst_ap, b=ta, m=cx.rk[0:npart, 1:2]: e.scalar_tensor_tensor(
        out=o, in0=b, scalar=m, in1=o, op0=ALU.mult, op1=ALU.add), [tb, cx.rk, dst_buf], [dst_buf])


def make_ctx(p, ident_ap, rank_ap=None):
    cx = Ctx()
    cx.rr = 0
    cx.rrt = 0
    cx.ident = load_const(p, ident_ap, [128, 128], BF16, "ident")
    cx.eps = p.sb([128, 1], F32, "eps")
    p.dve(lambda e: e.memset(cx.eps[:], EPS), [], [cx.eps])
    cx.small = [p.sb([128, 4], F32, f"small{i}") for i in range(4)]
    cx.small8 = [p.sb([128, 24], F32, f"small8_{i}") for i in range(4)]
    cx.junk = p.sb([128, D], BF16, "junk")
    cx.xn = [p.sb([128, D], BF16, f"xn{i}") for i in range(1)]
    cx.tps = [p.ps(6 + i, BF16, f"tps{i}") for i in range(2)]
    cx.mm = [p.ps(i, F32, f"mm{i}") for i in range(6)]
    cx.stage = [p.sb([128, STW], F32, f"stage{i}") for i in range(2)]
    cx.rk = None
    if rank_ap is not None:
        cx.rk = load_const(p, rank_ap, [128, 2], F32, "rank")
    return cx


TG = 512
NSUB = TG // 128
KC = D // 128


def stage_ffn(p, cx, T, x_in, x_out, gain, wg, wu, wd):
    m = p.mark()
    rr = [0]
    mm_ps = cx.mm
    g_bc = p.sb([128, D], F32, "ffn_g")
    p.dma(g_bc[:], gain.partition_broadcast(128), [], [g_bc])
    Wg = load_weight_bf16(p, wg, D, DFF, "Wg", cx.stage, rr)
    Wu = load_weight_bf16(p, wu, D, DFF, "Wu", cx.stage, rr)
    Wd = load_weight_bf16(p, wd, DFF, D, "Wd", cx.stage, rr)
    nf = DFF // 128
    xpool = [p.sb([128, D], F32, f"ffn_x{i}") for i in range(5)]
    xnT = p.sb([128, KC, TG], BF16, "ffn_xnT")
    hT = p.sb([128, nf, TG], BF16, "ffn_hT")
    sg = [p.sb([128, TG], F32, f"ffn_sg{i}") for i in range(1)]
    xv = x_in.rearrange("(n p) d -> n p d", p=128)
    ov = x_out.rearrange("(n p) d -> n p d", p=128)
    for g in range(T // TG):
        xs = [xpool[(g * NSUB + i) % len(xpool)] for i in range(NSUB)]
        for i in range(NSUB):
            p.dma(xs[i][:], xv[g * NSUB + i], [], [xs[i]])
        rms_to_T(p, cx, xs, g_bc, xnT, 0, NSUB)
        for f in range(nf):
            pg = mm_ps[(2 * f) % len(mm_ps)]
            pu = mm_ps[(2 * f + 1) % len(mm_ps)]
            for k in range(KC):
                p.pe(lambda e, o=pg[:], w=Wg[k][:, f * 128:(f + 1) * 128], a=xnT[:, k, :], k=k:
                     e.matmul(o, lhsT=w, rhs=a, start=(k == 0), stop=(k == KC - 1)), [Wg[k], xnT], [pg])
            for k in range(KC):
                p.pe(lambda e, o=pu[:], w=Wu[k][:, f * 128:(f + 1) * 128], a=xnT[:, k, :], k=k:
                     e.matmul(o, lhsT=w, rhs=a, start=(k == 0), stop=(k == KC - 1)), [Wu[k], xnT], [pu])
            s = sg[f % len(sg)]
            p.act(lambda e, o=s[:], a=pg[:]: e.activation(out=o, in_=a, func=AF.Silu), [pg], [s])
            p.dve(lambda e, o=hT[:, f, :], a=s[:], b=pu[:]: e.tensor_tensor(out=o, in0=a, in1=b, op=ALU.mult),
                  [s, pu], [hT])
        for i in range(NSUB):
            for half in range(2):
                py = mm_ps[(2 * i + half) % len(mm_ps)]
                for f in range(nf):
                    p.pe(lambda e, o=py[:], a=hT[:, f, i * 128:(i + 1) * 128], w=Wd[f][:, half * 512:(half + 1) * 512], f=f:
                         e.matmul(o, lhsT=a, rhs=w, start=(f == 0), stop=(f == nf - 1)), [hT, Wd[f]], [py])
                p.dve(lambda e, o=xs[i][:, half * 512:(half + 1) * 512], a=py[:]: e.scalar_tensor_tensor(
                    out=o, in0=a, scalar=0.5, in1=o, op0=ALU.mult, op1=ALU.add), [py, xs[i]], [xs[i]])
            p.dma(ov[g * NSUB + i], xs[i][:], [xs[i]], [])
    p.barrier()
    p.release(m)


def stage_outproj(p, cx, T, x_in, attn_in, x_out, wo):
    attn_list = list(attn_in) if isinstance(attn_in, (list, tuple)) else [attn_in]
    m = p.mark()
    rr = [0]
    mm_ps = cx.mm
    Wo = load_weight_bf16(p, wo, D, D, "Wo", cx.stage, rr)
    xpool = [p.sb([128, D], F32, f"op_x{i}") for i in range(6)]
    apool = [p.sb([128, D], BF16, f"op_a{i}") for i in range(6)]
    aT = p.sb([128, KC, TG], BF16, "op_aT")
    xv = x_in.rearrange("(n p) d -> n p d", p=128)
    avs = [a.rearrange("(n p) d -> n p d", p=128) for a in attn_list]
    acand = [p.sb([128, D], BF16, f"op_c{i}") for i in range(2)] if len(avs) == 2 else None
    ov = x_out.rearrange("(n p) d -> n p d", p=128)
    for g in range(T // TG):
        xs = [xpool[(g * NSUB + i) % len(xpool)] for i in range(NSUB)]
        as_ = [apool[(g * NSUB + i) % len(apool)] for i in range(NSUB)]
        for i in range(NSUB):
            p.dma(xs[i][:], xv[g * NSUB + i], [], [xs[i]])
            load_sel(p, cx, as_[i], as_[i][:], [a[g * NSUB + i] for a in avs],
                     (acand[i % 2], acand[i % 2][:]) if acand else None)
        for i in range(NSUB):
            transpose_to(p, cx, as_[i], aT, 0, 128 * i, KC)
        for i in range(NSUB):
            for half in range(2):
                py = mm_ps[(2 * i + half) % len(mm_ps)]
                for k in range(KC):
                    p.pe(lambda e, o=py[:], a=aT[:, k, i * 128:(i + 1) * 128], w=Wo[k][:, half * 512:(half + 1) * 512], k=k:
                         e.matmul(o, lhsT=a, rhs=w, start=(k == 0), stop=(k == KC - 1)), [aT, Wo[k]], [py])
                p.dve(lambda e, o=xs[i][:, half * 512:(half + 1) * 512], a=py[:]: e.tensor_tensor(
                    out=o, in0=a, in1=o, op=ALU.add), [py, xs[i]], [xs[i]])
            p.dma(ov[g * NSUB + i], xs[i][:], [xs[i]], [])
    p.barrier()
    p.release(m)


def stage_qkv(p, cx, T, layer, x_in, gain, w_in, gq, gk, cos, sin, qT_out, kT_out, v_out, bf=None, logf_out=None,
              ktok_out=None, pm=False):
    m = p.mark()
    rr = [0]
    mm_ps = cx.mm
    NW = 3 * D + (8 if layer == 1 else 0)
    g_bc = p.sb([128, D], F32, "qkv_g")
    p.dma(g_bc[:], gain.partition_broadcast(128), [], [g_bc])
    Win = load_weight_bf16(p, w_in, D, NW, "Win", cx.stage, rr)
    gains = {}
    for nm, ap, sc in [("q", gq, 0.125), ("k", gk, 1.0)]:
        for j, a in enumerate(ap):
            t = p.sb([128, DH], F32, f"gain_{nm}{j}")
            p.dma(t[:], a.partition_broadcast(128), [], [t])
            if sc != 1.0:
                p.dve(lambda e, t=t, sc=sc: e.tensor_scalar(out=t[:], in0=t[:], scalar1=sc, scalar2=None, op0=ALU.mult),
                      [t], [t])
            gains[(nm, j)] = t
    if layer == 1:
        bf_bc = p.sb([128, 8], F32, "bf_bc")
        p.dma(bf_bc[:], bf.partition_broadcast(128), [], [bf_bc])
        lf = [p.sb([128, 32], F32, f"lf{i}") for i in range(2)]
    cs_pool = [(p.sb([128, 32], F32, f"cos{i}"), p.sb([128, 32], F32, f"sin{i}")) for i in range(2)]
    xpool = [p.sb([128, D], F32, f"qkv_x{i}") for i in range(6)]
    xnT = p.sb([128, KC, TG], BF16, "qkv_xnT")
    qT = p.sb([128, 8, TG], BF16, "qkv_qT")
    kT = p.sb([128, 8, TG], BF16, "qkv_kT")
    vb = [p.sb([128, D], BF16, f"qkv_v{i}") for i in range(2)]
    sq_l = [p.sb([128, 8, DH], F32, f"qkv_sq{i}") for i in range(2)]
    qn_l = [p.sb([128, 8, DH], F32, f"qkv_qn{i}") for i in range(2)]
    tmp_l = [[p.sb([128, 8, 32], F32, f"qkv_t{j}_{i}") for i in range(4)] for j in range(2)]
    qb = [p.sb([128, 8, DH], BF16, f"qkv_qb{i}") for i in range(4)]
    L = lambda a: list(a) if isinstance(a, (list, tuple)) else [a]
    xv = x_in.rearrange("(n p) d -> n p d", p=128)
    if pm:
        vvs = [[a[:, :, n_, :].rearrange("n p d -> p n d") for n_ in range(T // 128)] for a in L(v_out)]
    else:
        vvs = [a.rearrange("(n p) d -> n p d", p=128) for a in L(v_out)]
    cv = cos.rearrange("(n p) d -> n p d", p=128)
    sv = sin.rearrange("(n p) d -> n p d", p=128)
    qTvs = [a.rearrange("(hp two) d t -> (two d) hp t", two=2) for a in L(qT_out)]
    kTvs = [a.rearrange("(hp two) d t -> (two d) hp t", two=2) for a in L(kT_out)]
    two = len(qTvs) == 2
    tq = [p.sb([128, 8, TG], BF16, f"qkv_tq{i}") for i in range(2)] if two else None
    tv = [p.sb([128, NH, DH], BF16, f"qkv_tv{i}") for i in range(2)] if two else None
    tk = [p.sb([128, 8, DH], BF16, f"qkv_tk{i}") for i in range(2)] if two else None
    tl = [p.sb([128, 8], F32, f"qkv_tl{i}") for i in range(2)] if two else None
    if ktok_out is None:
        ktvs = None
    elif pm:
        ktvs = [[a[:, :, n_, :].rearrange("n p d -> p n d") for n_ in range(T // 128)] for a in L(ktok_out)]
    else:
        ktvs = [a.rearrange("(n p) d -> n p d", p=128) for a in L(ktok_out)]
    if layer == 1:
        lvs = L(logf_out)
        lfT = [p.sb([8, TG], F32, f"qkv_lfT{i}") for i in range(2)]
        tlT = [p.sb([8, TG], F32, f"qkv_tlT{i}") for i in range(2)] if two else None
        ident32 = p.sb([128, 128], F32, "ident32")
        p.dve(lambda e: e.tensor_copy(out=ident32[:], in_=cx.ident[:]), [cx.ident], [ident32])
    nb = 0
    nqk = 0
    pending = []
    for g in range(T // TG):
        xs = [xpool[(g * NSUB + i) % len(xpool)] for i in range(NSUB)]
        for i in range(NSUB):
            p.dma(xs[i][:], xv[g * NSUB + i], [], [xs[i]])
        rms_to_T(p, cx, xs, g_bc, xnT, 0, NSUB)
        for i in range(NSUB):
            n = g * NSUB + i
            cosb, sinb = cs_pool[n % 2]
            p.dma(cosb[:], cv[n], [], [cosb])
            p.dma(sinb[:], sv[n], [], [sinb])
            vbuf = vb[n % 2]
            for cg in range(6):
                ps = mm_ps[(nb) % len(mm_ps)]
                nb += 1
                for k in range(KC):
                    p.pe(lambda e, o=ps[:], a=xnT[:, k, i * 128:(i + 1) * 128], w=Win[k][:, cg * 512:(cg + 1) * 512], k=k:
                         e.matmul(o, lhsT=a, rhs=w, start=(k == 0), stop=(k == KC - 1)), [xnT, Win[k]], [ps])
                if cg >= 4:
                    p.act(lambda e, o=vbuf[:, (cg - 4) * 512:(cg - 3) * 512], a=ps[:]: e.copy(out=o, in_=a), [ps], [vbuf])
                    continue
                isq = cg < 2
                hg = cg % 2
                normed = (layer == 1) or (hg == 1)
                roped = (hg == 1)
                qbuf = qb[nqk % 4]
                sq, qn, tmp = sq_l[nqk % 2], qn_l[nqk % 2], tmp_l[nqk % 2]
                nqk += 1
                ps3 = ps[:].rearrange("p (h d) -> p h d", h=8)
                if not normed:
                    p.act(lambda e, o=qbuf[:], a=ps3, sc=(0.125 if isq else 1.0): e.mul(out=o, in_=a, mul=sc), [ps], [qbuf])
                else:
                    gj = 0 if layer == 0 else hg
                    gt = gains[("q" if isq else "k", gj)]
                    ss = cx.small8[cx.rr % len(cx.small8)]
                    cx.rr += 1
                    p.act(lambda e, o=sq[:], a=ps3: e.activation(out=o, in_=a, func=AF.Square), [ps], [sq])
                    p.dve(lambda e, o=ss[:, 0:8], a=sq[:]: e.tensor_reduce(out=o, in_=a, axis=AX.X, op=ALU.add), [sq], [ss])
                    p.act(lambda e, s=ss: e.activation(out=s[:, 8:16], in_=s[:, 0:8], func=AF.Sqrt, scale=1.0 / DH,
                                                       bias=cx.eps[:, 0:1]), [ss, cx.eps], [ss])
                    p.dve(lambda e, s=ss: e.reciprocal(out=s[:, 16:24], in_=s[:, 8:16]), [ss], [ss])
                    p.dve(lambda e, o=qn[:], a=ps3, s=ss: e.tensor_tensor(
                        out=o, in0=a, in1=s[:, 16:24].unsqueeze(2).to_broadcast([128, 8, DH]), op=ALU.mult), [ps, ss], [qn])
                    gbc = gt[:].unsqueeze(1).to_broadcast([128, 8, DH])
                    if not roped:
                        p.dve(lambda e, o=qbuf[:], a=qn[:], g_=gbc: e.tensor_tensor(out=o, in0=a, in1=g_, op=ALU.mult),
                              [qn, gt], [qbuf])
                    else:
                        p.dve(lambda e, o=qn[:], a=qn[:], g_=gbc: e.tensor_tensor(out=o, in0=a, in1=g_, op=ALU.mult),
                              [qn, gt], [qn])
                        cb = cosb[:].unsqueeze(1).to_broadcast([128, 8, 32])
                        sb_ = sinb[:].unsqueeze(1).to_broadcast([128, 8, 32])
                        x1 = qn[:, :, 0:32]
                        x2 = qn[:, :, 32:64]
                        t0, t1, t2, t3 = tmp
                        p.pool(lambda e, o=t0[:], a=x1, b=cb: e.tensor_tensor(out=o, in0=a, in1=b, op=ALU.mult), [qn, cosb], [t0])
                        p.pool(lambda e, o=t1[:], a=x2, b=sb_: e.tensor_tensor(out=o, in0=a, in1=b, op=ALU.mult), [qn, sinb], [t1])
                        p.dve(lambda e, o=t2[:], a=x2, b=cb: e.tensor_tensor(out=o, in0=a, in1=b, op=ALU.mult), [qn, cosb], [t2])
                        p.dve(lambda e, o=t3[:], a=x1, b=sb_: e.tensor_tensor(out=o, in0=a, in1=b, op=ALU.mult), [qn, sinb], [t3])
                        p.dve(lambda e, o=qbuf[:, :, 0:32], a=t0[:], b=t1[:]: e.tensor_tensor(out=o, in0=a, in1=b, op=ALU.subtract),
                              [t0, t1], [qbuf])
                        p.dve(lambda e, o=qbuf[:, :, 32:64], a=t2[:], b=t3[:]: e.tensor_tensor(out=o, in0=a, in1=b, op=ALU.add),
                              [t2, t3], [qbuf])
                def fin(qbuf=qbuf, isq=isq, hg=hg, i=i, n=n):
                    transpose_to(p, cx, qbuf, qT if isq else kT, hg * 4, 128 * i, 4,
                                 flat=qbuf.t.rearrange("p h d -> p (h d)"))
                    if ktvs is not None and (not isq) and hg == 0:
                        emit_out(p, cx, qbuf, qbuf[:] if pm else qbuf.t.rearrange("p h d -> p (h d)"), [a[n] for a in ktvs], tk)
                pending.append(fin)
                while len(pending) > 2:
                    pending.pop(0)()
            if layer == 1:
                ps = mm_ps[(nb) % len(mm_ps)]
                nb += 1
                for k in range(KC):
                    p.pe(lambda e, o=ps[:, 0:8], a=xnT[:, k, i * 128:(i + 1) * 128], w=Win[k][:, 3 * D:3 * D + 8], k=k:
                         e.matmul(o, lhsT=a, rhs=w, start=(k == 0), stop=(k == KC - 1)), [xnT, Win[k]], [ps])
                l = lf[n % 2]
                p.dve(lambda e, o=l[:, 0:8], a=ps[:, 0:8], b=bf_bc[:]: e.tensor_tensor(out=o, in0=a, in1=b, op=ALU.add),
                      [ps, bf_bc], [l])
                p.act(lambda e, l=l: e.activation(out=l[:, 8:16], in_=l[:, 0:8], func=AF.Exp, scale=-1.0), [l], [l])
                p.act(lambda e, l=l: e.activation(out=l[:, 16:24], in_=l[:, 8:16], func=AF.Ln, bias=1.0), [l], [l])
                p.dve(lambda e, l=l: e.tensor_scalar(out=l[:, 24:32], in0=l[:, 16:24], scalar1=-1.0, scalar2=None, op0=ALU.mult),
                      [l], [l])
                pst = mm_ps[(nb) % len(mm_ps)]
                nb += 1
                p.pe(lambda e, o=pst[0:8, 0:128], a=l[:, 24:32]: e.transpose(out=o, in_=a, identity=ident32[:]),
                     [l, ident32], [pst])
                lt = lfT[g % 2]
                p.act(lambda e, o=lt[:, i * 128:(i + 1) * 128], a=pst[0:8, 0:128]: e.copy(out=o, in_=a), [pst], [lt])
                if i == NSUB - 1:
                    emit_out(p, cx, lt, lt[:], [a[:, g * TG:(g + 1) * TG] for a in lvs], tlT, npart=8)
            emit_out(p, cx, vbuf, vbuf[:].rearrange("p (h d) -> p h d", h=NH) if pm else vbuf[:], [a[n] for a in vvs], tv)
        while pending:
            pending.pop(0)()
        emit_out(p, cx, qT, qT[:], [a[:, :, g * TG:(g + 1) * TG] for a in qTvs], tq)
        emit_out(p, cx, kT, kT[:], [a[:, :, g * TG:(g + 1) * TG] for a in kTvs], tq)
    p.barrier()
    p.release(m)


def build_test_tok(T, layer, which):
    nc, stack, p = new_prog()
    with stack:
        dt = lambda n, sh, d=F32, k="ExternalInput": nc.dram_tensor(n, sh, d, kind=k).ap()
        ident = dt("ident", [128, 128], BF16)
        cx = make_ctx(p, ident)
        x = dt("x", [T, D])
        if which == "ffn":
            y = dt("y", [T, D], F32, "ExternalOutput")
            stage_ffn(p, cx, T, x, y, dt("gain", [D]), dt("wg", [D, DFF]), dt("wu", [D, DFF]), dt("wd", [DFF, D]))
        elif which == "outproj":
            y = dt("y", [T, D], F32, "ExternalOutput")
            stage_outproj(p, cx, T, x, dt("attn", [T, D], BF16), y, dt("wo", [D, D]))
        elif which == "qkv":
            NW = 3 * D + (8 if layer == 1 else 0)
            ng = 1 if layer == 0 else 2
            gq = [dt(f"gq{j}", [DH]) for j in range(ng)]
            gk = [dt(f"gk{j}", [DH]) for j in range(ng)]
            qT = dt("qT", [NH, DH, T], BF16, "ExternalOutput")
            kT = dt("kT", [NH, DH, T], BF16, "ExternalOutput")
            v = dt("v", [T, D], BF16, "ExternalOutput")
            kw = {}
            if layer == 1:
                kw = dict(bf=dt("bf", [8]), logf_out=dt("logf", [8, T], F32, "ExternalOutput"))
            stage_qkv(p, cx, T, layer, x, dt("gain", [D]), dt("w_in", [D, NW]), gq, gk,
                      dt("cos", [T, 32]), dt("sin", [T, 32]), qT, kT, v, **kw)
        p.emit()
    return nc


class AttnWork:
    pass


def attn_setup(p, cx, S, layer, consts, fused=False):
    W = AttnWork()
    W.S = S
    W.fused = fused
    if fused:
        W.cq = p.sb([64, S], BF16, "a_cq")
        W.cv = p.sb([128, S // 128, DH], BF16, "a_cv")
        W.otmp = [p.sb([128, 4, DH], BF16, f"a_otmp{i}") for i in range(2)]
        if layer == 0:
            W.J = load_const(p, consts["antiid"], [128, 128], BF16, "antiid")
            W.ktok = p.sb([128, S // 128, DH], BF16, "a_ktok")
            W.vnat = p.sb([128, S // 128, DH], BF16, "a_vnat")
    W.qT = p.sb([64, S], BF16, "a_qT")
    W.kT = p.sb([64, S], BF16, "a_kT")
    W.v = p.sb([128, S // 128, 65], BF16, "a_v")
    p.dve(lambda e: e.memset(W.v[:, :, 64:65], 1.0), [], [W.v])
    W.tri = load_const(p, consts["tri"], [128, 128], BF16, "tri")
    W.osb = [p.sb([128, 4, DH], BF16, f"a_o{i}") for i in range(2)]
    W.rinv = [p.sb([128, 1], F32, f"a_rinv{i}") for i in range(4)]
    W.pt = [p.sb([128, 512], BF16, f"a_pt{i}") for i in range(4)]
    if layer == 0:
        W.sbmask = load_const(p, consts["sbmask"], [128, 128], BF16, "sbmask")
        W.onehot = load_const(p, consts["onehot"], [32, S], BF16, "onehot")
        W.QA = p.sb([32, S], BF16, "a_QA")
        W.ones = p.sb([128, 512], F32, "a_ones")
        p.dve(lambda e: e.memset(W.ones[:], 1.0), [], [W.ones])
        W.e = [p.sb([128, 512], F32, f"a_e{i}") for i in range(4)]
        W.sp = [p.sb([128, 512], F32, f"a_sp{i}") for i in range(4)]
        W.cs = [p.sb([128, 512], F32, f"a_cs{i}") for i in range(4)]
        W.ecs = [p.sb([128, 512], F32, f"a_ecs{i}") for i in range(4)]
        W.a = [p.sb([128, 512], BF16, f"a_a{i}") for i in range(4)]
        W.at = [p.sb([128, 512], BF16, f"a_at{i}") for i in range(2)]
        W.km = p.sb([64, 32], F32, "a_km")
        W.kmh = p.sb([64, 32], BF16, "a_kmh")
        W.kml = p.sb([64, 32], BF16, "a_kml")
        W.gm = p.sb([128, 32], F32, "a_gm")
        W.top8 = [p.sb([128, 8], F32, f"a_top{i}") for i in range(2)]
        W.sel = [p.sb([128, 32], F32, f"a_sel{i}") for i in range(2)]
        W.bsel = [p.sb([128, 32], BF16, f"a_bsel{i}") for i in range(2)]
    else:
        W.dilm = load_const(p, consts["dilm"].rearrange("k p t -> p k t"), [128, 20, 512], BF16, "dilm")
        W.QA = p.sb([6, S], BF16, "a_QA")
        W.KA = p.sb([6, S], BF16, "a_KA")
    return W


def _L(a):
    return list(a) if isinstance(a, (list, tuple)) else [a]


def load_job(p, cx, W, q_ap, k_ap, v_ap):
    tq = (W.cq, W.cq[:]) if W.fused else None
    tv = (W.cv, W.cv[:]) if W.fused else None
    load_sel(p, cx, W.qT, W.qT[:], _L(q_ap), tq, npart=64)
    if k_ap is not None:
        load_sel(p, cx, W.kT, W.kT[:], _L(k_ap), tq, npart=64)
    if v_ap is not None:
        _load_tok(p, cx, W, W.v, W.v[:, :, 0:DH], v_ap)


def _load_tok(p, cx, W, dst_buf, dst_ap, src):
    srcs = _L(src)
    if len(srcs[0].shape) == 2:
        load_sel(p, cx, dst_buf, dst_ap, [a.rearrange("(k p) d -> p k d", p=128) for a in srcs],
                 (W.cv, W.cv[:]) if W.fused else None)
    else:
        r = lambda ap: ap.rearrange("p (h k) d -> p h k d", h=2)
        load_sel(p, cx, dst_buf, r(dst_ap), srcs, (W.cv, r(W.cv[:])))


def store_out(p, cx, W, osb, dsts):
    emit_out(p, cx, osb, osb[:], dsts, W.otmp if W.fused else None)


def sb_reverse(p, cx, W, ktok_ap, v_ap):
    S = W.S
    nb = S // 128
    _load_tok(p, cx, W, W.ktok, W.ktok[:], ktok_ap)
    _load_tok(p, cx, W, W.vnat, W.vnat[:], v_ap)
    for r0 in range(0, nb, 4):
        ps = p.ps(r0 // 4 % 2, F32)
        for rb in range(r0, r0 + 4):
            k = nb - 1 - rb
            p.pe(lambda e, o=ps[0:DH, (rb - r0) * 128:(rb - r0 + 1) * 128], a=W.ktok[:, k, :]:
                 e.matmul(o, lhsT=a, rhs=W.J[:], start=True, stop=True), [W.ktok, W.J], [ps])
        p.act(lambda e, o=W.kT[:, r0 * 128:(r0 + 4) * 128], a=ps[0:DH, :]: e.copy(out=o, in_=a), [ps], [W.kT])
    for r0 in range(0, nb, 8):
        ps = p.ps(2 + r0 // 8 % 2, F32)
        for rb in range(r0, r0 + 8):
            k = nb - 1 - rb
            p.pe(lambda e, o=ps[:, (rb - r0) * DH:(rb - r0 + 1) * DH], a=W.vnat[:, k, :]:
                 e.matmul(o, lhsT=W.J[:], rhs=a, start=True, stop=True), [W.vnat, W.J], [ps])
        p.dve(lambda e, o=W.v[:, r0:r0 + 8, 0:DH], a=ps[:].rearrange("p (k d) -> p k d", d=DH): e.tensor_copy(out=o, in_=a),
              [ps], [W.v])


def sb_job(p, cx, W, q_ap, k_ap, v_ap, o_ap, ktok_ap=None):
    S = W.S
    if ktok_ap is None:
        load_job(p, cx, W, q_ap, k_ap, v_ap)
    else:
        load_job(p, cx, W, q_ap, None, None)
        sb_reverse(p, cx, W, ktok_ap, v_ap)
    nb = S // 128
    ovs = [a.rearrange("(g q p) d -> g p q d", p=128, q=4) for a in _L(o_ap)]
    NS = len(W.e)
    LAG = NS - 1
    units = []
    for i in range(nb):
        rb0 = nb - 1 - i
        nun = (128 * (i + 1) + 511) // 512
        for u in range(nun):
            c0 = rb0 * 128 + 512 * u
            units.append(dict(i=i, u=u, nun=nun, c0=c0, N=min(512, S - c0), idx=len(units)))

    def stage_a(t):
        i, u, c0, N, k = t["i"], t["u"], t["c0"], t["N"], t["idx"]
        z = p.ps(k % 2, F32)
        p.pe(lambda e, o=z[:, 0:N], a=W.qT[:, i * 128:(i + 1) * 128], b=W.kT[:, c0:c0 + N], u=u:
             e.matmul(o, lhsT=a, rhs=b, start=True, stop=(u != 0)), [W.qT, W.kT], [z])
        if u == 0:
            p.pe(lambda e, o=z[:, 0:128]: e.matmul(o, lhsT=cx.ident[:], rhs=W.sbmask[:], start=False, stop=True),
                 [cx.ident, W.sbmask], [z])
        e_sb, sp, cs, ecs, a = (W.e[k % NS], W.sp[k % NS], W.cs[k % NS], W.ecs[k % NS], W.a[k % NS])
        p.act(lambda e, o=e_sb[:, 0:N], i_=z[:, 0:N]: e.activation(out=o, in_=i_, func=AF.Exp), [z], [e_sb])
        p.act(lambda e, o=sp[:, 0:N], i_=e_sb[:, 0:N]: e.activation(out=o, in_=i_, func=AF.Ln, bias=1.0), [e_sb], [sp])
        if u == 0:
            init, rd = 0.0, [W.ones, sp]
        else:
            cprev = W.cs[(k - 1) % NS]
            pN = units[k - 1]["N"]
            init, rd = cprev[:, pN - 1:pN], [W.ones, sp, cprev]
        p.dve(lambda e, o=cs[:, 0:N], d0=W.ones[:, 0:N], d1=sp[:, 0:N], init=init: e.tensor_tensor_scan(
            out=o, data0=d0, data1=d1, initial=init, op0=ALU.mult, op1=ALU.add), rd, [cs])

    def stage_a2(t):
        N, k = t["N"], t["idx"]
        e_sb, cs, ecs, a = (W.e[k % NS], W.cs[k % NS], W.ecs[k % NS], W.a[k % NS])
        p.act(lambda e, o=ecs[:, 0:N], i_=cs[:, 0:N]: e.activation(out=o, in_=i_, func=AF.Exp, scale=-1.0), [cs], [ecs])
        p.dve(lambda e, o=a[:, 0:N], x=e_sb[:, 0:N], y=ecs[:, 0:N]: e.tensor_tensor(out=o, in0=x, in1=y, op=ALU.mult),
              [e_sb, ecs], [a])

    def stage_b(t):
        i, u, nun, c0, N, k = t["i"], t["u"], t["nun"], t["c0"], t["N"], t["idx"]
        nk = N // 128
        a = W.a[k % NS]
        at = W.at[k % 2]
        o_ps = p.ps(4 + i % 2, F32)
        atp = p.ps(2 + k % 2, BF16)
        for kb in range(nk):
            p.pe(lambda e, o=atp[:, kb * 128:(kb + 1) * 128], x=a[:, kb * 128:(kb + 1) * 128]:
                 e.transpose(out=o, in_=x, identity=cx.ident[:]), [a, cx.ident], [atp])
        p.act(lambda e, o=at[:, 0:N], x=atp[:, 0:N]: e.copy(out=o, in_=x), [atp], [at])
        for kb in range(nk):
            p.pe(lambda e, o=o_ps[:, 0:DH], x=at[:, kb * 128:(kb + 1) * 128], vv=W.v[:, c0 // 128 + kb, 0:DH],
                 st=(u == 0 and kb == 0), sp_=(u == nun - 1 and kb == nk - 1):
                 e.matmul(o, lhsT=x, rhs=vv, start=st, stop=sp_), [at, W.v], [o_ps])
        if u == nun - 1:
            osb = W.osb[(i // 4) % 2]
            p.act(lambda e, o=osb[:, i % 4, :], x=o_ps[:, 0:DH]: e.copy(out=o, in_=x), [o_ps], [osb])
            if i % 4 == 3:
                store_out(p, cx, W, osb, [a_[i // 4] for a_ in ovs])

    nu = len(units)
    for k in range(nu + LAG):
        if k < nu:
            stage_a(units[k])
        if 0 <= k - 1 < nu:
            stage_a2(units[k - 1])
        if 0 <= k - LAG < nu:
            stage_b(units[k - LAG])


def moba_pre(p, cx, W):
    S = W.S
    nkb = S // 256
    p.dve(lambda e: e.tensor_reduce(out=W.km[:, 0:nkb], in_=W.kT[:].rearrange("p (n s) -> p n s", s=256),
                                    axis=AX.X, op=ALU.add), [W.kT], [W.km])
    p.dve(lambda e: e.tensor_scalar(out=W.km[:, 0:nkb], in0=W.km[:, 0:nkb], scalar1=1.0 / 256, scalar2=None, op0=ALU.mult),
          [W.km], [W.km])
    p.dve(lambda e: e.tensor_copy(out=W.kmh[:, 0:nkb], in_=W.km[:, 0:nkb]), [W.km], [W.kmh])
    p.dve(lambda e: e.tensor_tensor(out=W.kml[:, 0:nkb], in0=W.km[:, 0:nkb], in1=W.kmh[:, 0:nkb], op=ALU.subtract),
          [W.km, W.kmh], [W.kml])
    p.dve(lambda e: e.memset(W.gm[:], -1e30), [], [W.gm])
    for i in range(S // 128):
        own = i // 2
        bsel = W.bsel[i % 2]
        if own == 0:
            p.dve(lambda e, b=bsel: e.memset(b[:], 0.0), [], [bsel])
        else:
            gp = p.ps(4 + i % 2, F32)
            p.pe(lambda e, o=gp[:, 0:nkb], a=W.qT[:, i * 128:(i + 1) * 128]: e.matmul(o, lhsT=a, rhs=W.kmh[:, 0:nkb], start=True, stop=False),
                 [W.qT, W.kmh], [gp])
            p.pe(lambda e, o=gp[:, 0:nkb], a=W.qT[:, i * 128:(i + 1) * 128]: e.matmul(o, lhsT=a, rhs=W.kml[:, 0:nkb], start=False, stop=True),
                 [W.qT, W.kml], [gp])
            p.dve(lambda e, o=W.gm[:, 0:own], a=gp[:, 0:own]: e.tensor_copy(out=o, in_=a), [gp], [W.gm])
            top = W.top8[i % 2]
            sel = W.sel[i % 2]
            p.dve(lambda e, t=top: e.max(out=t[:], in_=W.gm[:]), [W.gm], [top])
            p.dve(lambda e, t=top: e.tensor_scalar(out=t[:, 3:4], in0=t[:, 2:3], scalar1=-1e29, scalar2=None, op0=ALU.max),
                  [top], [top])
            p.dve(lambda e, s=sel, t=top: e.tensor_scalar(out=s[:], in0=W.gm[:], scalar1=t[:, 3:4], scalar2=None, op0=ALU.is_ge),
                  [W.gm, top], [sel])
            p.dve(lambda e, s=sel, b=bsel: e.tensor_scalar(out=b[:], in0=s[:], scalar1=-NEG, scalar2=NEG, op0=ALU.mult, op1=ALU.add),
                  [sel], [bsel])
            p.dve(lambda e, b=bsel, own=own: e.memset(b[:, own:own + 1], 0.0), [], [bsel])
        tp = cx.tps[i % 2]
        p.pe(lambda e, o=tp[0:32, 0:128], b=bsel: e.transpose(out=o, in_=b[:], identity=cx.ident[:]), [bsel, cx.ident], [tp])
        p.act(lambda e, o=W.QA[:, i * 128:(i + 1) * 128], x=tp[0:32, 0:128]: e.copy(out=o, in_=x), [tp], [W.QA])


def softmax_job(p, cx, W, kind, q_ap, k_ap, v_ap, o_ap, fox_rows=None):
    S = W.S
    load_job(p, cx, W, q_ap, k_ap, v_ap)
    QA = KA = None
    if kind == "moba":
        moba_pre(p, cx, W)
        QA, KA = W.QA, W.onehot
    elif kind == "fox":
        p.dve(lambda e: e.memset(W.QA[:], 1.0), [], [W.QA])
        p.dve(lambda e: e.memset(W.KA[:], 1.0), [], [W.KA])
        p.dma(W.QA[0:3, :], fox_rows[0:3, :], [], [W.QA])
        p.dma(W.KA[3:6, :], fox_rows[3:6, :], [], [W.KA])
        QA, KA = W.QA, W.KA
    ovs = [a.rearrange("(g q p) d -> g p q d", p=128, q=4) for a in _L(o_ap)]
    units = []
    for g in range(S // 512):
        jlo = max(0, 4 * g - 16) if kind == "dil" else 0
        for j in range(jlo, 4 * g + 4):
            units.append(dict(g=g, j=j, jlo=jlo, idx=len(units)))
    NP = len(W.pt)

    def stage_a(t):
        g, j, k = t["g"], t["j"], t["idx"]
        r = max(0, j - 4 * g)
        c0 = 128 * r
        N = 512 - c0
        sps = p.ps(k % 2, F32)
        q0 = g * 512 + c0
        diag = (kind != "dil") and j >= 4 * g
        p.pe(lambda e, o=sps[:, 0:N], a=W.kT[:, j * 128:(j + 1) * 128], b=W.qT[:, q0:q0 + N], sp_=(QA is None and not diag):
             e.matmul(o, lhsT=a, rhs=b, start=True, stop=sp_), [W.kT, W.qT], [sps])
        if QA is not None:
            p.pe(lambda e, o=sps[:, 0:N], a=KA[:, j * 128:(j + 1) * 128], b=QA[:, q0:q0 + N], sp_=(not diag):
                 e.matmul(o, lhsT=a, rhs=b, start=False, stop=sp_), [KA, QA], [sps])
        if diag:
            p.pe(lambda e, o=sps[:, 0:128]: e.matmul(o, lhsT=cx.ident[:], rhs=W.tri[:], start=False, stop=True),
                 [cx.ident, W.tri], [sps])
        pt = W.pt[k % NP]
        p.act(lambda e, o=pt[:, 0:N], x=sps[:, 0:N]: e.activation(out=o, in_=x, func=AF.Exp), [sps], [pt])
        if kind == "dil":
            dm = W.dilm[:, 4 * g - j + 3, c0:512]
            p.dve(lambda e, o=pt[:, 0:N], m_=dm: e.tensor_tensor(out=o, in0=o, in1=m_, op=ALU.mult), [pt, W.dilm], [pt])

    def stage_b(t):
        g, j, jlo, k = t["g"], t["j"], t["jlo"], t["idx"]
        r = max(0, j - 4 * g)
        pt = W.pt[k % NP]
        accs = [p.ps(2 + qb, F32) for qb in range(4)]
        osb = W.osb[g % 2]
        for qb in range(r, 4):
            lc = (qb - r) * 128
            last = (j == 4 * g + qb)
            p.pe(lambda e, o=accs[qb][:, 0:65], x=pt[:, lc:lc + 128], vv=W.v[:, j, :], st=(j == jlo), sp_=last:
                 e.matmul(o, lhsT=x, rhs=vv, start=st, stop=sp_), [pt, W.v], [accs[qb]])
            if last:
                rv = W.rinv[qb]
                p.dve(lambda e, o=rv[:], x=accs[qb][:, 64:65]: e.reciprocal(out=o, in_=x), [accs[qb]], [rv])
                p.dve(lambda e, o=osb[:, qb, :], x=accs[qb][:, 0:DH], s_=rv[:, 0:1]: e.tensor_scalar(
                    out=o, in0=x, scalar1=s_, scalar2=None, op0=ALU.mult), [accs[qb], rv], [osb])
        if j == 4 * g + 3:
            store_out(p, cx, W, osb, [a_[g] for a_ in ovs])

    LAG = 2
    for k, t in enumerate(units):
        stage_a(t)
        if k >= LAG:
            stage_b(units[k - LAG])
    for t in units[max(0, len(units) - LAG):]:
        stage_b(t)


def fox_pre(p, cx, S, logf4, caug):
    m = p.mark()
    PW = min(2048, S)
    lf = [p.sb([4, PW], F32, f"fx_lf{i}") for i in range(2)]
    ones = p.sb([4, PW], F32, "fx_ones")
    c = [p.sb([4, PW], F32, f"fx_c{i}") for i in range(2)]
    r1 = p.sb([4, PW], F32, "fx_r1")
    r2 = p.sb([4, PW], F32, "fx_r2")
    rows = [[p.sb([4, PW], BF16, f"fx_row{k}_{i}") for k in range(6)] for i in range(2)]
    p.dve(lambda e: e.memset(ones[:], 1.0), [], [ones])
    prev = None
    for pc in range(S // PW):
        l = lf[pc % 2]
        cc = c[pc % 2]
        rw = rows[pc % 2]
        load_sel(p, cx, l, l[:], [a[:, pc * PW:(pc + 1) * PW] for a in _L(logf4)], (r1, r1[:]), npart=4)
        init = 0.0 if prev is None else prev[:, PW - 1:PW]
        rd = [ones, l] + ([prev] if prev is not None else [])
        p.dve(lambda e, o=cc[:], d1=l[:], init=init: e.tensor_tensor_scan(out=o, data0=ones[:], data1=d1, initial=init,
                                                                         op0=ALU.mult, op1=ALU.add), rd, [cc])
        prev = cc
        hi, mid, lo, nhi, nmid, nlo = rw
        p.dve(lambda e, o=hi[:], x=cc[:]: e.tensor_copy(out=o, in_=x), [cc], [hi])
        p.dve(lambda e, o=r1[:], x=cc[:], y=hi[:]: e.tensor_tensor(out=o, in0=x, in1=y, op=ALU.subtract), [cc, hi], [r1])
        p.dve(lambda e, o=mid[:]: e.tensor_copy(out=o, in_=r1[:]), [r1], [mid])
        p.dve(lambda e, o=r2[:], y=mid[:]: e.tensor_tensor(out=o, in0=r1[:], in1=y, op=ALU.subtract), [r1, mid], [r2])
        p.dve(lambda e, o=lo[:]: e.tensor_copy(out=o, in_=r2[:]), [r2], [lo])
        for src, dst in ((hi, nhi), (mid, nmid), (lo, nlo)):
            p.dve(lambda e, o=dst[:], x=src[:]: e.tensor_scalar(out=o, in0=x, scalar1=-1.0, scalar2=None, op0=ALU.mult),
                  [src], [dst])
        for k in range(6):
            p.dma(caug[:, k, pc * PW:(pc + 1) * PW], rw[k][:], [rw[k]], [])
    p.barrier()
    p.release(m)


def build_attn(S, layer, njobs=4, only=None):
    nc, stack, p = new_prog()
    with stack:
        dt = lambda n, sh, d=BF16, k="ExternalInput": nc.dram_tensor(n, sh, d, kind=k).ap()
        ident = dt("ident", [128, 128])
        cx = make_ctx(p, ident)
        consts = {"tri": dt("tri", [128, 128])}
        if layer == 0:
            consts["sbmask"] = dt("sbmask", [128, 128])
            consts["onehot"] = dt("onehot", [32, S])
        else:
            consts["dilm"] = dt("dilm", [20, 128, 512])
            logf4 = dt("logf4", [4, S], F32)
            caug = nc.dram_tensor("caug", [4, 6, S], BF16).ap()
            fox_pre(p, cx, S, logf4, caug)
        W = attn_setup(p, cx, S, layer, consts)
        dq, dk, dv = dt("dq", [njobs, DH, S]), dt("dk", [njobs, DH, S]), dt("dv", [njobs, S, DH])
        sq, sk, sv = dt("sq", [njobs, DH, S]), dt("sk", [njobs, DH, S]), dt("sv", [njobs, S, DH])
        od = dt("od", [njobs, S, DH], BF16, "ExternalOutput")
        os_ = dt("os", [njobs, S, DH], BF16, "ExternalOutput")
        for jb in range(njobs):
            if layer == 0:
                if only in (None, "d"):
                    sb_job(p, cx, W, dq[jb], dk[jb], dv[jb], od[jb])
                if only in (None, "s"):
                    softmax_job(p, cx, W, "moba", sq[jb], sk[jb], sv[jb], os_[jb])
            else:
                if only in (None, "d"):
                    softmax_job(p, cx, W, "fox", dq[jb], dk[jb], dv[jb], od[jb], fox_rows=caug[jb])
                if only in (None, "s"):
                    softmax_job(p, cx, W, "dil", sq[jb], sk[jb], sv[jb], os_[jb])
        p.barrier()
        p.emit()
    return nc


def attn_consts_np(S, layer):
    ii = np.arange(128)
    c = {"ident": np.eye(128, dtype=np.float32).astype(NPBF),
         "tri": np.where(ii[:, None] <= ii[None, :], 0.0, NEG).astype(np.float32).astype(NPBF)}
    if layer == 0:
        c["sbmask"] = np.where(ii[None, :] + ii[:, None] >= 128, 0.0, NEG).astype(np.float32).astype(NPBF)
        c["onehot"] = (np.arange(S)[None, :] // 256 == np.arange(32)[:, None]).astype(np.float32).astype(NPBF)
    else:
        dm = np.zeros((20, 128, 512), np.float32)
        ss = np.arange(128)[:, None]
        tt = np.arange(512)[None, :]
        for d in range(-3, 17):
            o = 128 * d + tt - ss
            m = np.zeros_like(o, dtype=np.float32)
            for w, r in ((128, 1), (512, 4), (2048, 16)):
                m += ((o >= 0) & (o <= w) & (o % r == 0)).astype(np.float32)
            dm[d + 3] = m
        c["dilm"] = dm.astype(NPBF)
    return c


def build_tok_launch(T, plan):
    nc, stack, p = new_prog()
    with stack:
        dt = lambda n, sh, d=F32, k="ExternalInput": nc.dram_tensor(n, sh, d, kind=k).ap()
        ident = dt("ident", [128, 128], BF16)
        cx = make_ctx(p, ident)
        cur = dt("x", [T, D])
        for si, st in enumerate(plan):
            last_x = not any(s_ in ("outproj", "ffn") for s_ in plan[si + 1:])
            if st in ("outproj", "ffn"):
                nxt = (dt(f"xo{si}", [T, D], F32, "ExternalOutput") if last_x
                       else nc.dram_tensor(f"xs{si}", [T, D], F32).ap())
            if st == "ffn":
                stage_ffn(p, cx, T, cur, nxt, dt(f"gain{si}", [D]), dt(f"wg{si}", [D, DFF]), dt(f"wu{si}", [D, DFF]),
                          dt(f"wd{si}", [DFF, D]))
                cur = nxt
            elif st == "outproj":
                stage_outproj(p, cx, T, cur, dt(f"attn{si}", [T, D], BF16), nxt, dt(f"wo{si}", [D, D]))
                cur = nxt
            else:
                layer = int(st[-1])
                NW = 3 * D + (8 if layer == 1 else 0)
                ng = 1 if layer == 0 else 2
                gq = [dt(f"gq{j}", [DH]) for j in range(ng)]
                gk = [dt(f"gk{j}", [DH]) for j in range(ng)]
                kw = {}
                if layer == 1:
                    kw = dict(bf=dt("bf", [8]), logf_out=dt("logf", [8, T], F32, "ExternalOutput"))
                stage_qkv(p, cx, T, layer, cur, dt(f"gain{si}", [D]), dt("w_in", [D, NW]), gq, gk,
                          dt("cos", [T, 32]), dt("sin", [T, 32]),
                          dt("qT", [NH, DH, T], BF16, "ExternalOutput"), dt("kT", [NH, DH, T], BF16, "ExternalOutput"),
                          dt("v", [T, D], BF16, "ExternalOutput"), **kw)
        p.emit()
    return nc


def _run(nc, in_maps):
    in_maps = [{k: np.ascontiguousarray(v) for k, v in m.items()} for m in in_maps]
    return run_bass_kernel_spmd(nc, in_maps, core_ids=list(range(len(in_maps)))).results


def kernel_unfused(x, norm_ffn1, ffn1_w_gate, ffn1_w_up, ffn1_w_down, norm_mix, w_in_ab, g_q_b, g_k_b,
           w_in_cd, b_f, g_q_c, g_k_c, g_q_d, g_k_d, w_out, norm_ffn2, ffn2_w_gate, ffn2_w_up,
           ffn2_w_down):
    A = lambda a: np.asarray(a, dtype=np.float32)
    x = A(x)
    B, S, _ = x.shape
    NC = 8
    T = B * S // NC
    half_of = lambda c: (c // 2, c % 2)
    ident = np.eye(128, dtype=np.float32).astype(NPBF)
    inv = (1.0 / (10000.0 ** (np.arange(0, DH, 2, dtype=np.float32) / DH))).astype(np.float32)
    ang = (np.arange(S, dtype=np.float32)[:, None] * inv[None, :]).astype(np.float32)
    cos, sin = np.cos(ang).astype(np.float32), np.sin(ang).astype(np.float32)

    def tok_shard(c, arr):
        b, h = half_of(c)
        return arr[b, h * T:(h + 1) * T]

    def ffn_w(si, norm, wg, wu, wd, l):
        return {f"gain{si}": A(norm[l]), f"wg{si}": A(wg[l]), f"wu{si}": A(wu[l]), f"wd{si}": A(wd[l])}

    def attn_inputs(layer, res):
        consts = attn_consts_np(S, layer)
        maps = []
        for c in range(NC):
            b, h = half_of(c)
            QT = np.concatenate([res[2 * b]["qT"], res[2 * b + 1]["qT"]], axis=2)
            KT = np.concatenate([res[2 * b]["kT"], res[2 * b + 1]["kT"]], axis=2)
            V = np.concatenate([res[2 * b]["v"], res[2 * b + 1]["v"]], axis=0)
            V = V.reshape(S, NH, DH).transpose(1, 0, 2)
            hd = slice(4 * h, 4 * h + 4)
            hs = slice(8 + 4 * h, 12 + 4 * h)
            m = dict(consts)
            if layer == 0:
                m.update(dq=QT[hd], dk=KT[hd][:, :, ::-1], dv=V[hd][:, ::-1, :])
            else:
                LF = np.concatenate([res[2 * b]["logf"], res[2 * b + 1]["logf"]], axis=1)
                m.update(dq=QT[hd], dk=KT[hd], dv=V[hd], logf4=LF[hd])
            m.update(sq=QT[hs], sk=KT[hs], sv=V[hs])
            maps.append(m)
        return maps

    def attn_gather(res):
        outs = []
        for b in range(B):
            full = np.empty((S, NH, DH), dtype=NPBF)
            for h in range(2):
                r = res[2 * b + h]
                full[:, 4 * h:4 * h + 4] = np.asarray(r["od"]).transpose(1, 0, 2)
                full[:, 8 + 4 * h:12 + 4 * h] = np.asarray(r["os"]).transpose(1, 0, 2)
            full = full.reshape(S, D)
            outs += [full[0:T], full[T:2 * T]]
        return outs

    ncA = build_tok_launch(T, ["ffn", "qkv0"])
    mA = []
    for c in range(NC):
        m = {"ident": ident, "x": tok_shard(c, x), "w_in": A(w_in_ab[0]), "gq0": A(g_q_b[0]), "gk0": A(g_k_b[0]),
             "gain1": A(norm_mix[0]), "cos": cos[(c % 2) * T:(c % 2 + 1) * T], "sin": sin[(c % 2) * T:(c % 2 + 1) * T]}
        m.update(ffn_w(0, norm_ffn1, ffn1_w_gate, ffn1_w_up, ffn1_w_down, 0))
        mA.append(m)
    rA = _run(ncA, mA)
    ncB = build_attn(S, 0)
    rB = _run(ncB, attn_inputs(0, rA))
    att0 = attn_gather(rB)
    ncC = build_tok_launch(T, ["outproj", "ffn", "ffn", "qkv1"])
    mC = []
    for c in range(NC):
        m = {"ident": ident, "x": rA[c]["xo0"], "attn0": att0[c], "wo0": A(w_out[0]),
             "w_in": A(w_in_cd[0]), "gq0": A(g_q_c[0]), "gk0": A(g_k_c[0]), "gq1": A(g_q_d[0]), "gk1": A(g_k_d[0]),
             "bf": A(b_f[0]), "gain3": A(norm_mix[1]),
             "cos": cos[(c % 2) * T:(c % 2 + 1) * T], "sin": sin[(c % 2) * T:(c % 2 + 1) * T]}
        m.update(ffn_w(1, norm_ffn2, ffn2_w_gate, ffn2_w_up, ffn2_w_down, 0))
        m.update(ffn_w(2, norm_ffn1, ffn1_w_gate, ffn1_w_up, ffn1_w_down, 1))
        mC.append(m)
    rC = _run(ncC, mC)
    ncD = build_attn(S, 1)
    rD = _run(ncD, attn_inputs(1, rC))
    att1 = attn_gather(rD)
    ncE = build_tok_launch(T, ["outproj", "ffn"])
    mE = []
    for c in range(NC):
        m = {"ident": ident, "x": rC[c]["xo2"], "attn0": att1[c], "wo0": A(w_out[1])}
        m.update(ffn_w(1, norm_ffn2, ffn2_w_gate, ffn2_w_up, ffn2_w_down, 1))
        mE.append(m)
    rE = _run(ncE, mE)
    out = np.empty((B, S, D), dtype=np.float32)
    for c in range(NC):
        b, h = half_of(c)
        out[b, h * T:(h + 1) * T] = rE[c]["xo1"]
    return out


RG_PAIRS = [[0, 1], [2, 3], [4, 5], [6, 7]]


class XBuf:
    def __init__(self, nc, name, nelem, dt, pattern, **axes):
        ce = min(nelem, 128 * 16384)
        self.nch = nelem // ce
        assert self.nch * ce == nelem and ce % 128 == 0
        self.i = nc.dram_tensor(name + "_i", [self.nch, 128, ce // 128], dt)
        self.o = nc.dram_tensor(name + "_o", [self.nch, 128, ce // 128], dt)
        self.vi = self.i.ap().rearrange("c p f -> (c p f)").rearrange(pattern, **axes)
        self.vo = self.o.ap().rearrange("c p f -> (c p f)").rearrange(pattern, **axes)

    def exchange(self, p):
        for c in range(self.nch):
            p.collective(self.i[c], self.o[c], RG_PAIRS)


def build_fused(T, S):
    nc, stack, p = new_prog()
    with stack:
        dt = lambda n, sh, d=F32, k="ExternalInput": nc.dram_tensor(n, sh, d, kind=k).ap()
        ident = dt("ident", [128, 128], BF16)
        cx = make_ctx(p, ident, dt("rank", [128, 2]))
        x = dt("x", [T, D])
        y = dt("y", [T, D], F32, "ExternalOutput")
        cos, sin = dt("cos", [T, 32]), dt("sin", [T, 32])
        consts0 = {"tri": dt("tri", [128, 128], BF16), "sbmask": dt("sbmask", [128, 128], BF16),
                   "onehot": dt("onehot", [32, S], BF16), "antiid": dt("antiid", [128, 128], BF16)}
        consts1 = {"tri": consts0["tri"], "dilm": dt("dilm", [20, 128, 512], BF16)}
        ffn = [dict(gain=dt(f"gain{i}", [D]), wg=dt(f"wg{i}", [D, DFF]), wu=dt(f"wu{i}", [D, DFF]), wd=dt(f"wd{i}", [DFF, D]))
               for i in range(4)]
        nmix = [dt(f"nmix{i}", [D]) for i in range(2)]
        w_in = [dt("w_in0", [D, 3 * D]), dt("w_in1", [D, 3 * D + 8])]
        wo = [dt(f"wo{i}", [D, D]) for i in range(2)]
        gqb, gkb = dt("gqb", [DH]), dt("gkb", [DH])
        gqc, gkc, gqd, gkd = dt("gqc", [DH]), dt("gkc", [DH]), dt("gqd", [DH]), dt("gkd", [DH])
        bf = dt("bf", [8])
        xs = [nc.dram_tensor(f"xs{i}", [T, D], F32).ap() for i in range(5)]
        EQ = XBuf(nc, "eq", NH * DH * S, BF16, "(n d h t) -> n d h t", n=NH, d=DH, h=2)
        EK = XBuf(nc, "ek", NH * DH * S, BF16, "(n d h t) -> n d h t", n=NH, d=DH, h=2)
        EV = XBuf(nc, "ev", S * D, BF16, "(h n p k d) -> h n p k d", h=2, n=NH, p=128, d=DH)
        EKT = XBuf(nc, "ekt", S * 512, BF16, "(h n p k d) -> h n p k d", h=2, n=8, p=128, d=DH)
        ELF = XBuf(nc, "elf", 8 * S, F32, "(n h t) -> n h t", n=8, h=2)
        EA = XBuf(nc, "ea", S * D, BF16, "(h t c) -> h t c", h=2, c=D)
        caug = nc.dram_tensor("caug", [4, 6, S], BF16).ap()

        def exchange(bufs):
            for b_ in bufs:
                b_.exchange(p)
            p.barrier()

        def head_q(E, hh):
            return E.vo[hh].rearrange("d h t -> d (h t)")

        def cols(E, hh, out=False):
            v = (E.vi if out else E.vo).rearrange("h t c -> (h t) c")
            return v[:, hh * DH:(hh + 1) * DH]

        def pmv(E, hh):
            return E.vo[:, hh].rearrange("h p k d -> p h k d")

        def attention(layer):
            m = p.mark()
            if layer == 1:
                lf = ELF.vo.rearrange("n h t -> n (h t)")
                fox_pre(p, cx, S, [lf[0:4], lf[4:8]], caug)
            W = attn_setup(p, cx, S, layer, consts0 if layer == 0 else consts1, fused=True)
            for j in range(4):
                d0, d1, s0, s1 = j, 4 + j, 8 + j, 12 + j
                if layer == 0:
                    sb_job(p, cx, W, [head_q(EQ, d0), head_q(EQ, d1)], None, [pmv(EV, d0), pmv(EV, d1)],
                           [cols(EA, d0, True), cols(EA, d1, True)], ktok_ap=[pmv(EKT, d0), pmv(EKT, d1)])
                    kind = "moba"
                else:
                    softmax_job(p, cx, W, "fox", [head_q(EQ, d0), head_q(EQ, d1)], [head_q(EK, d0), head_q(EK, d1)],
                                [pmv(EV, d0), pmv(EV, d1)], [cols(EA, d0, True), cols(EA, d1, True)], fox_rows=caug[j])
                    kind = "dil"
                softmax_job(p, cx, W, kind, [head_q(EQ, s0), head_q(EQ, s1)], [head_q(EK, s0), head_q(EK, s1)],
                            [pmv(EV, s0), pmv(EV, s1)], [cols(EA, s0, True), cols(EA, s1, True)])
            p.barrier()
            p.release(m)

        def qkv(layer, xin):
            kw = {"pm": True}
            if layer == 0:
                gq, gk = [gqb], [gkb]
                kw["ktok_out"] = [EKT.vi[0], EKT.vi[1]]
            else:
                gq, gk = [gqc, gqd], [gkc, gkd]
                kw.update(bf=bf, logf_out=[ELF.vi[:, 0, :], ELF.vi[:, 1, :]])
            stage_qkv(p, cx, T, layer, xin, nmix[layer], w_in[layer], gq, gk, cos, sin,
                      [EQ.vi[:, :, 0, :], EQ.vi[:, :, 1, :]], [EK.vi[:, :, 0, :], EK.vi[:, :, 1, :]],
                      [EV.vi[0], EV.vi[1]], **kw)

        stage_ffn(p, cx, T, x, xs[0], **ffn[0])
        qkv(0, xs[0])
        exchange([EQ, EK, EV, EKT])
        attention(0)
        exchange([EA])
        stage_outproj(p, cx, T, xs[0], [EA.vo[0], EA.vo[1]], xs[1], wo[0])
        stage_ffn(p, cx, T, xs[1], xs[2], **ffn[1])
        stage_ffn(p, cx, T, xs[2], xs[3], **ffn[2])
        qkv(1, xs[3])
        exchange([EQ, EK, EV, ELF])
        attention(1)
        exchange([EA])
        stage_outproj(p, cx, T, xs[3], [EA.vo[0], EA.vo[1]], xs[4], wo[1])
        stage_ffn(p, cx, T, xs[4], y, **ffn[3])
        p.emit()
    return nc


def kernel(x, norm_ffn1, ffn1_w_gate, ffn1_w_up, ffn1_w_down, norm_mix, w_in_ab, g_q_b, g_k_b,
           w_in_cd, b_f, g_q_c, g_k_c, g_q_d, g_k_d, w_out, norm_ffn2, ffn2_w_gate, ffn2_w_up,
           ffn2_w_down):
    A = lambda a: np.asarray(a, dtype=np.float32)
    x = A(x)
    B, S, _ = x.shape
    NC = 8
    T = B * S // NC
    inv = (1.0 / (10000.0 ** (np.arange(0, DH, 2, dtype=np.float32) / DH))).astype(np.float32)
    ang = (np.arange(S, dtype=np.float32)[:, None] * inv[None, :]).astype(np.float32)
    cos, sin = np.cos(ang).astype(np.float32), np.sin(ang).astype(np.float32)
    c0, c1 = attn_consts_np(S, 0), attn_consts_np(S, 1)
    shared = {"ident": c0["ident"], "tri": c0["tri"], "sbmask": c0["sbmask"], "onehot": c0["onehot"],
              "antiid": np.ascontiguousarray(np.eye(128, dtype=np.float32)[::-1]).astype(NPBF), "dilm": c1["dilm"],
              "nmix0": A(norm_mix[0]), "nmix1": A(norm_mix[1]), "w_in0": A(w_in_ab[0]), "w_in1": A(w_in_cd[0]),
              "wo0": A(w_out[0]), "wo1": A(w_out[1]), "gqb": A(g_q_b[0]), "gkb": A(g_k_b[0]),
              "gqc": A(g_q_c[0]), "gkc": A(g_k_c[0]), "gqd": A(g_q_d[0]), "gkd": A(g_k_d[0]), "bf": A(b_f[0])}
    for i, (nrm, wg, wu, wd, l) in enumerate([(norm_ffn1, ffn1_w_gate, ffn1_w_up, ffn1_w_down, 0),
                                               (norm_ffn2, ffn2_w_gate, ffn2_w_up, ffn2_w_down, 0),
                                               (norm_ffn1, ffn1_w_gate, ffn1_w_up, ffn1_w_down, 1),
                                               (norm_ffn2, ffn2_w_gate, ffn2_w_up, ffn2_w_down, 1)]):
        shared.update({f"gain{i}": A(nrm[l]), f"wg{i}": A(wg[l]), f"wu{i}": A(wu[l]), f"wd{i}": A(wd[l])})
    maps = []
    for c in range(NC):
        b, h = c // 2, c % 2
        m = dict(shared)
        rk = np.zeros((128, 2), np.float32)
        rk[:, h] = 1.0
        m.update(x=x[b, h * T:(h + 1) * T], rank=rk, cos=cos[h * T:(h + 1) * T], sin=sin[h * T:(h + 1) * T])
        maps.append(m)
    nc = build_fused(T, S)
    res = _run(nc, maps)
    out = np.empty((B, S, D), dtype=np.float32)
    for c in range(NC):
        b, h = c // 2, c % 2
        out[b, h * T:(h + 1) * T] = res[c]["y"]
    return out
```

```python
import numpy as np
import ml_dtypes
from contextlib import ExitStack
import concourse.bass as bass
import concourse.mybir as mybir
from concourse.bass_utils import run_bass_kernel_spmd

F32 = mybir.dt.float32
BF16 = mybir.dt.bfloat16
AF = mybir.ActivationFunctionType
ALU = mybir.AluOpType
AX = mybir.AxisListType
NPBF = ml_dtypes.bfloat16

D = 1024
DFF = 2816
NH = 16
DH = 64
S_FULL = 8192
B_FULL = 4
EPS = 1e-6
NEG = -30000.0
STW = 1540


class _State:
    __slots__ = ("last_w", "readers", "sem", "cnt", "sw")

    def __init__(self):
        self.last_w = None
        self.readers = []
        self.sem = None
        self.cnt = 0
        self.sw = False


class Buf:
    __slots__ = ("t", "st", "name")

    def __init__(self, t, name, st=None):
        self.t = t
        self.name = name
        self.st = st if st is not None else _State()

    def __getitem__(self, k):
        return self.t[k]

    last_w = property(lambda s: s.st.last_w, lambda s, v: setattr(s.st, "last_w", v))
    readers = property(lambda s: s.st.readers, lambda s, v: setattr(s.st, "readers", v))
    sem = property(lambda s: s.st.sem, lambda s, v: setattr(s.st, "sem", v))
    cnt = property(lambda s: s.st.cnt, lambda s, v: setattr(s.st, "cnt", v))


class Ins:
    __slots__ = ("eng", "fn", "deps", "needs_inc", "count", "epoch", "dma_tok", "idx", "inc")

    def __init__(self, eng, fn):
        self.eng = eng
        self.fn = fn
        self.deps = []
        self.needs_inc = False
        self.count = 0
        self.epoch = 0
        self.dma_tok = None
        self.idx = 0
        self.inc = 16


ENGS = ("pe", "act", "dve", "pool", "sp")
EPOCH = 20000


class Prog:
    ARENA_BYTES = 212000

    def __init__(self, nc, stack):
        self.nc = nc
        self.stack = stack
        self.ins = {e: [] for e in ENGS}
        self.nsb = 0
        self.arena = stack.enter_context(nc.sbuf_tensor("arena", [128, self.ARENA_BYTES // 2], BF16))
        self.off = 0
        self.live = []
        self.sem_pool = []
        self.sem_pool_sw = []
        self.nsem = 0
        self.cc_sem = None
        self.cc_cnt = 0
        self.banks = [stack.enter_context(nc.psum_tensor(f"bank{i}", [128, 512], F32)) for i in range(8)]
        self.bank_state = [_State() for _ in range(8)]

    def sb(self, shape, dt, name=None):
        self.nsb += 1
        name = name or f"sb{self.nsb}"
        esz = 4 if dt == F32 else 2
        n = 1
        for d in shape[1:]:
            n *= d
        nbytes = (n * esz + 63) // 64 * 64
        assert self.off + nbytes <= self.ARENA_BYTES, (name, self.off, nbytes)
        v = self.arena[0:shape[0], self.off // 2:(self.off + n * esz) // 2]
        if dt == F32:
            v = v.bitcast(F32)
        if len(shape) == 3:
            v = v.rearrange("p (a b) -> p a b", a=shape[1])
        self.off += nbytes
        b = Buf(v, name)
        self.live.append((self.off - nbytes, b))
        return b

    def mark(self):
        return self.off

    def release(self, m):
        keep = []
        for off, b in self.live:
            if off >= m:
                if b.sem is not None:
                    (self.sem_pool_sw if b.st.sw else self.sem_pool).append((b.sem, b.cnt))
                    b.st.sem = None
            else:
                keep.append((off, b))
        self.live = keep
        self.off = m

    def ps(self, bank, dt=F32, name=None):
        v = self.banks[bank][:, :]
        if dt == BF16:
            v = v.bitcast(BF16)
        return Buf(v, name or f"bank{bank}", self.bank_state[bank])

    def _add(self, eng, fn, reads, writes, dma=False):
        i = Ins(eng, fn)
        i.idx = len(self.ins[eng])
        deps = []
        for b in reads:
            if b.last_w is not None:
                deps.append(b.last_w)
        for b in writes:
            if b.last_w is not None:
                deps.append(b.last_w)
            for r in b.readers:
                deps.append(r)
        seen = set()
        for d in deps:
            if id(d) in seen or d is i:
                continue
            seen.add(id(d))
            if d.dma_tok is None and d.eng == eng:
                if eng == "pe":
                    continue
                if not any((b.last_w is d) for b in list(reads) + list(writes)):
                    continue
            d.needs_inc = True
            i.deps.append(d)
        if dma:
            key = None
            for b in list(writes) + list(reads):
                key = b
                break
            if key.sem is None:
                pool_ = self.sem_pool_sw if eng == "pool" else self.sem_pool
                key.st.sw = (eng == "pool")
                if pool_:
                    key.sem, key.cnt = pool_.pop()
                else:
                    self.nsem += 1
                    key.sem = self.stack.enter_context(self.nc.semaphore(f"d{self.nsem}_{key.name}"))
                    key.cnt = 0
            key.cnt += 16
            i.dma_tok = (key.sem, key.cnt)
        for b in reads:
            b.readers.append(i)
        for b in writes:
            b.last_w = i
            b.readers = []
        self.ins[eng].append(i)
        return i

    def pe(self, fn, reads, writes):
        return self._add("pe", fn, reads, writes)

    def act(self, fn, reads, writes):
        return self._add("act", fn, reads, writes)

    def dve(self, fn, reads, writes):
        return self._add("dve", fn, reads, writes)

    def pool(self, fn, reads, writes):
        return self._add("pool", fn, reads, writes)

    def dma(self, out_ap, in_ap, reads, writes, eng=None):
        if eng is None:
            eng = "sp"
        return self._add(eng, lambda e: e.dma_start(out=out_ap, in_=in_ap), reads, writes, dma=True)

    def collective(self, in_ap, out_ap, groups):
        if self.cc_sem is None:
            self.cc_sem = self.stack.enter_context(self.nc.semaphore("cc_sem"))
        i = Ins("pool", lambda e: e.collective_compute("AllReduce", ALU.add, replica_groups=groups,
                                                       ins=[in_ap], outs=[out_ap]))
        self.cc_cnt += 1
        i.dma_tok = (self.cc_sem, self.cc_cnt)
        i.inc = 1
        self.ins["pool"].append(i)
        return i

    def barrier(self):
        lasts = []
        for e in ENGS:
            if e == "sp":
                continue
            for i in reversed(self.ins[e]):
                if i.fn is not None and i.dma_tok is None:
                    lasts.append(i)
                    break
        dmas = {}
        for e in ENGS:
            for i in self.ins[e]:
                if i.dma_tok is not None:
                    dmas[id(i.dma_tok[0])] = i
        for e in ENGS:
            i = Ins(e, None)
            for d in lasts:
                if d.eng != e:
                    d.needs_inc = True
                    i.deps.append(d)
            for d in dmas.values():
                i.deps.append(d)
            self.ins[e].append(i)

    def emit(self):
        nc = self.nc
        sems = {}
        for e in ENGS:
            c = 0
            ep = 0
            for i in self.ins[e]:
                if i.dma_tok is None and i.needs_inc:
                    if c >= EPOCH:
                        ep += 1
                        c = 0
                    c += 1
                    i.count = c
                    i.epoch = ep
                    if (e, ep) not in sems:
                        sems[(e, ep)] = self.stack.enter_context(nc.semaphore(f"s_{e}{ep}"))
        block = self.stack.enter_context(nc.Block())
        engobj = {"pe": block.tensor, "act": block.scalar, "dve": block.vector,
                  "pool": block.gpsimd, "sp": block.sync}

        def make(e):
            def body(eng):
                waited = {}
                for i in self.ins[e]:
                    need = {}
                    for d in i.deps:
                        if d.dma_tok is not None:
                            sem, val = d.dma_tok
                        else:
                            sem, val = sems[(d.eng, d.epoch)], d.count
                        k = id(sem)
                        if waited.get(k, 0) >= val:
                            continue
                        if k not in need or need[k][1] < val:
                            need[k] = (sem, val)
                    for k, (sem, val) in need.items():
                        eng.wait_ge(sem, val)
                        waited[k] = val
                    if i.fn is None:
                        continue
                    r = i.fn(eng)
                    if i.dma_tok is not None:
                        r.then_inc(i.dma_tok[0], i.inc)
                    elif i.needs_inc:
                        r.then_inc(sems[(e, i.epoch)], 1)
            return body

        for e in ENGS:
            if self.ins[e]:
                engobj[e](make(e))


def new_prog():
    nc = bass.Bass("TRN2", target_bir_lowering=False)
    stack = ExitStack()
    return nc, stack, Prog(nc, stack)


class Ctx:
    pass


def load_const(p, ap, shape, dt, name):
    b = p.sb(shape, dt, name)
    p.dma(b[:], ap, [], [b])
    return b


def load_weight_bf16(p, w_ap, K, N, name, stage, eng_rr):
    kc = K // 128
    wv = w_ap.rearrange("(c p) n -> p c n", p=128)
    chunks = []
    for c in range(kc):
        wb = p.sb([128, N], BF16, f"{name}{c}")
        chunks.append(wb)
        p.dma(wb[:], wv[:, c, :], [], [wb], eng="pool")
    return chunks


def rms_to_T(p, cx, x_sb, g_bc, xnT, col0, nsub):
    for i in range(nsub):
        xs = x_sb[i]
        ss = cx.small[cx.rr % len(cx.small)]
        cx.rr += 1
        junk = cx.junk
        p.act(lambda e, o=junk[:], a=xs[:], s=ss[:, 0:1]: e.activation(out=o, in_=a, func=AF.Square, accum_out=s),
              [xs], [junk, ss])
        p.act(lambda e, s=ss: e.activation(out=s[:, 1:2], in_=s[:, 0:1], func=AF.Sqrt, scale=1.0 / D, bias=cx.eps[:, 0:1]),
              [ss, cx.eps], [ss])
        p.dve(lambda e, s=ss: e.reciprocal(out=s[:, 2:3], in_=s[:, 1:2]), [ss], [ss])
        xn = cx.xn[cx.rr % len(cx.xn)]
        p.dve(lambda e, o=xn[:], a=xs[:], s=ss[:, 2:3], g=g_bc[:]: e.scalar_tensor_tensor(
            out=o, in0=a, scalar=s, in1=g, op0=ALU.mult, op1=ALU.mult), [xs, ss, g_bc], [xn])
        transpose_to(p, cx, xn, xnT, 0, col0 + 128 * i, D // 128)


def transpose_to(p, cx, src, dstT, c0, col, nchunk, flat=None):
    tp = cx.tps[cx.rrt % len(cx.tps)]
    cx.rrt += 1
    sv = flat if flat is not None else src[:]
    for c in range(nchunk):
        p.pe(lambda e, o=tp[:, c * 128:(c + 1) * 128], a=sv[:, c * 128:(c + 1) * 128], idn=cx.ident[:]:
             e.transpose(out=o, in_=a, identity=idn), [src, cx.ident], [tp])
    o = dstT[:, c0:c0 + nchunk, col:col + 128]
    i_ = tp[:, 0:nchunk * 128].rearrange("p (c t) -> p c t", c=nchunk)
    if cx.rrt % 2 == 0:
        p.act(lambda e, o=o, i_=i_: e.copy(out=o, in_=i_), [tp], [dstT])
    else:
        p.dve(lambda e, o=o, i_=i_: e.tensor_copy(out=o, in_=i_), [tp], [dstT])


def emit_out(p, cx, src_buf, src_ap, dsts, tmps, npart=128):
    if len(dsts) == 1:
        p.dma(dsts[0], src_ap, [src_buf], [])
        return
    for s_ in range(2):
        t = tmps[s_]
        p.act(lambda e, o=t[:], a=src_ap, m=cx.rk[0:npart, s_:s_ + 1]: e.mul(out=o, in_=a, mul=m), [src_buf, cx.rk], [t])
        p.dma(dsts[s_], t[:], [t], [])


def load_sel(p, cx, dst_buf, dst_ap, srcs, tmp, npart=128):
    if len(srcs) == 1:
        p.dma(dst_ap, srcs[0], [], [dst_buf])
        return
    tb, ta = tmp
    if len(dst_ap.shape) == 4:
        for h_ in range(dst_ap.shape[1]):
            p.dma(dst_ap[:, h_], srcs[0][:, h_], [], [dst_buf])
    else:
        p.dma(dst_ap, srcs[0], [], [dst_buf])
    p.dma(ta, srcs[1], [], [tb])
    p.act(lambda e, o=dst_ap, m=cx.rk[0:npart, 0:1]: e.mul(out=o, in_=o, mul=m), [dst_buf, cx.rk], [dst_buf])
    p.dve(lambda e, o=dst_ap, b=ta, m=cx.rk[0:npart, 1:2]: e.scalar_tensor_tensor(
        out=o, in0=b, scalar=m, in1=o, op0=ALU.mult, op1=ALU.add), [tb, cx.rk, dst_buf], [dst_buf])


def make_ctx(p, ident_ap, rank_ap=None):
    cx = Ctx()
    cx.rr = 0
    cx.rrt = 0
    cx.ident = load_const(p, ident_ap, [128, 128], BF16, "ident")
    cx.eps = p.sb([128, 1], F32, "eps")
    p.dve(lambda e: e.memset(cx.eps[:], EPS), [], [cx.eps])
    cx.small = [p.sb([128, 4], F32, f"small{i}") for i in range(4)]
    cx.small8 = [p.sb([128, 24], F32, f"small8_{i}") for i in range(4)]
    cx.junk = p.sb([128, D], BF16, "junk")
    cx.xn = [p.sb([128, D], BF16, f"xn{i}") for i in range(1)]
    cx.tps = [p.ps(6 + i, BF16, f"tps{i}") for i in range(2)]
    cx.mm = [p.ps(i, F32, f"mm{i}") for i in range(6)]
    cx.stage = [p.sb([128, STW], F32, f"stage{i}") for i in range(2)]
    cx.rk = None
    if rank_ap is not None:
        cx.rk = load_const(p, rank_ap, [128, 2], F32, "rank")
    return cx


TG = 512
NSUB = TG // 128
KC = D // 128


def stage_ffn(p, cx, T, x_in, x_out, gain, wg, wu, wd):
    m = p.mark()
    rr = [0]
    mm_ps = cx.mm
    g_bc = p.sb([128, D], F32, "ffn_g")
    p.dma(g_bc[:], gain.partition_broadcast(128), [], [g_bc])
    Wg = load_weight_bf16(p, wg, D, DFF, "Wg", cx.stage, rr)
    Wu = load_weight_bf16(p, wu, D, DFF, "Wu", cx.stage, rr)
    Wd = load_weight_bf16(p, wd, DFF, D, "Wd", cx.stage, rr)
    nf = DFF // 128
    xpool = [p.sb([128, D], F32, f"ffn_x{i}") for i in range(5)]
    xnT = p.sb([128, KC, TG], BF16, "ffn_xnT")
    hT = p.sb([128, nf, TG], BF16, "ffn_hT")
    sg = [p.sb([128, TG], F32, f"ffn_sg{i}") for i in range(1)]
    xv = x_in.rearrange("(n p) d -> n p d", p=128)
    ov = x_out.rearrange("(n p) d -> n p d", p=128)
    for g in range(T // TG):
        xs = [xpool[(g * NSUB + i) % len(xpool)] for i in range(NSUB)]
        for i in range(NSUB):
            p.dma(xs[i][:], xv[g * NSUB + i], [], [xs[i]])
        rms_to_T(p, cx, xs, g_bc, xnT, 0, NSUB)
        for f in range(nf):
            pg = mm_ps[(2 * f) % len(mm_ps)]
            pu = mm_ps[(2 * f + 1) % len(mm_ps)]
            for k in range(KC):
                p.pe(lambda e, o=pg[:], w=Wg[k][:, f * 128:(f + 1) * 128], a=xnT[:, k, :], k=k:
                     e.matmul(o, lhsT=w, rhs=a, start=(k == 0), stop=(k == KC - 1)), [Wg[k], xnT], [pg])
            for k in range(KC):
                p.pe(lambda e, o=pu[:], w=Wu[k][:, f * 128:(f + 1) * 128], a=xnT[:, k, :], k=k:
                     e.matmul(o, lhsT=w, rhs=a, start=(k == 0), stop=(k == KC - 1)), [Wu[k], xnT], [pu])
            s = sg[f % len(sg)]
            p.act(lambda e, o=s[:], a=pg[:]: e.activation(out=o, in_=a, func=AF.Silu), [pg], [s])
            p.dve(lambda e, o=hT[:, f, :], a=s[:], b=pu[:]: e.tensor_tensor(out=o, in0=a, in1=b, op=ALU.mult),
                  [s, pu], [hT])
        for i in range(NSUB):
            for half in range(2):
                py = mm_ps[(2 * i + half) % len(mm_ps)]
                for f in range(nf):
                    p.pe(lambda e, o=py[:], a=hT[:, f, i * 128:(i + 1) * 128], w=Wd[f][:, half * 512:(half + 1) * 512], f=f:
                         e.matmul(o, lhsT=a, rhs=w, start=(f == 0), stop=(f == nf - 1)), [hT, Wd[f]], [py])
                p.dve(lambda e, o=xs[i][:, half * 512:(half + 1) * 512], a=py[:]: e.scalar_tensor_tensor(
                    out=o, in0=a, scalar=0.5, in1=o, op0=ALU.mult, op1=ALU.add), [py, xs[i]], [xs[i]])
            p.dma(ov[g * NSUB + i], xs[i][:], [xs[i]], [])
    p.barrier()
    p.release(m)


def stage_outproj(p, cx, T, x_in, attn_in, x_out, wo):
    attn_list = list(attn_in) if isinstance(attn_in, (list, tuple)) else [attn_in]
    m = p.mark()
    rr = [0]
    mm_ps = cx.mm
    Wo = load_weight_bf16(p, wo, D, D, "Wo", cx.stage, rr)
    xpool = [p.sb([128, D], F32, f"op_x{i}") for i in range(6)]
    apool = [p.sb([128, D], BF16, f"op_a{i}") for i in range(6)]
    aT = p.sb([128, KC, TG], BF16, "op_aT")
    xv = x_in.rearrange("(n p) d -> n p d", p=128)
    avs = [a.rearrange("(n p) d -> n p d", p=128) for a in attn_list]
    acand = [p.sb([128, D], BF16, f"op_c{i}") for i in range(2)] if len(avs) == 2 else None
    ov = x_out.rearrange("(n p) d -> n p d", p=128)
    for g in range(T // TG):
        xs = [xpool[(g * NSUB + i) % len(xpool)] for i in range(NSUB)]
        as_ = [apool[(g * NSUB + i) % len(apool)] for i in range(NSUB)]
        for i in range(NSUB):
            p.dma(xs[i][:], xv[g * NSUB + i], [], [xs[i]])
            load_sel(p, cx, as_[i], as_[i][:], [a[g * NSUB + i] for a in avs],
                     (acand[i % 2], acand[i % 2][:]) if acand else None)
        for i in range(NSUB):
            transpose_to(p, cx, as_[i], aT, 0, 128 * i, KC)
        for i in range(NSUB):
            for half in range(2):
                py = mm_ps[(2 * i + half) % len(mm_ps)]
                for k in range(KC):
                    p.pe(lambda e, o=py[:], a=aT[:, k, i * 128:(i + 1) * 128], w=Wo[k][:, half * 512:(half + 1) * 512], k=k:
                         e.matmul(o, lhsT=a, rhs=w, start=(k == 0), stop=(k == KC - 1)), [aT, Wo[k]], [py])
                p.dve(lambda e, o=xs[i][:, half * 512:(half + 1) * 512], a=py[:]: e.tensor_tensor(
                    out=o, in0=a, in1=o, op=ALU.add), [py, xs[i]], [xs[i]])
            p.dma(ov[g * NSUB + i], xs[i][:], [xs[i]], [])
    p.barrier()
    p.release(m)


def stage_qkv(p, cx, T, layer, x_in, gain, w_in, gq, gk, cos, sin, qT_out, kT_out, v_out, bf=None, logf_out=None,
              ktok_out=None, pm=False):
    m = p.mark()
    rr = [0]
    mm_ps = cx.mm
    NW = 3 * D + (8 if layer == 1 else 0)
    g_bc = p.sb([128, D], F32, "qkv_g")
    p.dma(g_bc[:], gain.partition_broadcast(128), [], [g_bc])
    Win = load_weight_bf16(p, w_in, D, NW, "Win", cx.stage, rr)
    gains = {}
    for nm, ap, sc in [("q", gq, 0.125), ("k", gk, 1.0)]:
        for j, a in enumerate(ap):
            t = p.sb([128, DH], F32, f"gain_{nm}{j}")
            p.dma(t[:], a.partition_broadcast(128), [], [t])
            if sc != 1.0:
                p.dve(lambda e, t=t, sc=sc: e.tensor_scalar(out=t[:], in0=t[:], scalar1=sc, scalar2=None, op0=ALU.mult),
                      [t], [t])
            gains[(nm, j)] = t
    if layer == 1:
        bf_bc = p.sb([128, 8], F32, "bf_bc")
        p.dma(bf_bc[:], bf.partition_broadcast(128), [], [bf_bc])
        lf = [p.sb([128, 32], F32, f"lf{i}") for i in range(2)]
    cs_pool = [(p.sb([128, 32], F32, f"cos{i}"), p.sb([128, 32], F32, f"sin{i}")) for i in range(2)]
    xpool = [p.sb([128, D], F32, f"qkv_x{i}") for i in range(6)]
    xnT = p.sb([128, KC, TG], BF16, "qkv_xnT")
    qT = p.sb([128, 8, TG], BF16, "qkv_qT")
    kT = p.sb([128, 8, TG], BF16, "qkv_kT")
    vb = [p.sb([128, D], BF16, f"qkv_v{i}") for i in range(2)]
    sq_l = [p.sb([128, 8, DH], F32, f"qkv_sq{i}") for i in range(2)]
    qn_l = [p.sb([128, 8, DH], F32, f"qkv_qn{i}") for i in range(2)]
    tmp_l = [[p.sb([128, 8, 32], F32, f"qkv_t{j}_{i}") for i in range(4)] for j in range(2)]
    qb = [p.sb([128, 8, DH], BF16, f"qkv_qb{i}") for i in range(4)]
    L = lambda a: list(a) if isinstance(a, (list, tuple)) else [a]
    xv = x_in.rearrange("(n p) d -> n p d", p=128)
    if pm:
        vvs = [[a[:, :, n_, :].rearrange("n p d -> p n d") for n_ in range(T // 128)] for a in L(v_out)]
    else:
        vvs = [a.rearrange("(n p) d -> n p d", p=128) for a in L(v_out)]
    cv = cos.rearrange("(n p) d -> n p d", p=128)
    sv = sin.rearrange("(n p) d -> n p d", p=128)
    qTvs = [a.rearrange("(hp two) d t -> (two d) hp t", two=2) for a in L(qT_out)]
    kTvs = [a.rearrange("(hp two) d t -> (two d) hp t", two=2) for a in L(kT_out)]
    two = len(qTvs) == 2
    tq = [p.sb([128, 8, TG], BF16, f"qkv_tq{i}") for i in range(2)] if two else None
    tv = [p.sb([128, NH, DH], BF16, f"qkv_tv{i}") for i in range(2)] if two else None
    tk = [p.sb([128, 8, DH], BF16, f"qkv_tk{i}") for i in range(2)] if two else None
    tl = [p.sb([128, 8], F32, f"qkv_tl{i}") for i in range(2)] if two else None
    if ktok_out is None:
        ktvs = None
    elif pm:
        ktvs = [[a[:, :, n_, :].rearrange("n p d -> p n d") for n_ in range(T // 128)] for a in L(ktok_out)]
    else:
        ktvs = [a.rearrange("(n p) d -> n p d", p=128) for a in L(ktok_out)]
    if layer == 1:
        lvs = L(logf_out)
        lfT = [p.sb([8, TG], F32, f"qkv_lfT{i}") for i in range(2)]
        tlT = [p.sb([8, TG], F32, f"qkv_tlT{i}") for i in range(2)] if two else None
        ident32 = p.sb([128, 128], F32, "ident32")
        p.dve(lambda e: e.tensor_copy(out=ident32[:], in_=cx.ident[:]), [cx.ident], [ident32])
    nb = 0
    nqk = 0
    pending = []
    for g in range(T // TG):
        xs = [xpool[(g * NSUB + i) % len(xpool)] for i in range(NSUB)]
        for i in range(NSUB):
            p.dma(xs[i][:], xv[g * NSUB + i], [], [xs[i]])
        rms_to_T(p, cx, xs, g_bc, xnT, 0, NSUB)
        for i in range(NSUB):
            n = g * NSUB + i
            cosb, sinb = cs_pool[n % 2]
            p.dma(cosb[:], cv[n], [], [cosb])
            p.dma(sinb[:], sv[n], [], [sinb])
            vbuf = vb[n % 2]
            for cg in range(6):
                ps = mm_ps[(nb) % len(mm_ps)]
                nb += 1
                for k in range(KC):
                    p.pe(lambda e, o=ps[:], a=xnT[:, k, i * 128:(i + 1) * 128], w=Win[k][:, cg * 512:(cg + 1) * 512], k=k:
                         e.matmul(o, lhsT=a, rhs=w, start=(k == 0), stop=(k == KC - 1)), [xnT, Win[k]], [ps])
                if cg >= 4:
                    p.act(lambda e, o=vbuf[:, (cg - 4) * 512:(cg - 3) * 512], a=ps[:]: e.copy(out=o, in_=a), [ps], [vbuf])
                    continue
                isq = cg < 2
                hg = cg % 2
                normed = (layer == 1) or (hg == 1)
                roped = (hg == 1)
                qbuf = qb[nqk % 4]
                sq, qn, tmp = sq_l[nqk % 2], qn_l[nqk % 2], tmp_l[nqk % 2]
                nqk += 1
                ps3 = ps[:].rearrange("p (h d) -> p h d", h=8)
                if not normed:
                    p.act(lambda e, o=qbuf[:], a=ps3, sc=(0.125 if isq else 1.0): e.mul(out=o, in_=a, mul=sc), [ps], [qbuf])
                else:
                    gj = 0 if layer == 0 else hg
                    gt = gains[("q" if isq else "k", gj)]
                    ss = cx.small8[cx.rr % len(cx.small8)]
                    cx.rr += 1
                    p.act(lambda e, o=sq[:], a=ps3: e.activation(out=o, in_=a, func=AF.Square), [ps], [sq])
                    p.dve(lambda e, o=ss[:, 0:8], a=sq[:]: e.tensor_reduce(out=o, in_=a, axis=AX.X, op=ALU.add), [sq], [ss])
                    p.act(lambda e, s=ss: e.activation(out=s[:, 8:16], in_=s[:, 0:8], func=AF.Sqrt, scale=1.0 / DH,
                                                       bias=cx.eps[:, 0:1]), [ss, cx.eps], [ss])
                    p.dve(lambda e, s=ss: e.reciprocal(out=s[:, 16:24], in_=s[:, 8:16]), [ss], [ss])
                    p.dve(lambda e, o=qn[:], a=ps3, s=ss: e.tensor_tensor(
                        out=o, in0=a, in1=s[:, 16:24].unsqueeze(2).to_broadcast([128, 8, DH]), op=ALU.mult), [ps, ss], [qn])
                    gbc = gt[:].unsqueeze(1).to_broadcast([128, 8, DH])
                    if not roped:
                        p.dve(lambda e, o=qbuf[:], a=qn[:], g_=gbc: e.tensor_tensor(out=o, in0=a, in1=g_, op=ALU.mult),
                              [qn, gt], [qbuf])
                    else:
                        p.dve(lambda e, o=qn[:], a=qn[:], g_=gbc: e.tensor_tensor(out=o, in0=a, in1=g_, op=ALU.mult),
                              [qn, gt], [qn])
                        cb = cosb[:].unsqueeze(1).to_broadcast([128, 8, 32])
                        sb_ = sinb[:].unsqueeze(1).to_broadcast([128, 8, 32])
                        x1 = qn[:, :, 0:32]
                        x2 = qn[:, :, 32:64]
                        t0, t1, t2, t3 = tmp
                        p.pool(lambda e, o=t0[:], a=x1, b=cb: e.tensor_tensor(out=o, in0=a, in1=b, op=ALU.mult), [qn, cosb], [t0])
                        p.pool(lambda e, o=t1[:], a=x2, b=sb_: e.tensor_tensor(out=o, in0=a, in1=b, op=ALU.mult), [qn, sinb], [t1])
                        p.dve(lambda e, o=t2[:], a=x2, b=cb: e.tensor_tensor(out=o, in0=a, in1=b, op=ALU.mult), [qn, cosb], [t2])
                        p.dve(lambda e, o=t3[:], a=x1, b=sb_: e.tensor_tensor(out=o, in0=a, in1=b, op=ALU.mult), [qn, sinb], [t3])
                        p.dve(lambda e, o=qbuf[:, :, 0:32], a=t0[:], b=t1[:]: e.tensor_tensor(out=o, in0=a, in1=b, op=ALU.subtract),
                              [t0, t1], [qbuf])
                        p.dve(lambda e, o=qbuf[:, :, 32:64], a=t2[:], b=t3[:]: e.tensor_tensor(out=o, in0=a, in1=b, op=ALU.add),
                              [t2, t3], [qbuf])
                def fin(qbuf=qbuf, isq=isq, hg=hg, i=i, n=n):
                    transpose_to(p, cx, qbuf, qT if isq else kT, hg * 4, 128 * i, 4,
                                 flat=qbuf.t.rearrange("p h d -> p (h d)"))
                    if ktvs is not None and (not isq) and hg == 0:
                        emit_out(p, cx, qbuf, qbuf[:] if pm else qbuf.t.rearrange("p h d -> p (h d)"), [a[n] for a in ktvs], tk)
                pending.append(fin)
                while len(pending) > 2:
                    pending.pop(0)()
            if layer == 1:
                ps = mm_ps[(nb) % len(mm_ps)]
                nb += 1
                for k in range(KC):
                    p.pe(lambda e, o=ps[:, 0:8], a=xnT[:, k, i * 128:(i + 1) * 128], w=Win[k][:, 3 * D:3 * D + 8], k=k:
                         e.matmul(o, lhsT=a, rhs=w, start=(k == 0), stop=(k == KC - 1)), [xnT, Win[k]], [ps])
                l = lf[n % 2]
                p.dve(lambda e, o=l[:, 0:8], a=ps[:, 0:8], b=bf_bc[:]: e.tensor_tensor(out=o, in0=a, in1=b, op=ALU.add),
                      [ps, bf_bc], [l])
                p.act(lambda e, l=l: e.activation(out=l[:, 8:16], in_=l[:, 0:8], func=AF.Exp, scale=-1.0), [l], [l])
                p.act(lambda e, l=l: e.activation(out=l[:, 16:24], in_=l[:, 8:16], func=AF.Ln, bias=1.0), [l], [l])
                p.dve(lambda e, l=l: e.tensor_scalar(out=l[:, 24:32], in0=l[:, 16:24], scalar1=-1.0, scalar2=None, op0=ALU.mult),
                      [l], [l])
                pst = mm_ps[(nb) % len(mm_ps)]
                nb += 1
                p.pe(lambda e, o=pst[0:8, 0:128], a=l[:, 24:32]: e.transpose(out=o, in_=a, identity=ident32[:]),
                     [l, ident32], [pst])
                lt = lfT[g % 2]
                p.act(lambda e, o=lt[:, i * 128:(i + 1) * 128], a=pst[0:8, 0:128]: e.copy(out=o, in_=a), [pst], [lt])
                if i == NSUB - 1:
                    emit_out(p, cx, lt, lt[:], [a[:, g * TG:(g + 1) * TG] for a in lvs], tlT, npart=8)
            emit_out(p, cx, vbuf, vbuf[:].rearrange("p (h d) -> p h d", h=NH) if pm else vbuf[:], [a[n] for a in vvs], tv)
        while pending:
            pending.pop(0)()
        emit_out(p, cx, qT, qT[:], [a[:, :, g * TG:(g + 1) * TG] for a in qTvs], tq)
        emit_out(p, cx, kT, kT[:], [a[:, :, g * TG:(g + 1) * TG] for a in kTvs], tq)
    p.barrier()
    p.release(m)


def build_test_tok(T, layer, which):
    nc, stack, p = new_prog()
    with stack:
        dt = lambda n, sh, d=F32, k="ExternalInput": nc.dram_tensor(n, sh, d, kind=k).ap()
        ident = dt("ident", [128, 128], BF16)
        cx = make_ctx(p, ident)
        x = dt("x", [T, D])
        if which == "ffn":
            y = dt("y", [T, D], F32, "ExternalOutput")
            stage_ffn(p, cx, T, x, y, dt("gain", [D]), dt("wg", [D, DFF]), dt("wu", [D, DFF]), dt("wd", [DFF, D]))
        elif which == "outproj":
            y = dt("y", [T, D], F32, "ExternalOutput")
            stage_outproj(p, cx, T, x, dt("attn", [T, D], BF16), y, dt("wo", [D, D]))
        elif which == "qkv":
            NW = 3 * D + (8 if layer == 1 else 0)
            ng = 1 if layer == 0 else 2
            gq = [dt(f"gq{j}", [DH]) for j in range(ng)]
            gk = [dt(f"gk{j}", [DH]) for j in range(ng)]
            qT = dt("qT", [NH, DH, T], BF16, "ExternalOutput")
            kT = dt("kT", [NH, DH, T], BF16, "ExternalOutput")
            v = dt("v", [T, D], BF16, "ExternalOutput")
            kw = {}
            if layer == 1:
                kw = dict(bf=dt("bf", [8]), logf_out=dt("logf", [8, T], F32, "ExternalOutput"))
            stage_qkv(p, cx, T, layer, x, dt("gain", [D]), dt("w_in", [D, NW]), gq, gk,
                      dt("cos", [T, 32]), dt("sin", [T, 32]), qT, kT, v, **kw)
        p.emit()
    return nc


class AttnWork:
    pass


def attn_setup(p, cx, S, layer, consts, fused=False):
    W = AttnWork()
    W.S = S
    W.fused = fused
    if fused:
        W.cq = p.sb([64, S], BF16, "a_cq")
        W.cv = p.sb([128, S // 128, DH], BF16, "a_cv")
        W.otmp = [p.sb([128, 4, DH], BF16, f"a_otmp{i}") for i in range(2)]
        if layer == 0:
            W.J = load_const(p, consts["antiid"], [128, 128], BF16, "antiid")
            W.ktok = p.sb([128, S // 128, DH], BF16, "a_ktok")
            W.vnat = p.sb([128, S // 128, DH], BF16, "a_vnat")
    W.qT = p.sb([64, S], BF16, "a_qT")
    W.kT = p.sb([64, S], BF16, "a_kT")
    W.v = p.sb([128, S // 128, 65], BF16, "a_v")
    p.dve(lambda e: e.memset(W.v[:, :, 64:65], 1.0), [], [W.v])
    W.tri = load_const(p, consts["tri"], [128, 128], BF16, "tri")
    W.osb = [p.sb([128, 4, DH], BF16, f"a_o{i}") for i in range(2)]
    W.rinv = [p.sb([128, 1], F32, f"a_rinv{i}") for i in range(4)]
    W.pt = [p.sb([128, 512], BF16, f"a_pt{i}") for i in range(4)]
    if layer == 0:
        W.sbmask = load_const(p, consts["sbmask"], [128, 128], BF16, "sbmask")
        W.onehot = load_const(p, consts["onehot"], [32, S], BF16, "onehot")
        W.QA = p.sb([32, S], BF16, "a_QA")
        W.ones = p.sb([128, 512], F32, "a_ones")
        p.dve(lambda e: e.memset(W.ones[:], 1.0), [], [W.ones])
        W.e = [p.sb([128, 512], F32, f"a_e{i}") for i in range(4)]
        W.sp = [p.sb([128, 512], F32, f"a_sp{i}") for i in range(4)]
        W.cs = [p.sb([128, 512], F32, f"a_cs{i}") for i in range(4)]
        W.ecs = [p.sb([128, 512], F32, f"a_ecs{i}") for i in range(4)]
        W.a = [p.sb([128, 512], BF16, f"a_a{i}") for i in range(4)]
        W.at = [p.sb([128, 512], BF16, f"a_at{i}") for i in range(2)]
        W.km = p.sb([64, 32], F32, "a_km")
        W.kmh = p.sb([64, 32], BF16, "a_kmh")
        W.kml = p.sb([64, 32], BF16, "a_kml")
        W.gm = p.sb([128, 32], F32, "a_gm")
        W.top8 = [p.sb([128, 8], F32, f"a_top{i}") for i in range(2)]
        W.sel = [p.sb([128, 32], F32, f"a_sel{i}") for i in range(2)]
        W.bsel = [p.sb([128, 32], BF16, f"a_bsel{i}") for i in range(2)]
    else:
        W.dilm = load_const(p, consts["dilm"].rearrange("k p t -> p k t"), [128, 20, 512], BF16, "dilm")
        W.QA = p.sb([6, S], BF16, "a_QA")
        W.KA = p.sb([6, S], BF16, "a_KA")
    return W


def _L(a):
    return list(a) if isinstance(a, (list, tuple)) else [a]


def load_job(p, cx, W, q_ap, k_ap, v_ap):
    tq = (W.cq, W.cq[:]) if W.fused else None
    tv = (W.cv, W.cv[:]) if W.fused else None
    load_sel(p, cx, W.qT, W.qT[:], _L(q_ap), tq, npart=64)
    if k_ap is not None:
        load_sel(p, cx, W.kT, W.kT[:], _L(k_ap), tq, npart=64)
    if v_ap is not None:
        _load_tok(p, cx, W, W.v, W.v[:, :, 0:DH], v_ap)


def _load_tok(p, cx, W, dst_buf, dst_ap, src):
    srcs = _L(src)
    if len(srcs[0].shape) == 2:
        load_sel(p, cx, dst_buf, dst_ap, [a.rearrange("(k p) d -> p k d", p=128) for a in srcs],
                 (W.cv, W.cv[:]) if W.fused else None)
    else:
        r = lambda ap: ap.rearrange("p (h k) d -> p h k d", h=2)
        load_sel(p, cx, dst_buf, r(dst_ap), srcs, (W.cv, r(W.cv[:])))


def store_out(p, cx, W, osb, dsts):
    emit_out(p, cx, osb, osb[:], dsts, W.otmp if W.fused else None)


def sb_reverse(p, cx, W, ktok_ap, v_ap):
    S = W.S
    nb = S // 128
    _load_tok(p, cx, W, W.ktok, W.ktok[:], ktok_ap)
    _load_tok(p, cx, W, W.vnat, W.vnat[:], v_ap)
    for r0 in range(0, nb, 4):
        ps = p.ps(r0 // 4 % 2, F32)
        for rb in range(r0, r0 + 4):
            k = nb - 1 - rb
            p.pe(lambda e, o=ps[0:DH, (rb - r0) * 128:(rb - r0 + 1) * 128], a=W.ktok[:, k, :]:
                 e.matmul(o, lhsT=a, rhs=W.J[:], start=True, stop=True), [W.ktok, W.J], [ps])
        p.act(lambda e, o=W.kT[:, r0 * 128:(r0 + 4) * 128], a=ps[0:DH, :]: e.copy(out=o, in_=a), [ps], [W.kT])
    for r0 in range(0, nb, 8):
        ps = p.ps(2 + r0 // 8 % 2, F32)
        for rb in range(r0, r0 + 8):
            k = nb - 1 - rb
            p.pe(lambda e, o=ps[:, (rb - r0) * DH:(rb - r0 + 1) * DH], a=W.vnat[:, k, :]:
                 e.matmul(o, lhsT=W.J[:], rhs=a, start=True, stop=True), [W.vnat, W.J], [ps])
        p.dve(lambda e, o=W.v[:, r0:r0 + 8, 0:DH], a=ps[:].rearrange("p (k d) -> p k d", d=DH): e.tensor_copy(out=o, in_=a),
              [ps], [W.v])


def sb_job(p, cx, W, q_ap, k_ap, v_ap, o_ap, ktok_ap=None):
    S = W.S
    if ktok_ap is None:
        load_job(p, cx, W, q_ap, k_ap, v_ap)
    else:
        load_job(p, cx, W, q_ap, None, None)
        sb_reverse(p, cx, W, ktok_ap, v_ap)
    nb = S // 128
    ovs = [a.rearrange("(g q p) d -> g p q d", p=128, q=4) for a in _L(o_ap)]
    NS = len(W.e)
    LAG = NS - 1
    units = []
    for i in range(nb):
        rb0 = nb - 1 - i
        nun = (128 * (i + 1) + 511) // 512
        for u in range(nun):
            c0 = rb0 * 128 + 512 * u
            units.append(dict(i=i, u=u, nun=nun, c0=c0, N=min(512, S - c0), idx=len(units)))

    def stage_a(t):
        i, u, c0, N, k = t["i"], t["u"], t["c0"], t["N"], t["idx"]
        z = p.ps(k % 2, F32)
        p.pe(lambda e, o=z[:, 0:N], a=W.qT[:, i * 128:(i + 1) * 128], b=W.kT[:, c0:c0 + N], u=u:
             e.matmul(o, lhsT=a, rhs=b, start=True, stop=(u != 0)), [W.qT, W.kT], [z])
        if u == 0:
            p.pe(lambda e, o=z[:, 0:128]: e.matmul(o, lhsT=cx.ident[:], rhs=W.sbmask[:], start=False, stop=True),
                 [cx.ident, W.sbmask], [z])
        e_sb, sp, cs, ecs, a = (W.e[k % NS], W.sp[k % NS], W.cs[k % NS], W.ecs[k % NS], W.a[k % NS])
        p.act(lambda e, o=e_sb[:, 0:N], i_=z[:, 0:N]: e.activation(out=o, in_=i_, func=AF.Exp), [z], [e_sb])
        p.act(lambda e, o=sp[:, 0:N], i_=e_sb[:, 0:N]: e.activation(out=o, in_=i_, func=AF.Ln, bias=1.0), [e_sb], [sp])
        if u == 0:
            init, rd = 0.0, [W.ones, sp]
        else:
            cprev = W.cs[(k - 1) % NS]
            pN = units[k - 1]["N"]
            init, rd = cprev[:, pN - 1:pN], [W.ones, sp, cprev]
        p.dve(lambda e, o=cs[:, 0:N], d0=W.ones[:, 0:N], d1=sp[:, 0:N], init=init: e.tensor_tensor_scan(
            out=o, data0=d0, data1=d1, initial=init, op0=ALU.mult, op1=ALU.add), rd, [cs])

    def stage_a2(t):
        N, k = t["N"], t["idx"]
        e_sb, cs, ecs, a = (W.e[k % NS], W.cs[k % NS], W.ecs[k % NS], W.a[k % NS])
        p.act(lambda e, o=ecs[:, 0:N], i_=cs[:, 0:N]: e.activation(out=o, in_=i_, func=AF.Exp, scale=-1.0), [cs], [ecs])
        p.dve(lambda e, o=a[:, 0:N], x=e_sb[:, 0:N], y=ecs[:, 0:N]: e.tensor_tensor(out=o, in0=x, in1=y, op=ALU.mult),
              [e_sb, ecs], [a])

    def stage_b(t):
        i, u, nun, c0, N, k = t["i"], t["u"], t["nun"], t["c0"], t["N"], t["idx"]
        nk = N // 128
        a = W.a[k % NS]
        at = W.at[k % 2]
        o_ps = p.ps(4 + i % 2, F32)
        atp = p.ps(2 + k % 2, BF16)
        for kb in range(nk):
            p.pe(lambda e, o=atp[:, kb * 128:(kb + 1) * 128], x=a[:, kb * 128:(kb + 1) * 128]:
                 e.transpose(out=o, in_=x, identity=cx.ident[:]), [a, cx.ident], [atp])
        p.act(lambda e, o=at[:, 0:N], x=atp[:, 0:N]: e.copy(out=o, in_=x), [atp], [at])
        for kb in range(nk):
            p.pe(lambda e, o=o_ps[:, 0:DH], x=at[:, kb * 128:(kb + 1) * 128], vv=W.v[:, c0 // 128 + kb, 0:DH],
                 st=(u == 0 and kb == 0), sp_=(u == nun - 1 and kb == nk - 1):
                 e.matmul(o, lhsT=x, rhs=vv, start=st, stop=sp_), [at, W.v], [o_ps])
        if u == nun - 1:
            osb = W.osb[(i // 4) % 2]
            p.act(lambda e, o=osb[:, i % 4, :], x=o_ps[:, 0:DH]: e.copy(out=o, in_=x), [o_ps], [osb])
            if i % 4 == 3:
                store_out(p, cx, W, osb, [a_[i // 4] for a_ in ovs])

    nu = len(units)
    for k in range(nu + LAG):
        if k < nu:
            stage_a(units[k])
        if 0 <= k - 1 < nu:
            stage_a2(units[k - 1])
        if 0 <= k - LAG < nu:
            stage_b(units[k - LAG])


def moba_pre(p, cx, W):
    S = W.S
    nkb = S // 256
    p.dve(lambda e: e.tensor_reduce(out=W.km[:, 0:nkb], in_=W.kT[:].rearrange("p (n s) -> p n s", s=256),
                                    axis=AX.X, op=ALU.add), [W.kT], [W.km])
    p.dve(lambda e: e.tensor_scalar(out=W.km[:, 0:nkb], in0=W.km[:, 0:nkb], scalar1=1.0 / 256, scalar2=None, op0=ALU.mult),
          [W.km], [W.km])
    p.dve(lambda e: e.tensor_copy(out=W.kmh[:, 0:nkb], in_=W.km[:, 0:nkb]), [W.km], [W.kmh])
    p.dve(lambda e: e.tensor_tensor(out=W.kml[:, 0:nkb], in0=W.km[:, 0:nkb], in1=W.kmh[:, 0:nkb], op=ALU.subtract),
          [W.km, W.kmh], [W.kml])
    p.dve(lambda e: e.memset(W.gm[:], -1e30), [], [W.gm])
    pend = []
    for i in range(S // 128):
        own = i // 2
        bsel = W.bsel[i % 2]
        if own == 0:
            p.dve(lambda e, b=bsel: e.memset(b[:], 0.0), [], [bsel])
        else:
            gp = p.ps(4 + i % 2, F32)
            p.pe(lambda e, o=gp[:, 0:nkb], a=W.qT[:, i * 128:(i + 1) * 128]: e.matmul(o, lhsT=a, rhs=W.kmh[:, 0:nkb], start=True, stop=False),
                 [W.qT, W.kmh], [gp])
            p.pe(lambda e, o=gp[:, 0:nkb], a=W.qT[:, i * 128:(i + 1) * 128]: e.matmul(o, lhsT=a, rhs=W.kml[:, 0:nkb], start=False, stop=True),
                 [W.qT, W.kml], [gp])
            p.dve(lambda e, o=W.gm[:, 0:own], a=gp[:, 0:own]: e.tensor_copy(out=o, in_=a), [gp], [W.gm])
            top = W.top8[i % 2]
            sel = W.sel[i % 2]
            p.dve(lambda e, t=top: e.max(out=t[:], in_=W.gm[:]), [W.gm], [top])
            p.dve(lambda e, t=top: e.tensor_scalar(out=t[:, 3:4], in0=t[:, 2:3], scalar1=-1e29, scalar2=None, op0=ALU.max),
                  [top], [top])
            p.dve(lambda e, s=sel, t=top: e.tensor_scalar(out=s[:], in0=W.gm[:], scalar1=t[:, 3:4], scalar2=None, op0=ALU.is_ge),
                  [W.gm, top], [sel])
            p.dve(lambda e, s=sel, b=bsel: e.tensor_scalar(out=b[:], in0=s[:], scalar1=-NEG, scalar2=NEG, op0=ALU.mult, op1=ALU.add),
                  [sel], [bsel])
            p.dve(lambda e, b=bsel, own=own: e.memset(b[:, own:own + 1], 0.0), [], [bsel])
        def fin(i=i, bsel=bsel):
            tp = cx.tps[i % 2]
            p.pe(lambda e, o=tp[0:32, 0:128], b=bsel: e.transpose(out=o, in_=b[:], identity=cx.ident[:]), [bsel, cx.ident], [tp])
            p.act(lambda e, o=W.QA[:, i * 128:(i + 1) * 128], x=tp[0:32, 0:128]: e.copy(out=o, in_=x), [tp], [W.QA])
        pend.append(fin)
        while len(pend) > 1:
            pend.pop(0)()
    while pend:
        pend.pop(0)()


def softmax_job(p, cx, W, kind, q_ap, k_ap, v_ap, o_ap, fox_rows=None):
    S = W.S
    load_job(p, cx, W, q_ap, k_ap, v_ap)
    QA = KA = None
    if kind == "moba":
        moba_pre(p, cx, W)
        QA, KA = W.QA, W.onehot
    elif kind == "fox":
        p.dve(lambda e: e.memset(W.QA[:], 1.0), [], [W.QA])
        p.dve(lambda e: e.memset(W.KA[:], 1.0), [], [W.KA])
        p.dma(W.QA[0:3, :], fox_rows[0:3, :], [], [W.QA])
        p.dma(W.KA[3:6, :], fox_rows[3:6, :], [], [W.KA])
        QA, KA = W.QA, W.KA
    ovs = [a.rearrange("(g q p) d -> g p q d", p=128, q=4) for a in _L(o_ap)]
    units = []
    for g in range(S // 512):
        jlo = max(0, 4 * g - 16) if kind == "dil" else 0
        for j in range(jlo, 4 * g + 4):
            units.append(dict(g=g, j=j, jlo=jlo, idx=len(units)))
    NP = len(W.pt)

    def stage_a(t):
        g, j, k = t["g"], t["j"], t["idx"]
        r = max(0, j - 4 * g)
        c0 = 128 * r
        N = 512 - c0
        sps = p.ps(k % 2, F32)
        q0 = g * 512 + c0
        diag = (kind != "dil") and j >= 4 * g
        p.pe(lambda e, o=sps[:, 0:N], a=W.kT[:, j * 128:(j + 1) * 128], b=W.qT[:, q0:q0 + N], sp_=(QA is None and not diag):
             e.matmul(o, lhsT=a, rhs=b, start=True, stop=sp_), [W.kT, W.qT], [sps])
        if QA is not None:
            p.pe(lambda e, o=sps[:, 0:N], a=KA[:, j * 128:(j + 1) * 128], b=QA[:, q0:q0 + N], sp_=(not diag):
                 e.matmul(o, lhsT=a, rhs=b, start=False, stop=sp_), [KA, QA], [sps])
        if diag:
            p.pe(lambda e, o=sps[:, 0:128]: e.matmul(o, lhsT=cx.ident[:], rhs=W.tri[:], start=False, stop=True),
                 [cx.ident, W.tri], [sps])
        pt = W.pt[k % NP]
        p.act(lambda e, o=pt[:, 0:N], x=sps[:, 0:N]: e.activation(out=o, in_=x, func=AF.Exp), [sps], [pt])
        if kind == "dil":
            dm = W.dilm[:, 4 * g - j + 3, c0:512]
            p.dve(lambda e, o=pt[:, 0:N], m_=dm: e.tensor_tensor(out=o, in0=o, in1=m_, op=ALU.mult), [pt, W.dilm], [pt])

    def stage_b(t):
        g, j, jlo, k = t["g"], t["j"], t["jlo"], t["idx"]
        r = max(0, j - 4 * g)
        pt = W.pt[k % NP]
        accs = [p.ps(2 + qb, F32) for qb in range(4)]
        osb = W.osb[g % 2]
        for qb in range(r, 4):
            lc = (qb - r) * 128
            last = (j == 4 * g + qb)
            p.pe(lambda e, o=accs[qb][:, 0:65], x=pt[:, lc:lc + 128], vv=W.v[:, j, :], st=(j == jlo), sp_=last:
                 e.matmul(o, lhsT=x, rhs=vv, start=st, stop=sp_), [pt, W.v], [accs[qb]])
            if last:
                rv = W.rinv[qb]
                p.dve(lambda e, o=rv[:], x=accs[qb][:, 64:65]: e.reciprocal(out=o, in_=x), [accs[qb]], [rv])
                p.dve(lambda e, o=osb[:, qb, :], x=accs[qb][:, 0:DH], s_=rv[:, 0:1]: e.tensor_scalar(
                    out=o, in0=x, scalar1=s_, scalar2=None, op0=ALU.mult), [accs[qb], rv], [osb])
        if j == 4 * g + 3:
            store_out(p, cx, W, osb, [a_[g] for a_ in ovs])

    LAG = 2
    for k, t in enumerate(units):
        stage_a(t)
        if k >= LAG:
            stage_b(units[k - LAG])
    for t in units[max(0, len(units) - LAG):]:
        stage_b(t)


def fox_pre(p, cx, S, logf4, caug):
    m = p.mark()
    PW = min(2048, S)
    lf = [p.sb([4, PW], F32, f"fx_lf{i}") for i in range(2)]
    ones = p.sb([4, PW], F32, "fx_ones")
    c = [p.sb([4, PW], F32, f"fx_c{i}") for i in range(2)]
    r1 = p.sb([4, PW], F32, "fx_r1")
    r2 = p.sb([4, PW], F32, "fx_r2")
    rows = [[p.sb([4, PW], BF16, f"fx_row{k}_{i}") for k in range(6)] for i in range(2)]
    p.dve(lambda e: e.memset(ones[:], 1.0), [], [ones])
    prev = None
    for pc in range(S // PW):
        l = lf[pc % 2]
        cc = c[pc % 2]
        rw = rows[pc % 2]
        load_sel(p, cx, l, l[:], [a[:, pc * PW:(pc + 1) * PW] for a in _L(logf4)], (r1, r1[:]), npart=4)
        init = 0.0 if prev is None else prev[:, PW - 1:PW]
        rd = [ones, l] + ([prev] if prev is not None else [])
        p.dve(lambda e, o=cc[:], d1=l[:], init=init: e.tensor_tensor_scan(out=o, data0=ones[:], data1=d1, initial=init,
                                                                         op0=ALU.mult, op1=ALU.add), rd, [cc])
        prev = cc
        hi, mid, lo, nhi, nmid, nlo = rw
        p.dve(lambda e, o=hi[:], x=cc[:]: e.tensor_copy(out=o, in_=x), [cc], [hi])
        p.dve(lambda e, o=r1[:], x=cc[:], y=hi[:]: e.tensor_tensor(out=o, in0=x, in1=y, op=ALU.subtract), [cc, hi], [r1])
        p.dve(lambda e, o=mid[:]: e.tensor_copy(out=o, in_=r1[:]), [r1], [mid])
        p.dve(lambda e, o=r2[:], y=mid[:]: e.tensor_tensor(out=o, in0=r1[:], in1=y, op=ALU.subtract), [r1, mid], [r2])
        p.dve(lambda e, o=lo[:]: e.tensor_copy(out=o, in_=r2[:]), [r2], [lo])
        for src, dst in ((hi, nhi), (mid, nmid), (lo, nlo)):
            p.dve(lambda e, o=dst[:], x=src[:]: e.tensor_scalar(out=o, in0=x, scalar1=-1.0, scalar2=None, op0=ALU.mult),
                  [src], [dst])
        for k in range(6):
            p.dma(caug[:, k, pc * PW:(pc + 1) * PW], rw[k][:], [rw[k]], [])
    p.barrier()
    p.release(m)


def build_attn(S, layer, njobs=4, only=None):
    nc, stack, p = new_prog()
    with stack:
        dt = lambda n, sh, d=BF16, k="ExternalInput": nc.dram_tensor(n, sh, d, kind=k).ap()
        ident = dt("ident", [128, 128])
        cx = make_ctx(p, ident)
        consts = {"tri": dt("tri", [128, 128])}
        if layer == 0:
            consts["sbmask"] = dt("sbmask", [128, 128])
            consts["onehot"] = dt("onehot", [32, S])
        else:
            consts["dilm"] = dt("dilm", [20, 128, 512])
            logf4 = dt("logf4", [4, S], F32)
            caug = nc.dram_tensor("caug", [4, 6, S], BF16).ap()
            fox_pre(p, cx, S, logf4, caug)
        W = attn_setup(p, cx, S, layer, consts)
        dq, dk, dv = dt("dq", [njobs, DH, S]), dt("dk", [njobs, DH, S]), dt("dv", [njobs, S, DH])
        sq, sk, sv = dt("sq", [njobs, DH, S]), dt("sk", [njobs, DH, S]), dt("sv", [njobs, S, DH])
        od = dt("od", [njobs, S, DH], BF16, "ExternalOutput")
        os_ = dt("os", [njobs, S, DH], BF16, "ExternalOutput")
        for jb in range(njobs):
            if layer == 0:
                if only in (None, "d"):
                    sb_job(p, cx, W, dq[jb], dk[jb], dv[jb], od[jb])
                if only in (None, "s"):
                    softmax_job(p, cx, W, "moba", sq[jb], sk[jb], sv[jb], os_[jb])
            else:
                if only in (None, "d"):
                    softmax_job(p, cx, W, "fox", dq[jb], dk[jb], dv[jb], od[jb], fox_rows=caug[jb])
                if only in (None, "s"):
                    softmax_job(p, cx, W, "dil", sq[jb], sk[jb], sv[jb], os_[jb])
        p.barrier()
        p.emit()
    return nc


def attn_consts_np(S, layer):
    ii = np.arange(128)
    c = {"ident": np.eye(128, dtype=np.float32).astype(NPBF),
         "tri": np.where(ii[:, None] <= ii[None, :], 0.0, NEG).astype(np.float32).astype(NPBF)}
    if layer == 0:
        c["sbmask"] = np.where(ii[None, :] + ii[:, None] >= 128, 0.0, NEG).astype(np.float32).astype(NPBF)
        c["onehot"] = (np.arange(S)[None, :] // 256 == np.arange(32)[:, None]).astype(np.float32).astype(NPBF)
    else:
        dm = np.zeros((20, 128, 512), np.float32)
        ss = np.arange(128)[:, None]
        tt = np.arange(512)[None, :]
        for d in range(-3, 17):
            o = 128 * d + tt - ss
            m = np.zeros_like(o, dtype=np.float32)
            for w, r in ((128, 1), (512, 4), (2048, 16)):
                m += ((o >= 0) & (o <= w) & (o % r == 0)).astype(np.float32)
            dm[d + 3] = m
        c["dilm"] = dm.astype(NPBF)
    return c


def build_tok_launch(T, plan):
    nc, stack, p = new_prog()
    with stack:
        dt = lambda n, sh, d=F32, k="ExternalInput": nc.dram_tensor(n, sh, d, kind=k).ap()
        ident = dt("ident", [128, 128], BF16)
        cx = make_ctx(p, ident)
        cur = dt("x", [T, D])
        for si, st in enumerate(plan):
            last_x = not any(s_ in ("outproj", "ffn") for s_ in plan[si + 1:])
            if st in ("outproj", "ffn"):
                nxt = (dt(f"xo{si}", [T, D], F32, "ExternalOutput") if last_x
                       else nc.dram_tensor(f"xs{si}", [T, D], F32).ap())
            if st == "ffn":
                stage_ffn(p, cx, T, cur, nxt, dt(f"gain{si}", [D]), dt(f"wg{si}", [D, DFF]), dt(f"wu{si}", [D, DFF]),
                          dt(f"wd{si}", [DFF, D]))
                cur = nxt
            elif st == "outproj":
                stage_outproj(p, cx, T, cur, dt(f"attn{si}", [T, D], BF16), nxt, dt(f"wo{si}", [D, D]))
                cur = nxt
            else:
                layer = int(st[-1])
                NW = 3 * D + (8 if layer == 1 else 0)
                ng = 1 if layer == 0 else 2
                gq = [dt(f"gq{j}", [DH]) for j in range(ng)]
                gk = [dt(f"gk{j}", [DH]) for j in range(ng)]
                kw = {}
                if layer == 1:
                    kw = dict(bf=dt("bf", [8]), logf_out=dt("logf", [8, T], F32, "ExternalOutput"))
                stage_qkv(p, cx, T, layer, cur, dt(f"gain{si}", [D]), dt("w_in", [D, NW]), gq, gk,
                          dt("cos", [T, 32]), dt("sin", [T, 32]),
                          dt("qT", [NH, DH, T], BF16, "ExternalOutput"), dt("kT", [NH, DH, T], BF16, "ExternalOutput"),
                          dt("v", [T, D], BF16, "ExternalOutput"), **kw)
        p.emit()
    return nc


def _run(nc, in_maps):
    in_maps = [{k: np.ascontiguousarray(v) for k, v in m.items()} for m in in_maps]
    return run_bass_kernel_spmd(nc, in_maps, core_ids=list(range(len(in_maps)))).results


def kernel_unfused(x, norm_ffn1, ffn1_w_gate, ffn1_w_up, ffn1_w_down, norm_mix, w_in_ab, g_q_b, g_k_b,
           w_in_cd, b_f, g_q_c, g_k_c, g_q_d, g_k_d, w_out, norm_ffn2, ffn2_w_gate, ffn2_w_up,
           ffn2_w_down):
    A = lambda a: np.asarray(a, dtype=np.float32)
    x = A(x)
    B, S, _ = x.shape
    NC = 8
    T = B * S // NC
    half_of = lambda c: (c // 2, c % 2)
    ident = np.eye(128, dtype=np.float32).astype(NPBF)
    inv = (1.0 / (10000.0 ** (np.arange(0, DH, 2, dtype=np.float32) / DH))).astype(np.float32)
    ang = (np.arange(S, dtype=np.float32)[:, None] * inv[None, :]).astype(np.float32)
    cos, sin = np.cos(ang).astype(np.float32), np.sin(ang).astype(np.float32)

    def tok_shard(c, arr):
        b, h = half_of(c)
        return arr[b, h * T:(h + 1) * T]

    def ffn_w(si, norm, wg, wu, wd, l):
        return {f"gain{si}": A(norm[l]), f"wg{si}": A(wg[l]), f"wu{si}": A(wu[l]), f"wd{si}": A(wd[l])}

    def attn_inputs(layer, res):
        consts = attn_consts_np(S, layer)
        maps = []
        for c in range(NC):
            b, h = half_of(c)
            QT = np.concatenate([res[2 * b]["qT"], res[2 * b + 1]["qT"]], axis=2)
            KT = np.concatenate([res[2 * b]["kT"], res[2 * b + 1]["kT"]], axis=2)
            V = np.concatenate([res[2 * b]["v"], res[2 * b + 1]["v"]], axis=0)
            V = V.reshape(S, NH, DH).transpose(1, 0, 2)
            hd = slice(4 * h, 4 * h + 4)
            hs = slice(8 + 4 * h, 12 + 4 * h)
            m = dict(consts)
            if layer == 0:
                m.update(dq=QT[hd], dk=KT[hd][:, :, ::-1], dv=V[hd][:, ::-1, :])
            else:
                LF = np.concatenate([res[2 * b]["logf"], res[2 * b + 1]["logf"]], axis=1)
                m.update(dq=QT[hd], dk=KT[hd], dv=V[hd], logf4=LF[hd])
            m.update(sq=QT[hs], sk=KT[hs], sv=V[hs])
            maps.append(m)
        return maps

    def attn_gather(res):
        outs = []
        for b in range(B):
            full = np.empty((S, NH, DH), dtype=NPBF)
            for h in range(2):
                r = res[2 * b + h]
                full[:, 4 * h:4 * h + 4] = np.asarray(r["od"]).transpose(1, 0, 2)
                full[:, 8 + 4 * h:12 + 4 * h] = np.asarray(r["os"]).transpose(1, 0, 2)
            full = full.reshape(S, D)
            outs += [full[0:T], full[T:2 * T]]
        return outs

    ncA = build_tok_launch(T, ["ffn", "qkv0"])
    mA = []
    for c in range(NC):
        m = {"ident": ident, "x": tok_shard(c, x), "w_in": A(w_in_ab[0]), "gq0": A(g_q_b[0]), "gk0": A(g_k_b[0]),
             "gain1": A(norm_mix[0]), "cos": cos[(c % 2) * T:(c % 2 + 1) * T], "sin": sin[(c % 2) * T:(c % 2 + 1) * T]}
        m.update(ffn_w(0, norm_ffn1, ffn1_w_gate, ffn1_w_up, ffn1_w_down, 0))
        mA.append(m)
    rA = _run(ncA, mA)
    ncB = build_attn(S, 0)
    rB = _run(ncB, attn_inputs(0, rA))
    att0 = attn_gather(rB)
    ncC = build_tok_launch(T, ["outproj", "ffn", "ffn", "qkv1"])
    mC = []
    for c in range(NC):
        m = {"ident": ident, "x": rA[c]["xo0"], "attn0": att0[c], "wo0": A(w_out[0]),
             "w_in": A(w_in_cd[0]), "gq0": A(g_q_c[0]), "gk0": A(g_k_c[0]), "gq1": A(g_q_d[0]), "gk1": A(g_k_d[0]),
             "bf": A(b_f[0]), "gain3": A(norm_mix[1]),
             "cos": cos[(c % 2) * T:(c % 2 + 1) * T], "sin": sin[(c % 2) * T:(c % 2 + 1) * T]}
        m.update(ffn_w(1, norm_ffn2, ffn2_w_gate, ffn2_w_up, ffn2_w_down, 0))
        m.update(ffn_w(2, norm_ffn1, ffn1_w_gate, ffn1_w_up, ffn1_w_down, 1))
        mC.append(m)
    rC = _run(ncC, mC)
    ncD = build_attn(S, 1)
    rD = _run(ncD, attn_inputs(1, rC))
    att1 = attn_gather(rD)
    ncE = build_tok_launch(T, ["outproj", "ffn"])
    mE = []
    for c in range(NC):
        m = {"ident": ident, "x": rC[c]["xo2"], "attn0": att1[c], "wo0": A(w_out[1])}
        m.update(ffn_w(1, norm_ffn2, ffn2_w_gate, ffn2_w_up, ffn2_w_down, 1))
        mE.append(m)
    rE = _run(ncE, mE)
    out = np.empty((B, S, D), dtype=np.float32)
    for c in range(NC):
        b, h = half_of(c)
        out[b, h * T:(h + 1) * T] = rE[c]["xo1"]
    return out


RG_PAIRS = [[0, 1], [2, 3], [4, 5], [6, 7]]


class XBuf:
    def __init__(self, nc, name, nelem, dt, pattern, **axes):
        ce = min(nelem, 128 * 16384)
        self.nch = nelem // ce
        assert self.nch * ce == nelem and ce % 128 == 0
        self.i = nc.dram_tensor(name + "_i", [self.nch, 128, ce // 128], dt)
        self.o = nc.dram_tensor(name + "_o", [self.nch, 128, ce // 128], dt)
        self.vi = self.i.ap().rearrange("c p f -> (c p f)").rearrange(pattern, **axes)
        self.vo = self.o.ap().rearrange("c p f -> (c p f)").rearrange(pattern, **axes)

    def exchange(self, p, skip_first_half=False):
        for c in range(self.nch):
            if skip_first_half and self.nch % 2 == 0 and c < self.nch // 2:
                continue
            p.collective(self.i[c], self.o[c], RG_PAIRS)


def build_fused(T, S):
    nc, stack, p = new_prog()
    with stack:
        dt = lambda n, sh, d=F32, k="ExternalInput": nc.dram_tensor(n, sh, d, kind=k).ap()
        ident = dt("ident", [128, 128], BF16)
        cx = make_ctx(p, ident, dt("rank", [128, 2]))
        x = dt("x", [T, D])
        y = dt("y", [T, D], F32, "ExternalOutput")
        cos, sin = dt("cos", [T, 32]), dt("sin", [T, 32])
        consts0 = {"tri": dt("tri", [128, 128], BF16), "sbmask": dt("sbmask", [128, 128], BF16),
                   "onehot": dt("onehot", [32, S], BF16), "antiid": dt("antiid", [128, 128], BF16)}
        consts1 = {"tri": consts0["tri"], "dilm": dt("dilm", [20, 128, 512], BF16)}
        ffn = [dict(gain=dt(f"gain{i}", [D]), wg=dt(f"wg{i}", [D, DFF]), wu=dt(f"wu{i}", [D, DFF]), wd=dt(f"wd{i}", [DFF, D]))
               for i in range(4)]
        nmix = [dt(f"nmix{i}", [D]) for i in range(2)]
        w_in = [dt("w_in0", [D, 3 * D]), dt("w_in1", [D, 3 * D + 8])]
        wo = [dt(f"wo{i}", [D, D]) for i in range(2)]
        gqb, gkb = dt("gqb", [DH]), dt("gkb", [DH])
        gqc, gkc, gqd, gkd = dt("gqc", [DH]), dt("gkc", [DH]), dt("gqd", [DH]), dt("gkd", [DH])
        bf = dt("bf", [8])
        xs = [nc.dram_tensor(f"xs{i}", [T, D], F32).ap() for i in range(5)]
        EQ = XBuf(nc, "eq", NH * DH * S, BF16, "(n d h t) -> n d h t", n=NH, d=DH, h=2)
        EK = XBuf(nc, "ek", NH * DH * S, BF16, "(n d h t) -> n d h t", n=NH, d=DH, h=2)
        EV = XBuf(nc, "ev", S * D, BF16, "(h n p k d) -> h n p k d", h=2, n=NH, p=128, d=DH)
        EKT = XBuf(nc, "ekt", S * 512, BF16, "(h n p k d) -> h n p k d", h=2, n=8, p=128, d=DH)
        ELF = XBuf(nc, "elf", 8 * S, F32, "(n h t) -> n h t", n=8, h=2)
        EA = XBuf(nc, "ea", S * D, BF16, "(h t c) -> h t c", h=2, c=D)
        caug = nc.dram_tensor("caug", [4, 6, S], BF16).ap()

        def exchange(bufs, skip=()):
            for b_ in bufs:
                b_.exchange(p, skip_first_half=(b_ in skip))
            p.barrier()

        def head_q(E, hh):
            return E.vo[hh].rearrange("d h t -> d (h t)")

        def cols(E, hh, out=False):
            v = (E.vi if out else E.vo).rearrange("h t c -> (h t) c")
            return v[:, hh * DH:(hh + 1) * DH]

        def pmv(E, hh):
            return E.vo[:, hh].rearrange("h p k d -> p h k d")

        def attention(layer):
            m = p.mark()
            if layer == 1:
                lf = ELF.vo.rearrange("n h t -> n (h t)")
                fox_pre(p, cx, S, [lf[0:4], lf[4:8]], caug)
            W = attn_setup(p, cx, S, layer, consts0 if layer == 0 else consts1, fused=True)
            for j in range(4):
                d0, d1, s0, s1 = j, 4 + j, 8 + j, 12 + j
                if layer == 0:
                    sb_job(p, cx, W, [head_q(EQ, d0), head_q(EQ, d1)], None, [pmv(EV, d0), pmv(EV, d1)],
                           [cols(EA, d0, True), cols(EA, d1, True)], ktok_ap=[pmv(EKT, d0), pmv(EKT, d1)])
                    kind = "moba"
                else:
                    softmax_job(p, cx, W, "fox", [head_q(EQ, d0), head_q(EQ, d1)], [head_q(EK, d0), head_q(EK, d1)],
                                [pmv(EV, d0), pmv(EV, d1)], [cols(EA, d0, True), cols(EA, d1, True)], fox_rows=caug[j])
                    kind = "dil"
                softmax_job(p, cx, W, kind, [head_q(EQ, s0), head_q(EQ, s1)], [head_q(EK, s0), head_q(EK, s1)],
                            [pmv(EV, s0), pmv(EV, s1)], [cols(EA, s0, True), cols(EA, s1, True)])
            p.barrier()
            p.release(m)

        def qkv(layer, xin):
            kw = {"pm": True}
            if layer == 0:
                gq, gk = [gqb], [gkb]
                kw["ktok_out"] = [EKT.vi[0], EKT.vi[1]]
            else:
                gq, gk = [gqc, gqd], [gkc, gkd]
                kw.update(bf=bf, logf_out=[ELF.vi[:, 0, :], ELF.vi[:, 1, :]])
            stage_qkv(p, cx, T, layer, xin, nmix[layer], w_in[layer], gq, gk, cos, sin,
                      [EQ.vi[:, :, 0, :], EQ.vi[:, :, 1, :]], [EK.vi[:, :, 0, :], EK.vi[:, :, 1, :]],
                      [EV.vi[0], EV.vi[1]], **kw)

        stage_ffn(p, cx, T, x, xs[0], **ffn[0])
        qkv(0, xs[0])
        exchange([EQ, EK, EV, EKT], skip=(EK,))
        attention(0)
        exchange([EA])
        stage_outproj(p, cx, T, xs[0], [EA.vo[0], EA.vo[1]], xs[1], wo[0])
        stage_ffn(p, cx, T, xs[1], xs[2], **ffn[1])
        stage_ffn(p, cx, T, xs[2], xs[3], **ffn[2])
        qkv(1, xs[3])
        exchange([EQ, EK, EV, ELF])
        attention(1)
        exchange([EA])
        stage_outproj(p, cx, T, xs[3], [EA.vo[0], EA.vo[1]], xs[4], wo[1])
        stage_ffn(p, cx, T, xs[4], y, **ffn[3])
        p.emit()
    return nc


def kernel(x, norm_ffn1, ffn1_w_gate, ffn1_w_up, ffn1_w_down, norm_mix, w_in_ab, g_q_b, g_k_b,
           w_in_cd, b_f, g_q_c, g_k_c, g_q_d, g_k_d, w_out, norm_ffn2, ffn2_w_gate, ffn2_w_up,
           ffn2_w_down):
    A = lambda a: np.asarray(a, dtype=np.float32)
    x = A(x)
    B, S, _ = x.shape
    NC = 8
    T = B * S // NC
    inv = (1.0 / (10000.0 ** (np.arange(0, DH, 2, dtype=np.float32) / DH))).astype(np.float32)
    ang = (np.arange(S, dtype=np.float32)[:, None] * inv[None, :]).astype(np.float32)
    cos, sin = np.cos(ang).astype(np.float32), np.sin(ang).astype(np.float32)
    c0, c1 = attn_consts_np(S, 0), attn_consts_np(S, 1)
    shared = {"ident": c0["ident"], "tri": c0["tri"], "sbmask": c0["sbmask"], "onehot": c0["onehot"],
              "antiid": np.ascontiguousarray(np.eye(128, dtype=np.float32)[::-1]).astype(NPBF), "dilm": c1["dilm"],
              "nmix0": A(norm_mix[0]), "nmix1": A(norm_mix[1]), "w_in0": A(w_in_ab[0]), "w_in1": A(w_in_cd[0]),
              "wo0": A(w_out[0]), "wo1": A(w_out[1]), "gqb": A(g_q_b[0]), "gkb": A(g_k_b[0]),
              "gqc": A(g_q_c[0]), "gkc": A(g_k_c[0]), "gqd": A(g_q_d[0]), "gkd": A(g_k_d[0]), "bf": A(b_f[0])}
    for i, (nrm, wg, wu, wd, l) in enumerate([(norm_ffn1, ffn1_w_gate, ffn1_w_up, ffn1_w_down, 0),
                                               (norm_ffn2, ffn2_w_gate, ffn2_w_up, ffn2_w_down, 0),
                                               (norm_ffn1, ffn1_w_gate, ffn1_w_up, ffn1_w_down, 1),
                                               (norm_ffn2, ffn2_w_gate, ffn2_w_up, ffn2_w_down, 1)]):
        shared.update({f"gain{i}": A(nrm[l]), f"wg{i}": A(wg[l]), f"wu{i}": A(wu[l]), f"wd{i}": A(wd[l])})
    maps = []
    for c in range(NC):
        b, h = c // 2, c % 2
        m = dict(shared)
        rk = np.zeros((128, 2), np.float32)
        rk[:, h] = 1.0
        m.update(x=x[b, h * T:(h + 1) * T], rank=rk, cos=cos[h * T:(h + 1) * T], sin=sin[h * T:(h + 1) * T])
        maps.append(m)
    nc = build_fused(T, S)
    res = _run(nc, maps)
    out = np.empty((B, S, D), dtype=np.float32)
    for c in range(NC):
        b, h = c // 2, c % 2
        out[b, h * T:(h + 1) * T] = res[c]["y"]
    return out
```
